# Optimizing a Trainium2 kernel written in Bass

```python
import math
import jax, jax.numpy as jnp
from jax import lax
import numpy as np

D_MODEL = 1024
BATCH = 4
SEQ = 4096
DEPTH = 4

CHUNK = 64
Q_BLOCK = 128
MEM_LEN = 256
HEAD_DIM = 64
BR_W = 512
N_BRANCH = 4
A_HEADS = BR_W // (2 * HEAD_DIM)
F_HEADS = BR_W // HEAD_DIM
CONV_W = BR_W
CONV_K = 31
SC_W = BR_W
SC_K = 3
IN_COLS = 3 * BR_W + (3 * BR_W + F_HEADS) + 2 * CONV_W + 3 * SC_W
X_HEADS = 4
X_HEAD_DIM = 128
X_W = X_HEADS * X_HEAD_DIM
D_FF = 2816
FFN_K = 3
EPS = 1e-6

kernel_name = "hybrid_parallel_gated_streaming_encoder"


def _rms_norm(x, g):
    xf = x.astype(jnp.float32)
    y = xf * lax.rsqrt(jnp.mean(xf * xf, axis=-1, keepdims=True) + EPS)
    return (y * g.astype(jnp.float32)).astype(x.dtype)


def _layer_norm(x, g, b):
    xf = x.astype(jnp.float32)
    mu = jnp.mean(xf, axis=-1, keepdims=True)
    var = jnp.mean(jnp.square(xf - mu), axis=-1, keepdims=True)
    y = (xf - mu) * lax.rsqrt(var + EPS)
    return (y * g.astype(jnp.float32) + b.astype(jnp.float32)).astype(x.dtype)


def _dwconv_causal(x, w):
    K, C = w.shape
    return lax.conv_general_dilated(x, w[:, None, :].astype(x.dtype), window_strides=(1,),
                                    padding=[(K - 1, 0)], dimension_numbers=('NWC', 'WIO', 'NWC'),
                                    feature_group_count=C)


def _diff_attention(q, k, v, lam, t_pos):
    B, S, H = q.shape[:3]
    nb = S // Q_BLOCK
    scale = HEAD_DIM ** -0.5
    qb = q.reshape(B, nb, Q_BLOCK, H, 2, HEAD_DIM).transpose(1, 0, 2, 3, 4, 5)
    tq = t_pos.reshape(nb, Q_BLOCK)
    k_chunk = t_pos // CHUNK

    def one(args):
        q_blk, tq_blk = args
        s = jnp.einsum('bqhcd,bkhcd->bhcqk', q_blk, k, preferred_element_type=jnp.float32) * scale
        mask = k_chunk[None, :] <= (tq_blk // CHUNK)[:, None]
        p = jax.nn.softmax(jnp.where(mask, s, -jnp.inf), axis=-1)
        a = p[:, :, 0] - lam * p[:, :, 1]
        return jnp.einsum('bhqk,bkhe->bqhe', a.astype(v.dtype), v)

    o = lax.map(one, (qb, tq))
    return o.transpose(1, 0, 2, 3, 4).reshape(B, S, H, 2 * HEAD_DIM)


def _forgetting_attention(q, k, v, log_f, t_pos):
    B, S, H = q.shape[:3]
    nb = S // Q_BLOCK
    scale = HEAD_DIM ** -0.5
    c = jnp.cumsum(log_f.astype(jnp.float32), axis=1)
    c_k = c.transpose(0, 2, 1)
    qb = q.reshape(B, nb, Q_BLOCK, H, HEAD_DIM).transpose(1, 0, 2, 3, 4)
    cb = c.reshape(B, nb, Q_BLOCK, H).transpose(1, 0, 3, 2)
    tq = t_pos.reshape(nb, Q_BLOCK)

    def one(args):
        q_blk, c_blk, tq_blk = args
        s = jnp.einsum('bqhd,bkhd->bhqk', q_blk, k, preferred_element_type=jnp.float32) * scale
        s = s + c_blk[..., None] - c_k[:, :, None, :]
        mask = t_pos[None, :] <= tq_blk[:, None]
        p = jax.nn.softmax(jnp.where(mask, s, -jnp.inf), axis=-1)
        return jnp.einsum('bhqk,bkhd->bqhd', p.astype(v.dtype), v)

    o = lax.map(one, (qb, cb, tq))
    return o.transpose(1, 0, 2, 3, 4).reshape(B, S, H * HEAD_DIM)


def setup_inputs(seed: int = 0) -> dict:
    key = jax.random.key(seed)
    ks = iter(jax.random.split(key, 48))
    L, D = DEPTH, D_MODEL

    def nrm(shape, scale):
        return jax.random.normal(next(ks), shape, jnp.float32) * scale

    def gain(shape):
        return 1.0 + nrm(shape, 0.05)

    return {
        "x": nrm((BATCH, SEQ, D), 1.0),
        "mem": nrm((BATCH, MEM_LEN, D), 1.0),
        "norm_mix_pre": gain((L, D)),
        "norm_mix_post": gain((L, D)),
        "w_in": nrm((L, D, IN_COLS), D ** -0.5),
        "b_fgt": 2.0 + nrm((L, F_HEADS), 0.1),
        "lam_q1": nrm((L, HEAD_DIM), 0.1),
        "lam_k1": nrm((L, HEAD_DIM), 0.1),
        "lam_q2": nrm((L, HEAD_DIM), 0.1),
        "lam_k2": nrm((L, HEAD_DIM), 0.1),
        "diff_norm": gain((L, 2 * HEAD_DIM)),
        "b_glu": nrm((L, 2 * CONV_W), 0.02),
        "conv_dw": nrm((L, CONV_K, CONV_W), CONV_K ** -0.5),
        "conv_dw_b": nrm((L, CONV_W), 0.02),
        "conv_ln_g": gain((L, CONV_W)),
        "conv_ln_b": nrm((L, CONV_W), 0.02),
        "sc_w": nrm((L, SC_K, SC_W), SC_K ** -0.5),
        "w_branch": nrm((L, N_BRANCH, BR_W, D), BR_W ** -0.5),
        "w_gate": nrm((L, D, N_BRANCH * D), D ** -0.5),
        "b_gate": nrm((L, N_BRANCH * D), 0.02),
        "w_out": nrm((L, D, D), D ** -0.5),
        "norm_x_pre": gain((L, D)),
        "norm_x_post": gain((L, D)),
        "norm_mem": gain((L, D)),
        "w_xq": nrm((L, D, X_W), D ** -0.5),
        "w_xkv": nrm((L, D, 2 * X_W), D ** -0.5),
        "w_xo": nrm((L, X_W, D), X_W ** -0.5),
        "norm_ffn_pre": gain((L, D)),
        "norm_ffn_post": gain((L, D)),
        "w_up": nrm((L, D, 2 * D_FF), D ** -0.5),
        "ffn_dw": nrm((L, FFN_K, 2 * D_FF), FFN_K ** -0.5),
        "ffn_dw_b": nrm((L, 2 * D_FF), 0.02),
        "w_down": nrm((L, D_FF, D), D_FF ** -0.5),
    }


def reference(x, mem, norm_mix_pre, norm_mix_post, w_in, b_fgt, lam_q1, lam_k1, lam_q2, lam_k2,
              diff_norm, b_glu, conv_dw, conv_dw_b, conv_ln_g, conv_ln_b, sc_w, w_branch, w_gate,
              b_gate, w_out, norm_x_pre, norm_x_post, norm_mem, w_xq, w_xkv, w_xo, norm_ffn_pre,
              norm_ffn_post, w_up, ffn_dw, ffn_dw_b, w_down):
    B, S, D = x.shape
    t_pos = jnp.arange(S, dtype=jnp.int32)
    sizes = [BR_W, BR_W, BR_W, BR_W, BR_W, BR_W, F_HEADS, 2 * CONV_W, SC_W, SC_W, SC_W]
    cuts = [int(v) for v in np.cumsum(sizes)[:-1]]

    for l in range(DEPTH):
        h = _rms_norm(x, norm_mix_pre[l])
        (a_q, a_k, a_v, f_q, f_k, f_v, f_g, c_u, s_x, s_b, s_c) = jnp.split(h @ w_in[l], cuts, axis=-1)

        lambda_init = 0.8 - 0.6 * math.exp(-0.3 * l)
        lam = (jnp.exp(jnp.sum(lam_q1[l].astype(jnp.float32) * lam_k1[l].astype(jnp.float32)))
               - jnp.exp(jnp.sum(lam_q2[l].astype(jnp.float32) * lam_k2[l].astype(jnp.float32))) + lambda_init)
        ya = _diff_attention(a_q.reshape(B, S, A_HEADS, 2, HEAD_DIM), a_k.reshape(B, S, A_HEADS, 2, HEAD_DIM),
                             a_v.reshape(B, S, A_HEADS, 2 * HEAD_DIM), lam, t_pos)
        ya = (_rms_norm(ya, diff_norm[l]) * (1.0 - lambda_init)).reshape(B, S, BR_W)

        log_f = jax.nn.log_sigmoid(f_g.astype(jnp.float32) + b_fgt[l].astype(jnp.float32))
        yf = _forgetting_attention(f_q.reshape(B, S, F_HEADS, HEAD_DIM), f_k.reshape(B, S, F_HEADS, HEAD_DIM),
                                   f_v.reshape(B, S, F_HEADS, HEAD_DIM), log_f, t_pos)

        c_a, c_g = jnp.split(c_u + b_glu[l], 2, axis=-1)
        yc = _dwconv_causal(c_a * jax.nn.sigmoid(c_g), conv_dw[l]) + conv_dw_b[l]
        yc = jax.nn.silu(_layer_norm(yc, conv_ln_g[l], conv_ln_b[l]))

        ys = s_b * _dwconv_causal(s_c * s_x, sc_w[l])

        br = jnp.stack([ya, yf, yc, ys], axis=2)
        proj = jnp.einsum('bsnc,ncd->bsnd', br, w_branch[l])
        gates = jax.nn.sigmoid(h @ w_gate[l] + b_gate[l]).reshape(B, S, N_BRANCH, D)
        mixed = jnp.sum(gates * proj, axis=2) @ w_out[l]
        x = x + _rms_norm(mixed, norm_mix_post[l])

        hx = _rms_norm(x, norm_x_pre[l])
        m = _rms_norm(mem, norm_mem[l])
        xq = (hx @ w_xq[l]).reshape(B, S, X_HEADS, X_HEAD_DIM)
        xk, xv = jnp.split(m @ w_xkv[l], 2, axis=-1)
        xk = xk.reshape(B, MEM_LEN, X_HEADS, X_HEAD_DIM)
        xv = xv.reshape(B, MEM_LEN, X_HEADS, X_HEAD_DIM)
        sx = jnp.einsum('bqhd,bkhd->bhqk', xq, xk, preferred_element_type=jnp.float32) * (X_HEAD_DIM ** -0.5)
        px = jax.nn.softmax(sx, axis=-1).astype(xv.dtype)
        ox = jnp.einsum('bhqk,bkhd->bqhd', px, xv).reshape(B, S, X_W) @ w_xo[l]
        x = x + _rms_norm(ox, norm_x_post[l])

        hf = _rms_norm(x, norm_ffn_pre[l])
        u = _dwconv_causal(hf @ w_up[l], ffn_dw[l]) + ffn_dw_b[l]
        u_g, u_v = jnp.split(u, 2, axis=-1)
        yff = (jax.nn.silu(u_g) * u_v) @ w_down[l]
        x = x + _rms_norm(yff, norm_ffn_post[l])

    return x
```

```python
import math
import numpy as np
import concourse.bass as bass
import concourse.mybir as mybir
from concourse.bass_utils import run_bass_kernel_spmd
from contextlib import ExitStack

F32 = mybir.dt.float32
BF16 = mybir.dt.bfloat16
ALU = mybir.AluOpType
AF = mybir.ActivationFunctionType
AX = mybir.AxisListType

L = 4
D = 1024
T = 2048
NT = 4
TW = 512
KC = 8
EPS = 1e-6
NEG = -30000.0
DFF = 2816
NG = 22
MASK_PE = True
F_PINGPONG = True

PP = {}
_o = 0
for _n, _w in [("nmp", 8), ("nmo", 8), ("nxp", 8), ("nxo", 8), ("nmem", 8), ("nfp", 8), ("nfo", 8),
               ("bglu", 8), ("cdw", 124), ("cdwb", 4), ("clng", 4), ("clnb", 4), ("scw", 12),
               ("bgate", 32), ("fdw", 132), ("fdwb", 44), ("dnorm", 1), ("bfgt", 1),
               ("lq1", 64), ("lk1", 64), ("lq2", 64), ("lk2", 64)]:
    PP[_n] = (_o, _w)
    _o += _w
PPL = _o
NPP = PPL * L


class _Op:
    __slots__ = ("eng", "fn", "deps", "signal", "sigidx", "dma", "sem", "semval", "prev", "idx", "cc")


class Prog:
    ENGS = ("pe", "act", "dve", "pool", "sp")
    KQ = 8

    def __init__(self):
        self.ops = []
        self.lastw = {}
        self.readers = {}
        self.gnames = set()
        self.groups = {}

    def _expand(self, reads, writes):
        r2, w2, extra = [], [], []
        for k in reads:
            if k in self.gnames:
                g = self.groups.setdefault(k, {"mem": [], "read": False, "n": 0})
                g["read"] = True
                r2.extend(g["mem"])
            else:
                r2.append(k)
        for k in writes:
            if k in self.gnames:
                g = self.groups.setdefault(k, {"mem": [], "read": False, "n": 0})
                if g["read"]:
                    extra.extend(g["mem"])
                    g["mem"] = []
                    g["read"] = False
                g["n"] += 1
                sk = ("#g", k, g["n"])
                g["mem"].append(sk)
                w2.append(sk)
            else:
                w2.append(k)
        return r2, w2, extra

    def add(self, eng, fn, reads=(), writes=(), dma=False, cc=False):
        op = _Op()
        op.eng, op.fn, op.dma, op.cc = eng, fn, dma or cc, cc
        op.signal = op.dma
        op.idx = len(self.ops)
        op.sem = op.semval = op.prev = op.sigidx = None
        deps = set()
        reads, writes, extra = self._expand(list(reads), list(writes))
        for k in extra:
            w = self.lastw.get(k)
            if w is not None:
                deps.add(w)
            for rd in self.readers.get(k, ()):
                deps.add(rd)
        for r in reads:
            w = self.lastw.get(r)
            if w is not None:
                deps.add(w)
        for k in writes:
            w = self.lastw.get(k)
            if w is not None:
                deps.add(w)
            for rd in self.readers.get(k, ()):
                deps.add(rd)
        op.deps = deps
        for d in deps:
            self.ops[d].signal = True
        for r in reads:
            self.readers.setdefault(r, []).append(op.idx)
        for k in writes:
            self.lastw[k] = op.idx
            self.readers[k] = []
        self.ops.append(op)
        return op.idx

    def finalize(self, nc, es):
        self.sems = {e: es.enter_context(nc.semaphore("pg_" + e)) for e in self.ENGS}
        self.dsems = {q: [es.enter_context(nc.semaphore("dq_%s%d" % (q, i))) for i in range(self.KQ)]
                      for q in ("sp", "pool", "act")}
        cnt = {e: 0 for e in self.ENGS}
        dq = {"sp": [], "pool": [], "act": []}
        for op in self.ops:
            if op.cc:
                op.sem = es.enter_context(nc.semaphore("cc%d" % op.idx))
                op.semval = 1
            elif op.dma:
                lst = dq[op.eng]
                n = len(lst)
                op.sem = self.dsems[op.eng][n % self.KQ]
                op.semval = 16 * (n // self.KQ + 1)
                op.prev = lst[n - self.KQ] if n >= self.KQ else None
                lst.append(op.idx)
            elif op.signal:
                cnt[op.eng] += 1
                op.sigidx = cnt[op.eng]

    def emit(self, ename, eng):
        seen = {}
        ops = self.ops

        def wait(sem, key, val):
            if seen.get(key, 0) < val:
                eng.wait_ge(sem, val)
                seen[key] = val

        for op in ops:
            if op.eng != ename:
                continue
            for d in sorted(op.deps):
                dop = ops[d]
                if dop.dma:
                    wait(dop.sem, ("d", id(dop.sem)), dop.semval)
                else:
                    if dop.eng == ename and ename == "pe" and not op.dma:
                        continue
                    wait(self.sems[dop.eng], dop.eng, dop.sigidx)
            if op.dma and not op.cc and op.prev is not None:
                p = ops[op.prev]
                wait(p.sem, ("d", id(p.sem)), p.semval)
            inst = op.fn(eng)
            if op.cc:
                inst.then_inc(op.sem)
            elif op.dma:
                inst.then_inc(op.sem, 16)
            elif op.signal:
                inst.then_inc(self.sems[ename], 1)


class _Stop(Exception):
    pass


def build(nlayers=L, stop=None):
    nc = bass.Bass("TRN2", target_bir_lowering=False)
    P = Prog()
    P.gnames = set(["qa", "cKA", "qf", "kfo", "cKF", "cKFa", "cV", "cH", "glu", "sxc", "sb"]
                   + [("mix", t) for t in range(NT)] + [("ysc", t) for t in range(NT)])

    def chk(name):
        if stop == name:
            raise _Stop()

    def din(name, shape):
        return nc.dram_tensor(name, list(shape), F32, kind="ExternalInput").ap()

    xT_in = din("xT", [D, T])
    memT_in = din("memT", [D, 256])
    pp_in = din("pp", [128, NPP])
    flags_in = din("flags", [128, 2])
    masks_in = din("masks", [128, 384])
    win = din("win", [L, 36, 128, 1024])
    wfg = din("wfg", [L, 128, 64])
    wv = din("wv", [L, 2, 128, 4096])
    wb = din("wb", [L, 8, 128, 2048])
    wg = din("wg", [L, 8, 128, 4096])
    wo = din("wo", [L, 8, 128, 1024])
    wxq = din("wxq", [L, 4, 128, 1024])
    wxk = din("wxk", [L, 4, 128, 1024])
    wxv = din("wxv", [L, 128, 4096])
    wxo = din("wxo", [L, 8, 128, 512])
    wup = din("wup", [L, 44, 128, 1024])
    wdn = din("wdn", [L, 8, 128, 2816])
    out = nc.dram_tensor("out", [D, T], F32, kind="ExternalOutput").ap()

    def dscr(name, shape, dt):
        return nc.dram_tensor(name, list(shape), dt).ap()

    xs = dscr("xs", [8, 128, T], F32)
    ysc = dscr("ysc", [8, 128, T], F32)
    qa_s = dscr("qa_s", [512, T], BF16)
    qf_s = dscr("qf_s", [8 * 68, T], BF16)
    kfo_s = dscr("kfo_s", [8 * 68, T], BF16)
    glu_s = dscr("glu_s", [512, T], BF16)
    sxc_s = dscr("sxc_s", [512, T], BF16)
    sb_s = dscr("sb_s", [512, T], BF16)
    zf_s = dscr("zf_s", [8, 128, T], F32)
    mix_s = dscr("mix_s", [8, 128, T], BF16)
    cKA = dscr("cKA", [512, T], BF16)
    oKA = dscr("oKA", [1024, T], BF16)
    cKFk = dscr("cKFk", [512, T], BF16)
    oKFk = dscr("oKFk", [1024, T], BF16)
    cKFa = dscr("cKFa", [32, T], BF16)
    oKFa = dscr("oKFa", [64, T], BF16)
    cVs_ = [dscr("cV%d" % i, [512, T], BF16) for i in range(3)]
    oVs_ = [dscr("oV%d" % i, [1024, T], BF16) for i in range(3)]
    cH_ = dscr("cH", [16, T], BF16)
    oH_ = dscr("oH", [32, T], BF16)
    cF_ = dscr("cF", [1, T], BF16)
    oF_ = dscr("oF", [2, T], BF16)
    cVg = [a.rearrange("r (a c) -> (r a) c", c=128).rearrange("(h t) c -> h t c", h=4) for a in cVs_]
    oVg = [a.rearrange("r (a c) -> (r a) c", c=128).rearrange("(r h t) c -> r h t c", r=2, h=4) for a in oVs_]
    cH = cH_.rearrange("r (a c) -> (r a) c", c=64)
    oH = oH_.rearrange("r (a c) -> (r a) c", c=64)
    cF = cF_.rearrange("r (a c) -> (r a) c", c=2)
    oF = oF_.rearrange("r (a c) -> (r a) c", c=2)
    RG = [[0, 1], [2, 3], [4, 5], [6, 7]]

    es = ExitStack()

    def sb(name, shape, dt):
        return es.enter_context(nc.sbuf_tensor("s_" + name, list(shape), dt))

    ARX = sb("ARX", [128, 32768], BF16)
    ARA = sb("ARA", [128, 16384], BF16)
    ARB = sb("ARB", [128, 14336], BF16)
    wsl = [sb("wsl%d" % i, [128, 4096], BF16) for i in range(3)]
    PTs = [sb("PT%d" % i, [128, 512], BF16) for i in range(4)]
    evs = [sb("ev%d" % i, [128, 512], F32) for i in range(4)]
    sq = sb("sq", [128, 8, 512], BF16)
    stg = [sb("stg%d" % i, [128, 512], BF16) for i in range(4)]
    pp = sb("pp", [128, NPP], F32)
    flags = sb("flags", [128, 2], F32)
    ones_bf = sb("ones_bf", [128, 128], BF16)
    meanD = sb("meanD", [128, 128], BF16)
    mean512 = sb("mean512", [128, 128], BF16)
    masks = sb("masks", [128, 384], BF16)
    small = sb("small", [128, 16], F32)
    uh = sb("uh", [128, 44, 2], F32)
    hfh = sb("hfh", [128, 8, 2], BF16)
    hfh2 = sb("hfh2", [128, 8, 2], BF16)
    vst = sb("vst", [128, 8, 128], BF16)
    ONE8t = sb("ONE8", [8, 2048], BF16)
    ONE8 = ONE8t[:, :]
    ARC = sb("ARC", [128, 8704], BF16)
    ps = [es.enter_context(nc.psum_tensor("ps%d" % i, [128, 512], F32)) for i in range(8)]

    mTb = ARB[:, 0:2048].rearrange("p (c n) -> p c n", c=8)
    xkT = ARB[:, 2048:3072].rearrange("p (c n) -> p c n", c=4)
    xvs = ARB[:, 3072:4096].rearrange("p (c n) -> p c n", c=2)
    A = ARA[:, :].rearrange("p (c n) -> p c n", c=8)
    BR = ARX[:, :].rearrange("p (c n) -> p c n", c=16)
    XF = ARX[:, :].bitcast(F32)
    pmask = flags[:, 0:1]
    hflag = flags[:, 1:2]

    ctr = {"ps": 0, "ev": 0, "stg": 0, "pt": 0, "w": 0}

    def rr(kind, n):
        i = ctr[kind] % n
        ctr[kind] += 1
        return i

    def ppc(l, name, j=0, n=1):
        o, w = PP[name]
        return pp[:, l * PPL + o + j: l * PPL + o + j + n]

    def pe(fn, reads, writes):
        return P.add("pe", fn, reads, writes)

    def act(fn, reads, writes):
        return P.add("act", fn, reads, writes)

    def dve(fn, reads, writes):
        return P.add("dve", fn, reads, writes)

    def dma(q, o, i, reads, writes):
        return P.add(q, lambda e, o=o, i=i: e.dma_start(out=o, in_=i), reads, writes, dma=True)

    def mm_group(bank, pairs, reads, n0=0, n1=512, rows=128):
        def fn(e, bank=bank, pairs=pairs):
            last = None
            for i, (lt, rh) in enumerate(pairs):
                last = e.matmul(ps[bank][0:rows, n0:n1], lt, rh, start=(i == 0), stop=(i == len(pairs) - 1))
            return last
        return pe(fn, reads, [("ps", bank)])

    dma("sp", pp[:, :], pp_in, [], ["pp"])
    dma("sp", flags[:, :], flags_in, [], ["flags"])
    dma("pool", masks[:, :], masks_in, [], ["masks"])
    dve(lambda e: e.memset(ones_bf[:, :], 1.0), [], ["consts"])
    dve(lambda e: e.memset(meanD[:, :], 1.0 / 1024), [], ["consts"])
    dve(lambda e: e.memset(mean512[:, :], 1.0 / 512), [], ["consts"])
    dve(lambda e: e.memset(vst[:, :, 64:128], 1.0), [], ["vst"])
    dve(lambda e: e.memset(ONE8, 1.0), [], ["ONE8"])
    dve(lambda e: e.memset(small[:, :], 0.0), [], ["small"])
    dve(lambda e: e.memset(small[:, 2:3], EPS), ["small"], ["small"])

    XKEYS = (["XT0", "YT0", ("XT0", 0), ("XT0", 1), ("YT0", 0), ("YT0", 1), "FZ", "FS", "FC", "FD", "FO", "HB", "MX", "ACT_T"]
             + [("XQ", h, t) for h in range(4) for t in range(NT)] + [("OX", h, t) for h in range(4) for t in range(NT)]
             + [("BR", c, t) for c in range(16) for t in range(NT)] + [("ACT_T", g, tt) for g in range(NG) for tt in range(2)])
    BKEYS = ([("KT", 0), ("KT", 1), ("KT1", 0), ("KT1", 1), ("VV", 0), ("VV", 1)] + [("QT", j) for j in range(NT)] + ["mTb", "xkT", "xvs"])

    def fenceX():
        P.add("dve", lambda e: e.memset(small[:, 8:9], 0.0), [], XKEYS)

    def fenceB():
        P.add("dve", lambda e: e.memset(small[:, 8:9], 0.0), [], BKEYS)

    def rstd_from_sq(nchunks, meanmat, sq_reads, n=512):
        b = 4 + rr("ps", 4)
        mm_group(b, [(meanmat[:, :], sq[:, c, 0:n]) for c in range(nchunks)], sq_reads + ["consts"], 0, n)
        i = rr("ev", 4)
        r = evs[i]
        act(lambda e, r=r, b=b: e.activation(out=r[:, 0:n], in_=ps[b][:, 0:n], func=AF.Ln, bias=small[:, 2:3], scale=1.0),
            [("ps", b), "small"], [("ev", i)])
        act(lambda e, r=r: e.activation(out=r[:, 0:n], in_=r[:, 0:n], func=AF.Exp, scale=-0.5),
            [("ev", i)], [("ev", i)])
        return r, ("ev", i)

    def prenorm_tile(xt, xkey, l, gname, t):
        for c in range(8):
            act(lambda e, c=c: e.activation(out=sq[:, c, :], in_=xt[:, c, :], func=AF.Square), [xkey], [("sq", c)])
        r, rk = rstd_from_sq(8, meanD, [("sq", c) for c in range(8)])
        for c in range(8):
            dve(lambda e, c=c, r=r: e.scalar_tensor_tensor(out=A[:, c, t * TW:(t + 1) * TW], in0=xt[:, c, :],
                                                        scalar=ppc(l, gname, c), in1=r[:, :],
                                                        op0=ALU.mult, op1=ALU.mult),
                [xkey, rk, "pp"], [("A", c, t)])

    XT0 = XF[:, 0:4096].rearrange("p (c n) -> p c n", c=8)
    YT0 = XF[:, 4096:8192].rearrange("p (c n) -> p c n", c=8)

    def post_pass(l, gpost, gnext, lnext, final=False):
        SW = 256
        NU = T // SW
        outv = out.rearrange("(c p) n -> p c n", p=128)

        def bufs(u):
            hh = u % 2
            return (YT0[:, :, hh * SW:(hh + 1) * SW], ("YT0", hh), XT0[:, :, hh * SW:(hh + 1) * SW], ("XT0", hh))

        def stage_a(u):
            YT, yk, XT, xk = bufs(u)
            c0 = u * SW
            dma("sp", YT, ysc[:, :, c0:c0 + SW].rearrange("c p n -> p c n"), [("ysc", u // 2)], [yk])
            dma("sp", XT, xs[:, :, c0:c0 + SW].rearrange("c p n -> p c n"), [("xs", u // 2)], [xk])
            for c in range(8):
                act(lambda e, c=c, YT=YT: e.activation(out=sq[:, c, 0:SW], in_=YT[:, c, :], func=AF.Square), [yk], [("sq", c)])
            return rstd_from_sq(8, meanD, [("sq", c) for c in range(8)], n=SW)

        def stage_b(u, r, rk):
            YT, yk, XT, xk = bufs(u)
            c0 = u * SW
            for c in range(8):
                dve(lambda e, c=c, r=r, YT=YT: e.scalar_tensor_tensor(out=YT[:, c, :], in0=YT[:, c, :], scalar=ppc(l, gpost, c),
                                                                   in1=r[:, 0:SW], op0=ALU.mult, op1=ALU.mult),
                    [yk, rk, "pp"], [yk])
                dve(lambda e, c=c, YT=YT, XT=XT: e.tensor_tensor(out=XT[:, c, :], in0=XT[:, c, :], in1=YT[:, c, :], op=ALU.add),
                    [yk, xk], [xk])
            if final:
                dma("sp", outv[:, :, c0:c0 + SW], XT, [xk], ["out"])
                return None
            dma("sp", xs[:, :, c0:c0 + SW].rearrange("c p n -> p c n"), XT, [xk], [("xs", u // 2)])
            for c in range(8):
                act(lambda e, c=c, XT=XT: e.activation(out=sq[:, c, 0:SW], in_=XT[:, c, :], func=AF.Square), [xk], [("sq", c)])
            return rstd_from_sq(8, meanD, [("sq", c) for c in range(8)], n=SW)

        def stage_c(u, r, rk):
            YT, yk, XT, xk = bufs(u)
            c0 = u * SW
            for c in range(8):
                dve(lambda e, c=c, r=r, XT=XT: e.scalar_tensor_tensor(out=A[:, c, c0:c0 + SW], in0=XT[:, c, :], scalar=ppc(lnext, gnext, c),
                                                                   in1=r[:, 0:SW], op0=ALU.mult, op1=ALU.mult),
                    [xk, rk, "pp"], [("A", c, u // 2)])

        ra = {0: stage_a(0)}
        for u in range(NU):
            if u + 1 < NU:
                ra[u + 1] = stage_a(u + 1)
            r2 = stage_b(u, *ra[u])
            if r2 is not None:
                stage_c(u, *r2)

    def load_w(src, ncol, key_extra=()):
        i = rr("w", 3)
        w = wsl[i]
        dma("pool", w[:, 0:ncol], src, [], [("w", i)])
        return w, ("w", i)

    def lin_fm(inp_keyfn, inp, kc, wsrcs, handler, tiles=range(NT)):
        for ci, src in enumerate(wsrcs):
            w, wk = load_w(src, kc * 128)
            for t in tiles:
                b = rr("ps", 8)
                mm_group(b, [(w[:, k * 128:(k + 1) * 128], inp[:, k, t * TW:(t + 1) * TW]) for k in range(kc)],
                         [wk] + [inp_keyfn(k, t) for k in range(kc)])
                handler(ci, t, b)

    def Akey(k, t):
        return ("A", k, t)

    for t in range(NT):
        dma("sp", XT0, xT_in.rearrange("(c p) n -> p c n", p=128)[:, :, t * TW:(t + 1) * TW], [], ["XT0"])
        dma("sp", xs[:, :, t * TW:(t + 1) * TW].rearrange("c p n -> p c n"), XT0, ["XT0"], [("xs", t)])
        prenorm_tile(XT0, "XT0", 0, "nmp", t)

    def do_layer(l):
      if True:
          lam_init = 0.8 - 0.6 * math.exp(-0.3 * l)
          tmp64 = evs[0]
          for j, (a_, b_) in enumerate((("lq1", "lk1"), ("lq2", "lk2"))):
              dve(lambda e, a_=a_, b_=b_: e.tensor_tensor(out=tmp64[:, 0:64], in0=ppc(l, a_, 0, 64), in1=ppc(l, b_, 0, 64), op=ALU.mult),
                  ["pp"], [("ev", 0)])
              dve(lambda e, j=j: e.reduce_sum(out=small[:, 4 + j:5 + j], in_=tmp64[:, 0:64], axis=AX.X), [("ev", 0)], ["small"])
          act(lambda e: e.activation(out=small[:, 4:6], in_=small[:, 4:6], func=AF.Exp), ["small"], ["small"])
          dve(lambda e: e.scalar_tensor_tensor(out=small[:, 0:1], in0=small[:, 5:6], scalar=-lam_init, in1=small[:, 4:5],
                                               op0=ALU.add, op1=ALU.subtract), ["small"], ["small"])
          dve(lambda e: e.tensor_scalar(out=small[:, 1:2], in0=ppc(l, "dnorm"), scalar1=1.0 - lam_init, scalar2=None, op0=ALU.mult),
              ["pp"], ["small"])
          dve(lambda e: e.tensor_scalar(out=small[:, 3:4], in0=ppc(l, "bfgt"), scalar1=-1.0, scalar2=None, op0=ALU.mult),
              ["pp"], ["small"])

          fenceX()
          fenceB()
          def evac_scaled_to(dst_rows_fn, scale):
              def h(ci, t, b):
                  i = rr("stg", 4)
                  s = stg[i]
                  act(lambda e, s=s, b=b: e.activation(out=s[:, :], in_=ps[b][:, :], func=AF.Copy, scale=scale),
                      [("ps", b)], [("stg", i)])
                  for (dst, key, r0, r1) in dst_rows_fn(ci):
                      dma("sp", dst[:, t * TW:(t + 1) * TW], s[r0:r1, :], [("stg", i)], [key])
              return h

          lin_fm(Akey, A, 8, [win[l, c] for c in range(4, 8)],
                 evac_scaled_to(lambda ci: [(cKA[ci * 128:(ci + 1) * 128, :], "cKA", 0, 128)], 1.0))
          lin_fm(Akey, A, 8, [win[l, c] for c in range(12, 16)],
                 evac_scaled_to(lambda ci: [(kfo_s[(2 * ci) * 68:(2 * ci) * 68 + 64, :], "kfo", 0, 64),
                                            (kfo_s[(2 * ci + 1) * 68:(2 * ci + 1) * 68 + 64, :], "kfo", 64, 128),
                                            (cKFk[ci * 128:(ci + 1) * 128, :], "cKF", 0, 128)], 1.0))

          FZ = XF[0:8, 0:2048]
          FS = XF[0:8, 2048:4096]
          FC = XF[0:8, 4096:6144]
          FD = XF[0:8, 6144:8192]
          FO = XF[0:8, 8192:10240]
          HB = ARX[0:8, 20480:32768].rearrange("p (a n) -> p a n", a=6)
          wfs, wfk = load_w(wfg[l], 64)
          for t in range(NT):
              b = rr("ps", 8)
              mm_group(b, [(wfs[:, k * 8:(k + 1) * 8], A[:, k, t * TW:(t + 1) * TW]) for k in range(8)],
                       [wfk] + [("A", k, t) for k in range(8)], rows=8)
              act(lambda e, b=b, t=t: e.activation(out=FZ[:, t * TW:(t + 1) * TW], in_=ps[b][0:8, :], func=AF.Exp,
                                                   bias=small[0:8, 3:4], scale=-1.0), [("ps", b), "small"], ["FZ"])
          act(lambda e: e.activation(out=FS, in_=FZ, func=AF.Ln, bias=1.0, scale=1.0), ["FZ"], ["FS"])
          dve(lambda e: e.memset(FO, 1.0), [], ["FO"])
          dve(lambda e: e.tensor_tensor_scan(out=FC, data0=FO, data1=FS, initial=0.0, op0=ALU.mult, op1=ALU.add),
              ["FO", "FS"], ["FC"])

          def hilo(src_fn, ihi, key):
              dve(lambda e: src_fn(e, FD), [key, "FC"], ["FD"])
              dve(lambda e: e.tensor_copy(out=HB[:, ihi, :], in_=FD), ["FD"], ["HB"])
              dve(lambda e: e.tensor_tensor(out=HB[:, ihi + 1, :], in0=FD, in1=HB[:, ihi, :], op=ALU.subtract), ["FD", "HB"], ["HB"])
          hilo(lambda e, o: e.tensor_scalar(out=o, in0=FC, scalar1=-1.0, scalar2=None, op0=ALU.mult), 0, "FC")
          hilo(lambda e, o: e.tensor_copy(out=o, in_=FC), 2, "FC")
          hilo(lambda e, o: e.tensor_scalar(out=o, in0=FC, scalar1=FC[:, 2047:2048], scalar2=None, op0=ALU.subtract), 4, "FC")
          qf3 = qf_s.rearrange("(h r) n -> h r n", r=68)
          kfo3 = kfo_s.rearrange("(h r) n -> h r n", r=68)
          cKFa3 = cKFa.rearrange("(h r) n -> h r n", r=4)
          dma("sp", qf3[:, 64, :], HB[:, 0, :], ["HB"], ["qf"])
          dma("sp", qf3[:, 65, :], HB[:, 1, :], ["HB"], ["qf"])
          dma("sp", qf3[:, 66, :], ONE8, ["ONE8"], ["qf"])
          dma("sp", qf3[:, 67, :], ONE8, ["ONE8"], ["qf"])
          for dst, key, ih, r0 in ((kfo3, "kfo", 2, 64), (cKFa3, "cKFa", 4, 0)):
              dma("sp", dst[:, r0, :], ONE8, ["ONE8"], [key])
              dma("sp", dst[:, r0 + 1, :], ONE8, ["ONE8"], [key])
              dma("sp", dst[:, r0 + 2, :], HB[:, ih, :], ["HB"], [key])
              dma("sp", dst[:, r0 + 3, :], HB[:, ih + 1, :], ["HB"], [key])

          dve(lambda e: e.memset(vst[:, :, 64:128], 1.0), [], ["vst"])
          for vi in range(2):
              w, wk = load_w(wv[l, vi], 4096)
              for blk in range(16):
                  b = rr("ps", 8)
                  t_, o_ = blk // 4, (blk % 4) * 128
                  mm_group(b, [(A[:, k, blk * 128:(blk + 1) * 128], w[:, k * 512:(k + 1) * 512]) for k in range(8)],
                           [wk] + [("A", k, t_) for k in range(8)])
                  if vi == 0:
                      si = rr("stg", 4)
                      s = stg[si]
                      act(lambda e, s=s, b=b: e.activation(out=s[:, :], in_=ps[b][:, :], func=AF.Copy), [("ps", b)], [("stg", si)])
                      dma("sp", cVg[0][:, blk * 128:(blk + 1) * 128, :].rearrange("h t c -> t h c"),
                          s[:, :].rearrange("p (h c) -> p h c", h=4), [("stg", si)], ["cV"])
                  else:
                      act(lambda e, b=b: e.activation(out=vst[:, :, 0:64], in_=ps[b][:, :].rearrange("p (h c) -> p h c", h=8), func=AF.Copy),
                          [("ps", b)], ["vst"])
                      dma("sp", cVg[1][:, blk * 128:(blk + 1) * 128, :].rearrange("h t c -> t h c"), vst[:, 0:4, :], ["vst"], ["cV"])
                      dma("sp", cVg[2][:, blk * 128:(blk + 1) * 128, :].rearrange("h t c -> t h c"), vst[:, 4:8, :], ["vst"], ["cV"])

          for j in range(4):
              wa, wak = load_w(win[l, 16 + j], 1024)
              wgl, wgk = load_w(win[l, 20 + j], 1024)
              for t in range(NT):
                  ba = rr("ps", 8)
                  mm_group(ba, [(wa[:, k * 128:(k + 1) * 128], A[:, k, t * TW:(t + 1) * TW]) for k in range(8)],
                           [wak] + [("A", k, t) for k in range(8)])
                  bg = rr("ps", 8)
                  mm_group(bg, [(wgl[:, k * 128:(k + 1) * 128], A[:, k, t * TW:(t + 1) * TW]) for k in range(8)],
                           [wgk] + [("A", k, t) for k in range(8)])
                  i = rr("ev", 4)
                  ev = evs[i]
                  act(lambda e, ev=ev, bg=bg, j=j: e.activation(out=ev[:, :], in_=ps[bg][:, :], func=AF.Sigmoid,
                                                               bias=ppc(l, "bglu", 4 + j), scale=1.0),
                      [("ps", bg), "pp"], [("ev", i)])
                  si = rr("stg", 4)
                  s = stg[si]
                  dve(lambda e, s=s, ev=ev, ba=ba, j=j: e.scalar_tensor_tensor(out=s[:, :], in0=ps[ba][:, :], scalar=ppc(l, "bglu", j),
                                                                           in1=ev[:, :], op0=ALU.add, op1=ALU.mult),
                      [("ps", ba), ("ev", i), "pp"], [("stg", si)])
                  dma("sp", glu_s[j * 128:(j + 1) * 128, t * TW:(t + 1) * TW], s[:, :], [("stg", si)], ["glu"])
                  if t == NT - 1:
                      dma("sp", cH[j * 128:(j + 1) * 128, 0:30], s[:, 482:512], [("stg", si)], ["cH"])
          for j in range(4):
              wa, wak = load_w(win[l, 24 + j], 1024)
              wgl, wgk = load_w(win[l, 32 + j], 1024)
              for t in range(NT):
                  ba = rr("ps", 8)
                  mm_group(ba, [(wa[:, k * 128:(k + 1) * 128], A[:, k, t * TW:(t + 1) * TW]) for k in range(8)],
                           [wak] + [("A", k, t) for k in range(8)])
                  bg = rr("ps", 8)
                  mm_group(bg, [(wgl[:, k * 128:(k + 1) * 128], A[:, k, t * TW:(t + 1) * TW]) for k in range(8)],
                           [wgk] + [("A", k, t) for k in range(8)])
                  i = rr("ev", 4)
                  ev = evs[i]
                  act(lambda e, ev=ev, bg=bg: e.activation(out=ev[:, :], in_=ps[bg][:, :], func=AF.Copy),
                      [("ps", bg)], [("ev", i)])
                  si = rr("stg", 4)
                  s = stg[si]
                  dve(lambda e, s=s, ev=ev, ba=ba: e.tensor_tensor(out=s[:, :], in0=ps[ba][:, :], in1=ev[:, :], op=ALU.mult),
                      [("ps", ba), ("ev", i)], [("stg", si)])
                  dma("sp", sxc_s[j * 128:(j + 1) * 128, t * TW:(t + 1) * TW], s[:, :], [("stg", si)], ["sxc"])
                  if t == NT - 1:
                      dma("sp", cH[j * 128:(j + 1) * 128, 32:34], s[:, 510:512], [("stg", si)], ["cH"])
          chk("S1")
          for (ci_, co_, ki, ko) in ((cKA, oKA, "cKA", "oKA"), (cKFk, oKFk, "cKF", "oKF"), (cKFa, oKFa, "cKFa", "oKFa"), (cVs_[0], oVs_[0], "cV", "oV0"),
                                       (cVs_[1], oVs_[1], "cV", "oV1"), (cVs_[2], oVs_[2], "cV", "oV2"), (cH_, oH_, "cH", "oH")):
              P.add("pool", lambda e, ci_=ci_, co_=co_: e.collective_compute("AllGather", ALU.bypass, replica_groups=RG,
                                                                            ins=[ci_], outs=[co_]),
                    [ki], [ko], cc=True)

          fenceX()
          fenceB()
          lin_fm(Akey, A, 8, [win[l, c] for c in range(0, 4)],
                 evac_scaled_to(lambda ci: [(qa_s[ci * 128:(ci + 1) * 128, :], "qa", 0, 128)], 0.125))
          lin_fm(Akey, A, 8, [win[l, c] for c in range(8, 12)],
                 evac_scaled_to(lambda ci: [(qf_s[(2 * ci) * 68:(2 * ci) * 68 + 64, :], "qf", 0, 64),
                                            (qf_s[(2 * ci + 1) * 68:(2 * ci + 1) * 68 + 64, :], "qf", 64, 128)], 0.125))
          lin_fm(Akey, A, 8, [win[l, c] for c in range(28, 32)],
                 evac_scaled_to(lambda ci: [(sb_s[ci * 128:(ci + 1) * 128, :], "sb", 0, 128)], 1.0))

          chk("CC")
          def conv_gen():
              GB = ARC[:, 0:2176].rearrange("p (c n) -> p c n", c=4)
              ACC = ARC[:, 2176:6272].bitcast(F32).rearrange("p (c n) -> p c n", c=4)
              HL = ARC[:, 6272:6528].rearrange("p (c n) -> p c n", c=4)
              SB2 = ARC[:, 6528:8576].rearrange("p (c n) -> p c n", c=4)
              dma("sp", HL, oH[0:512, :].rearrange("(c p) n -> p c n", p=128), ["oH"], ["HL"])
              dve(lambda e: e.tensor_scalar(out=HL, in0=HL, scalar1=hflag, scalar2=None, op0=ALU.mult), ["HL", "flags"], ["HL"])
              for t in range(NT):
                  if t == 0:
                      dve(lambda e: e.tensor_copy(out=GB[:, :, 0:30], in_=HL[:, :, 0:30]), ["HL"], ["GB"])
                      dma("sp", GB[:, :, 30:542], glu_s[:, 0:TW].rearrange("(c p) n -> p c n", p=128), ["glu"], ["GB"])
                  else:
                      dma("sp", GB[:, :, 0:542], glu_s[:, t * TW - 30:(t + 1) * TW].rearrange("(c p) n -> p c n", p=128), ["glu"], ["GB"])
                  for c in range(4):
                      dve(lambda e, c=c: e.tensor_scalar(out=ACC[:, c, :], in0=GB[:, c, 0:512], scalar1=ppc(l, "cdw", c * 31),
                                                         scalar2=ppc(l, "cdwb", c), op0=ALU.mult, op1=ALU.add),
                          ["GB", "pp"], [("ACC", c)])
                      for k in range(1, 31):
                          dve(lambda e, c=c, k=k: e.scalar_tensor_tensor(out=ACC[:, c, :], in0=GB[:, c, k:k + 512],
                                                                         scalar=ppc(l, "cdw", c * 31 + k), in1=ACC[:, c, :],
                                                                         op0=ALU.mult, op1=ALU.add),
                              ["GB", "pp", ("ACC", c)], [("ACC", c)])
                          if k % 3 == 0:
                              yield
                      yield ("ln" if c == 3 else None)
                  for c in range(4):
                      act(lambda e, c=c: e.activation(out=sq[:, c, :], in_=ACC[:, c, :], func=AF.Copy), [("ACC", c)], [("sq", c)])
                      act(lambda e, c=c: e.activation(out=sq[:, 4 + c, :], in_=ACC[:, c, :], func=AF.Square), [("ACC", c)], [("sq", 4 + c)])
                  bm = 4 + rr("ps", 4)
                  mm_group(bm, [(mean512[:, :], sq[:, c, :]) for c in range(4)], [("sq", c) for c in range(4)] + ["consts"])
                  bq = 4 + rr("ps", 4)
                  mm_group(bq, [(mean512[:, :], sq[:, 4 + c, :]) for c in range(4)], [("sq", 4 + c) for c in range(4)] + ["consts"])
                  i0 = rr("ev", 4)
                  mu = evs[i0]
                  act(lambda e, mu=mu, bm=bm: e.activation(out=mu[:, :], in_=ps[bm][:, :], func=AF.Copy), [("ps", bm)], [("ev", i0)])
                  i1 = rr("ev", 4)
                  rs = evs[i1]
                  dve(lambda e, rs=rs, mu=mu: e.tensor_tensor(out=rs[:, :], in0=mu[:, :], in1=mu[:, :], op=ALU.mult), [("ev", i0)], [("ev", i1)])
                  dve(lambda e, rs=rs, bq=bq: e.tensor_tensor(out=rs[:, :], in0=ps[bq][:, :], in1=rs[:, :], op=ALU.subtract),
                      [("ps", bq), ("ev", i1)], [("ev", i1)])
                  act(lambda e, rs=rs: e.activation(out=rs[:, :], in_=rs[:, :], func=AF.Ln, bias=small[:, 2:3], scale=1.0), [("ev", i1), "small"], [("ev", i1)])
                  act(lambda e, rs=rs: e.activation(out=rs[:, :], in_=rs[:, :], func=AF.Exp, scale=-0.5), [("ev", i1)], [("ev", i1)])
                  for c in range(4):
                      dve(lambda e, c=c, mu=mu: e.tensor_tensor(out=ACC[:, c, :], in0=ACC[:, c, :], in1=mu[:, :], op=ALU.subtract),
                          [("ACC", c), ("ev", i0)], [("ACC", c)])
                      dve(lambda e, c=c, rs=rs: e.tensor_tensor(out=ACC[:, c, :], in0=ACC[:, c, :], in1=rs[:, :], op=ALU.mult),
                          [("ACC", c), ("ev", i1)], [("ACC", c)])
                      act(lambda e, c=c, t=t: e.activation(out=BR[:, 8 + c, t * TW:(t + 1) * TW], in_=ACC[:, c, :], func=AF.Silu,
                                                           bias=ppc(l, "clnb", c), scale=ppc(l, "clng", c)),
                          [("ACC", c), "pp"], [("BR", 8 + c, t)])
                  yield
                  if t == 0:
                      dve(lambda e: e.tensor_copy(out=GB[:, :, 0:2], in_=HL[:, :, 32:34]), ["HL", ("BR", 11, t)], ["GB"])
                      dma("sp", GB[:, :, 2:514], sxc_s[:, 0:TW].rearrange("(c p) n -> p c n", p=128), ["sxc"], ["GB"])
                  else:
                      dma("sp", GB[:, :, 0:514], sxc_s[:, t * TW - 2:(t + 1) * TW].rearrange("(c p) n -> p c n", p=128),
                          ["sxc", ("BR", 11, t)], ["GB"])
                  dma("sp", SB2, sb_s[:, t * TW:(t + 1) * TW].rearrange("(c p) n -> p c n", p=128), ["sb"], ["SB2"])
                  for c in range(4):
                      dve(lambda e, c=c: e.tensor_scalar(out=ACC[:, c, :], in0=GB[:, c, 0:512], scalar1=ppc(l, "scw", c * 3),
                                                         scalar2=None, op0=ALU.mult), ["GB", "pp"], [("ACC", c)])
                      for k in range(1, 3):
                          dve(lambda e, c=c, k=k: e.scalar_tensor_tensor(out=ACC[:, c, :], in0=GB[:, c, k:k + 512],
                                                                         scalar=ppc(l, "scw", c * 3 + k), in1=ACC[:, c, :],
                                                                         op0=ALU.mult, op1=ALU.add),
                              ["GB", "pp", ("ACC", c)], [("ACC", c)])
                      dve(lambda e, c=c, t=t: e.tensor_tensor(out=BR[:, 12 + c, t * TW:(t + 1) * TW], in0=ACC[:, c, :], in1=SB2[:, c, :], op=ALU.mult),
                          [("ACC", c), "SB2"], [("BR", 12 + c, t)])
              return

          chk("S2a")
          KT = ARB[:, 0:4096]
          QT = ARB[:, 4096:6144]
          VV = ARB[:, 6144:10240].rearrange("p (b c) -> p b c", b=32)
          KT1 = ARB[:, 10240:14336]
          oKA2 = oKA
          cg = conv_gen()
          dve(lambda e: e.memset(KT[64:128, :], 0.0), [], [("KT", 0), ("KT", 1)])
          dve(lambda e: e.memset(KT1[0:64, :], 0.0), [], [("KT1", 0), ("KT1", 1)])

          fbank = [0]
          cg_state = [None]

          def pull(allow_psum):
              if cg_state[0] == "ln" and not allow_psum:
                  return
              cg_state[0] = next(cg, None)

          def attention(rows_list, kind, h, finalize):
              for j in range(NT):
                  steps = []
                  for kb in range(16):
                      steps.append((kb, 0, True))
                  for kb in range(4 * j + 4):
                      q0 = max(0, kb - 4 * j) * 128
                      steps.append((16 + kb, q0, False))
                  seq = [(s_, ci) for s_ in steps for ci in range(len(rows_list))]
                  sbank = {}

                  def qk(idx):
                      (kb, q0, prev), ci = seq[idx]
                      kbuf, kkey, r0, r1 = rows_list[ci]
                      b = 4 + rr("ps", 4)
                      sbank[idx] = b
                      if MASK_PE and (not prev) and kb - 16 >= 4 * j:
                          mo = 0 if kind == "A" else 128

                          def fn(e, b=b, kbuf=kbuf, r0=r0, r1=r1, kb=kb, q0=q0, mo=mo, j=j):
                              e.matmul(ps[b][:, q0:512], kbuf[r0:r1, kb * 128:(kb + 1) * 128], QT[r0:r1, j * TW + q0:(j + 1) * TW],
                                       start=True, stop=True)
                              return e.matmul(ps[b][:, q0:q0 + 128], masks[:, 256:384], masks[:, mo:mo + 128], start=False, stop=True)
                          pe(fn, [(kkey, kb // 16), ("QT", j), "masks"], [("ps", b)])
                      else:
                          mm_group(b, [(kbuf[r0:r1, kb * 128:(kb + 1) * 128], QT[r0:r1, j * TW + q0:(j + 1) * TW])],
                                   [(kkey, kb // 16), ("QT", j)], q0, 512)
                  for idx in range(min(3, len(seq))):
                      qk(idx)
                  for idx in range(len(seq)):
                      (kb, q0, prev), ci = seq[idx]
                      b = sbank[idx]
                      pi = rr("pt", 4)
                      pt = PTs[pi]
                      if prev:
                          act(lambda e, pt=pt, b=b: e.activation(out=pt[:, :], in_=ps[b][:, :], func=AF.Exp, bias=pmask, scale=1.0),
                              [("ps", b), "flags"], [("pt", pi)])
                      else:
                          act(lambda e, pt=pt, b=b, q0=q0: e.activation(out=pt[:, q0:512], in_=ps[b][:, q0:512], func=AF.Exp),
                              [("ps", b)], [("pt", pi)])
                          if (not MASK_PE) and kb - 16 >= 4 * j:
                              mo = 0 if kind == "A" else 128
                              dve(lambda e, pt=pt, q0=q0, mo=mo: e.tensor_tensor(out=pt[:, q0:q0 + 128], in0=pt[:, q0:q0 + 128],
                                                                                 in1=masks[:, mo:mo + 128], op=ALU.mult),
                                  [("pt", pi), "masks"], [("pt", pi)])
                      if idx + 3 < len(seq):
                          qk(idx + 3)
                      if idx == len(seq) // 2:
                          flush_pending()
                      if idx % 8 == 7:
                          pull(False)
                      first = (idx < len(rows_list))
                      last = (idx >= len(seq) - len(rows_list))
                      ob = ci if kind == "A" else (fbank[0] if F_PINGPONG else 0)
                      pe(lambda e, ob=ob, kb=kb, pt=pt, q0=q0, first=first, last=last:
                         e.matmul(ps[ob][:, q0:512], VV[:, kb, :], pt[:, q0:512], start=first, stop=last),
                         [("VV", kb // 16), ("pt", pi)], [("ps", ob)])
                      if kind == "A":
                          pe(lambda e, ob=ob, pt=pt, q0=q0, first=first, last=last:
                             e.matmul(ps[2 + ob][:, q0:512], ones_bf[:, :], pt[:, q0:512], start=first, stop=last),
                             ["consts", ("pt", pi)], [("ps", 2 + ob)])
                  finalize(h, j)
                  fbank[0] ^= 1
                  pull(True)

          SQF = sq[:, :, :].rearrange("p c n -> p (c n)").bitcast(F32).rearrange("p (c n) -> p c n", c=4)
          SQK = [("sq", c) for c in range(8)]

          YAP = vst[:, :, :].rearrange("p c n -> p (c n)").bitcast(F32)
          pending = []

          def flush_pending():
              while pending:
                  pending.pop(0)()

          def fin_A(h, j):
              for bi in (0, 2, 1, 3):
                  dve(lambda e, bi=bi: e.tensor_copy(out=SQF[:, bi, :], in_=ps[bi][:, :]), [("ps", bi)], SQK)
              dve(lambda e: e.reciprocal(out=SQF[:, 2, :], in_=SQF[:, 2, :]), SQK, SQK)
              dve(lambda e: e.tensor_tensor(out=SQF[:, 0, :], in0=SQF[:, 0, :], in1=SQF[:, 2, :], op=ALU.mult), SQK, SQK)
              dve(lambda e: e.reciprocal(out=SQF[:, 3, :], in_=SQF[:, 3, :]), SQK, SQK)
              dve(lambda e: e.tensor_tensor(out=SQF[:, 1, :], in0=SQF[:, 1, :], in1=SQF[:, 3, :], op=ALU.mult), SQK, SQK)
              dve(lambda e: e.scalar_tensor_tensor(out=YAP, in0=SQF[:, 1, :], scalar=small[:, 0:1], in1=SQF[:, 0, :],
                                                   op0=ALU.mult, op1=ALU.add), SQK + ["small"], ["vst"])

              def part2():
                  si = rr("stg", 4)
                  sg = stg[si]
                  act(lambda e: e.activation(out=sg[:, :], in_=YAP, func=AF.Square), ["vst"], [("stg", si)])
                  b = 4 + rr("ps", 4)
                  mm_group(b, [(ones_bf[:, :], sg[:, :])], [("stg", si), "consts"])
                  i = rr("ev", 4)
                  r = evs[i]
                  act(lambda e: e.activation(out=r[:, :], in_=ps[b][:, :], func=AF.Ln, bias=small[:, 2:3], scale=1.0 / 128),
                      [("ps", b), "small"], [("ev", i)])
                  act(lambda e: e.activation(out=r[:, :], in_=r[:, :], func=AF.Exp, scale=-0.5), [("ev", i)], [("ev", i)])
                  dve(lambda e: e.scalar_tensor_tensor(out=BR[:, h, j * TW:(j + 1) * TW], in0=YAP, scalar=small[:, 1:2], in1=r[:, :],
                                                       op0=ALU.mult, op1=ALU.mult), ["vst", ("ev", i), "small"], [("BR", h, j)])
              pending.append(part2)

          def fin_F(h, j):
              i = rr("ev", 4)
              ev = evs[i]
              i2 = rr("ev", 4)
              dn = evs[i2]
              p0 = (h % 2) * 64
              zb = fbank[0] if F_PINGPONG else 0
              dve(lambda e: e.tensor_copy(out=ev[:, :], in_=ps[zb][:, :]), [("ps", zb)], [("ev", i)])
              dma("sp", dn[0:64, :], ev[64:128, :], [("ev", i)], [("ev", i2)])
              dve(lambda e: e.reciprocal(out=dn[0:64, :], in_=dn[0:64, :]), [("ev", i2)], [("ev", i2)])
              dve(lambda e: e.tensor_tensor(out=BR[p0:p0 + 64, 4 + h // 2, j * TW:(j + 1) * TW], in0=ev[0:64, :], in1=dn[0:64, :], op=ALU.mult),
                  [("ev", i), ("ev", i2)], [("BR", 4 + h // 2, j)])

          def load_q(src, rows):
              for j in range(NT):
                  dma("sp", QT[0:rows, j * TW:(j + 1) * TW], src[:, j * TW:(j + 1) * TW], ["qa", "qf"], [("QT", j)])

          for h in range(4):
              dma("sp", KT[0:64, 0:2048], oKA2[h * 128:h * 128 + 64, :], ["oKA"], [("KT", 0)])
              dma("sp", KT1[64:128, 0:2048], oKA2[h * 128 + 64:(h + 1) * 128, :], ["oKA"], [("KT1", 0)])
              load_q(qa_s[h * 128:(h + 1) * 128, :], 128)
              dma("sp", VV[:, 0:16, :], oVg[0][0, h].rearrange("(b p) c -> p b c", p=128), ["oV0"], [("VV", 0)])
              dma("sp", KT[0:64, 2048:4096], cKA[h * 128:h * 128 + 64, :], ["cKA"], [("KT", 1)])
              dma("sp", KT1[64:128, 2048:4096], cKA[h * 128 + 64:(h + 1) * 128, :], ["cKA"], [("KT1", 1)])
              dma("sp", VV[:, 16:32, :], cVg[0][h].rearrange("(b p) c -> p b c", p=128), ["cV"], [("VV", 1)])
              attention([(KT, "KT", 0, 128), (KT1, "KT1", 0, 128)], "A", h, fin_A)
          for h in range(8):
              dma("sp", KT[0:64, 0:2048], oKFk[h * 64:(h + 1) * 64, :], ["oKF"], [("KT", 0)])
              dma("sp", KT[64:68, 0:2048], oKFa[h * 4:(h + 1) * 4, :], ["oKFa"], [("KT", 0)])
              load_q(qf3[h], 68)
              dma("sp", VV[:, 0:16, :], oVg[1 + h // 4][0, h % 4].rearrange("(b p) c -> p b c", p=128), ["oV%d" % (1 + h // 4)], [("VV", 0)])
              dma("sp", KT[0:68, 2048:4096], kfo3[h], ["kfo"], [("KT", 1)])
              dma("sp", VV[:, 16:32, :], cVg[1 + h // 4][h % 4].rearrange("(b p) c -> p b c", p=128), ["cV"], [("VV", 1)])
              attention([(KT, "KT", 0, 68)], "F", h, fin_F)

          flush_pending()
          for _ in cg:
              pass
          chk("S2b")
          for d in range(8):
              wbs, wbk = load_w(wb[l, d], 2048)
              wgs, wgk = load_w(wg[l, d], 4096)
              for t in range(NT):
                  pb = []
                  for n in range(4):
                      b = n
                      mm_group(b, [(wbs[:, (n * 4 + k) * 128:(n * 4 + k + 1) * 128], BR[:, n * 4 + k, t * TW:(t + 1) * TW]) for k in range(4)],
                               [wbk] + [("BR", n * 4 + k, t) for k in range(4)])
                      pb.append(b)
                  i = rr("ev", 4)
                  acc = evs[i]
                  for n in range(4):
                      b = 4 + n
                      mm_group(b, [(wgs[:, (n * 8 + k) * 128:(n * 8 + k + 1) * 128], A[:, k, t * TW:(t + 1) * TW]) for k in range(8)],
                               [wgk] + [("A", k, t) for k in range(8)])
                      gi = rr("pt", 4)
                      gt = PTs[gi]
                      act(lambda e, gt=gt, b=b, n=n, d=d: e.activation(out=gt[:, :], in_=ps[b][:, :], func=AF.Sigmoid,
                                                                      bias=ppc(l, "bgate", n * 8 + d), scale=1.0),
                          [("ps", b), "pp"], [("pt", gi)])
                      if n == 0:
                          dve(lambda e, gt=gt, acc=acc: e.tensor_tensor(out=acc[:, :], in0=ps[0][:, :], in1=gt[:, :], op=ALU.mult),
                              [("ps", 0), ("pt", gi)], [("ev", i)])
                      else:
                          ti = rr("ev", 4)
                          if ti == i:
                              ti = rr("ev", 4)
                          tm = evs[ti]
                          dve(lambda e, gt=gt, tm=tm, n=n: e.tensor_tensor(out=tm[:, :], in0=ps[n][:, :], in1=gt[:, :], op=ALU.mult),
                              [("ps", n), ("pt", gi)], [("ev", ti)])
                          dve(lambda e, tm=tm, acc=acc: e.tensor_tensor(out=acc[:, :], in0=acc[:, :], in1=tm[:, :], op=ALU.add),
                              [("ev", ti), ("ev", i)], [("ev", i)])
                  si = rr("stg", 4)
                  s = stg[si]
                  act(lambda e, s=s, acc=acc: e.activation(out=s[:, :], in_=acc[:, :], func=AF.Copy), [("ev", i)], [("stg", si)])
                  dma("sp", mix_s[d, :, t * TW:(t + 1) * TW], s[:, :], [("stg", si)], [("mix", t)])

          chk("S3")
          def y_to_ysc(ci, t, b):
              i = rr("ev", 4)
              ev = evs[i]
              act(lambda e, ev=ev, b=b: e.activation(out=ev[:, :], in_=ps[b][:, :], func=AF.Copy), [("ps", b)], [("ev", i)])
              dma("sp", ysc[ci, :, t * TW:(t + 1) * TW], ev[:, :], [("ev", i)], [("ysc", t)])

          for t in range(NT):
              for c in range(8):
                  dma("sp", A[:, c, t * TW:(t + 1) * TW], mix_s[c, :, t * TW:(t + 1) * TW], [("mix", t)], [("A", c, t)])
          lin_fm(Akey, A, 8, [wo[l, d] for d in range(8)], y_to_ysc)
          fenceX()
          post_pass(l, "nmo", "nxp", l)
          fenceX()

          chk("S4")
          fenceB()
          MX = XF[:, 8192:10240].rearrange("p (c n) -> p c n", c=8)
          XQ = ARX[:, 20480:28672].rearrange("p (c n) -> p c n", c=4)
          dma("sp", MX, memT_in.rearrange("(c p) n -> p c n", p=128), [], ["MX"])
          for c in range(8):
              act(lambda e, c=c: e.activation(out=sq[:, c, 0:256], in_=MX[:, c, :], func=AF.Square), ["MX"], [("sq", c)])
          r, rk = rstd_from_sq(8, meanD, [("sq", c) for c in range(8)], n=256)
          for c in range(8):
              dve(lambda e, c=c, r=r: e.scalar_tensor_tensor(out=mTb[:, c, :], in0=MX[:, c, :], scalar=ppc(l, "nmem", c), in1=r[:, 0:256],
                                                          op0=ALU.mult, op1=ALU.mult), ["MX", rk, "pp"], ["mTb"])
          for h in range(4):
              w, wk = load_w(wxk[l, h], 1024)
              b = rr("ps", 8)
              mm_group(b, [(w[:, k * 128:(k + 1) * 128], mTb[:, k, :]) for k in range(8)], [wk, "mTb"], 0, 256)
              act(lambda e, b=b, h=h: e.activation(out=xkT[:, h, :], in_=ps[b][:, 0:256], func=AF.Copy), [("ps", b)], ["xkT"])
          w, wk = load_w(wxv[l], 4096)
          for kc2 in range(2):
              b = rr("ps", 8)
              mm_group(b, [(mTb[:, k, kc2 * 128:(kc2 + 1) * 128], w[:, k * 512:(k + 1) * 512]) for k in range(8)], [wk, "mTb"])
              act(lambda e, b=b, kc2=kc2: e.activation(out=xvs[:, kc2, :], in_=ps[b][:, :], func=AF.Copy), [("ps", b)], ["xvs"])

          def xq_h(ci, t, b):
              act(lambda e, b=b, ci=ci, t=t: e.activation(out=XQ[:, ci, t * TW:(t + 1) * TW], in_=ps[b][:, :], func=AF.Copy),
                  [("ps", b)], [("XQ", ci, t)])
          lin_fm(Akey, A, 8, [wxq[l, h] for h in range(4)], xq_h)
          xscale = 128 ** -0.5
          OX = ARX[:, 0:8192].rearrange("p (c n) -> p c n", c=4)
          for h in range(4):
              for t in range(NT):
                  pts = []
                  bo = 2 * ((h * NT + t) % 2)
                  for kc2 in range(2):
                      b = 4 + rr("ps", 4)
                      mm_group(b, [(xkT[:, h, kc2 * 128:(kc2 + 1) * 128], XQ[:, h, t * TW:(t + 1) * TW])], ["xkT", ("XQ", h, t)])
                      pi = rr("pt", 4)
                      pt = PTs[pi]
                      act(lambda e, pt=pt, b=b: e.activation(out=pt[:, :], in_=ps[b][:, :], func=AF.Exp, scale=xscale), [("ps", b)], [("pt", pi)])
                      pts.append((pt, pi))
                  mm_group(bo, [(xvs[:, kc2, h * 128:(h + 1) * 128], pts[kc2][0][:, :]) for kc2 in range(2)],
                           ["xvs"] + [("pt", p_[1]) for p_ in pts])
                  mm_group(bo + 1, [(ones_bf[:, :], pts[kc2][0][:, :]) for kc2 in range(2)], ["consts"] + [("pt", p_[1]) for p_ in pts])
                  i = rr("ev", 4)
                  ev = evs[i]
                  act(lambda e, ev=ev, bo=bo: e.activation(out=ev[:, :], in_=ps[bo + 1][:, :], func=AF.Ln), [("ps", bo + 1)], [("ev", i)])
                  act(lambda e, ev=ev: e.activation(out=ev[:, :], in_=ev[:, :], func=AF.Exp, scale=-1.0), [("ev", i)], [("ev", i)])
                  dve(lambda e, ev=ev, h=h, t=t, bo=bo: e.tensor_tensor(out=OX[:, h, t * TW:(t + 1) * TW], in0=ps[bo][:, :], in1=ev[:, :], op=ALU.mult),
                      [("ps", bo), ("ev", i)], [("OX", h, t)])
          lin_fm(lambda k, t: ("OX", k, t), OX, 4, [wxo[l, d] for d in range(8)], y_to_ysc)
          fenceX()
          post_pass(l, "nxo", "nfp", l)

          chk("S5")
          dma("sp", cF.rearrange("(c p) n -> p c n", p=128), A[:, :, 2046:2048], [("A", c, 3) for c in range(8)], ["cF"])
          P.add("pool", lambda e: e.collective_compute("AllGather", ALU.bypass, replica_groups=RG, ins=[cF_], outs=[oF_]),
                ["cF"], ["oF", "ccorder"], cc=True)
          dma("sp", hfh2[:, :, :], oF[0:1024, :].rearrange("(c p) n -> p c n", p=128), ["oF"], ["hfh2"])
          dve(lambda e: e.tensor_scalar(out=hfh[:, :, :], in0=hfh2[:, :, :], scalar1=hflag, scalar2=None, op0=ALU.mult), ["hfh2", "flags"], ["hfh"])
          ACT_T = ARX[:, 0:22528].rearrange("p (g n) -> p g n", g=NG)
          fenceX()
          for half in range(2):
              for g in range(NG):
                  i = rr("w", 3)
                  w = wsl[i]
                  wk = ("w", i)
                  dma("pool", w[:, 0:1024], wup[l, 2 * g], [], [wk])
                  dma("pool", w[:, 1024:2048], wup[l, 2 * g + 1], [], [wk])
                  if half == 0:
                      for gv in range(2):
                          b = rr("ps", 8)
                          mm_group(b, [(w[:, gv * 1024 + k * 128: gv * 1024 + (k + 1) * 128], hfh[:, k, :]) for k in range(8)],
                                   [wk, "hfh"], 0, 2)
                          dve(lambda e, b=b, g=g, gv=gv: e.tensor_copy(out=uh[:, 2 * g + gv, :], in_=ps[b][:, 0:2]), [("ps", b)], [("uh", g)])
                  for tt in range(2):
                      t = half * 2 + tt
                      res = []
                      for gv in range(2):
                          b = rr("ps", 8)
                          mm_group(b, [(w[:, gv * 1024 + k * 128: gv * 1024 + (k + 1) * 128], A[:, k, t * TW:(t + 1) * TW]) for k in range(8)],
                                   [wk] + [("A", k, t) for k in range(8)])
                          ch = 2 * g + gv
                          col = (gv * NG + g)
                          i2 = rr("ev", 4)
                          ev = evs[i2]
                          act(lambda e, ev=ev, b=b, col=col: e.activation(out=ev[:, :], in_=ps[b][:, :], func=AF.Identity,
                                                                          scale=ppc(l, "fdw", 2 * 44 + col), bias=ppc(l, "fdwb", col)),
                              [("ps", b), "pp"], [("ev", i2)])
                          dve(lambda e, ev=ev, b=b, col=col: e.scalar_tensor_tensor(out=ev[:, 1:512], in0=ps[b][:, 0:511], scalar=ppc(l, "fdw", 44 + col),
                                                                                    in1=ev[:, 1:512], op0=ALU.mult, op1=ALU.add),
                              [("ps", b), "pp", ("ev", i2)], [("ev", i2)])
                          dve(lambda e, ev=ev, b=b, col=col: e.scalar_tensor_tensor(out=ev[:, 2:512], in0=ps[b][:, 0:510], scalar=ppc(l, "fdw", col),
                                                                                    in1=ev[:, 2:512], op0=ALU.mult, op1=ALU.add),
                              [("ps", b), "pp", ("ev", i2)], [("ev", i2)])
                          dve(lambda e, ev=ev, ch=ch, col=col: e.scalar_tensor_tensor(out=ev[:, 0:1], in0=uh[:, ch, 1:2], scalar=ppc(l, "fdw", 44 + col),
                                                                                      in1=ev[:, 0:1], op0=ALU.mult, op1=ALU.add),
                              [("uh", g), "pp", ("ev", i2)], [("ev", i2)])
                          dve(lambda e, ev=ev, ch=ch, col=col: e.scalar_tensor_tensor(out=ev[:, 0:2], in0=uh[:, ch, 0:2], scalar=ppc(l, "fdw", col),
                                                                                      in1=ev[:, 0:2], op0=ALU.mult, op1=ALU.add),
                              [("uh", g), "pp", ("ev", i2)], [("ev", i2)])
                          dve(lambda e, b=b, ch=ch: e.tensor_copy(out=uh[:, ch, :], in_=ps[b][:, 510:512]), [("ps", b), ("ev", i2)], [("uh", g)])
                          res.append((ev, i2))
                      (eg, ig), (evv, iv) = res
                      si = rr("stg", 4)
                      s = stg[si]
                      act(lambda e, s=s, eg=eg: e.activation(out=s[:, :], in_=eg[:, :], func=AF.Silu), [("ev", ig)], [("stg", si)])
                      dve(lambda e, s=s, evv=evv, g=g, tt=tt: e.tensor_tensor(out=ACT_T[:, g, tt * TW:(tt + 1) * TW], in0=s[:, :], in1=evv[:, :], op=ALU.mult),
                          [("stg", si), ("ev", iv)], [("ACT_T", g, tt)])
              for d in range(8):
                  w, wk = load_w(wdn[l, d], 2816)
                  for tt in range(2):
                      t = half * 2 + tt
                      b = rr("ps", 8)
                      mm_group(b, [(w[:, g * 128:(g + 1) * 128], ACT_T[:, g, tt * TW:(tt + 1) * TW]) for g in range(NG)],
                               [wk] + [("ACT_T", g, tt) for g in range(NG)])
                      y_to_ysc(d, t, b)
          fenceX()
          last = (l == nlayers - 1)
          post_pass(l, "nfo", "nmp", min(l + 1, L - 1), final=last)

    for l_ in range(nlayers if stop != "S0" else 0):
        try:
            do_layer(l_)
        except _Stop:
            break

    P.add("sp", lambda e: e.nop(), ["out"], [])

    P.finalize(nc, es)
    with es:
        with nc.Block() as block:
            @block.tensor
            def _(e):
                P.emit("pe", e)

            @block.scalar
            def _(e):
                P.emit("act", e)

            @block.vector
            def _(e):
                P.emit("dve", e)

            @block.gpsimd
            def _(e):
                P.emit("pool", e)

            @block.sync
            def _(e):
                P.emit("sp", e)
    return nc, P


def _chunks_fm(W, cols):
    K = W.shape[0]
    kc = K // 128
    outl = []
    for c0 in cols:
        blk = W[:, c0:c0 + 128].reshape(kc, 128, 128).transpose(1, 0, 2)
        outl.append(blk.reshape(128, kc * 128))
    return np.ascontiguousarray(np.stack(outl, 0))


def _mov(W):
    K, n = W.shape
    return np.ascontiguousarray(W.reshape(K // 128, 128, n).transpose(1, 0, 2).reshape(128, (K // 128) * n))


def _cols(v):
    return np.ascontiguousarray(np.asarray(v, np.float32).reshape(-1, 128).T)


def prep_inputs(inp):
    f = lambda k: np.asarray(inp[k], np.float32)
    w_in, w_branch, w_gate, w_out = f("w_in"), f("w_branch"), f("w_gate"), f("w_out")
    w_xq, w_xkv, w_xo, w_up, w_down = f("w_xq"), f("w_xkv"), f("w_xo"), f("w_up"), f("w_down")
    seg = [0, 512, 1536, 2048, 3080, 3592, 4104, 4616, 5128]
    cols36 = [s + j * 128 for s in seg for j in range(4)]
    shared = {}
    shared["win"] = np.stack([_chunks_fm(w_in[l], cols36) for l in range(L)], 0)
    shared["wfg"] = np.stack([_mov(w_in[l][:, 3072:3080]) for l in range(L)], 0)
    shared["wv"] = np.stack([np.stack([_mov(w_in[l][:, 1024:1536]), _mov(w_in[l][:, 2560:3072])], 0) for l in range(L)], 0)
    wb = np.zeros((L, 8, 128, 2048), np.float32)
    wg = np.zeros((L, 8, 128, 4096), np.float32)
    for l in range(L):
        for d in range(8):
            wb[l, d] = np.concatenate([_chunks_fm(w_branch[l, n], [d * 128])[0] for n in range(4)], 1)
            wg[l, d] = np.concatenate([_chunks_fm(w_gate[l], [n * 1024 + d * 128])[0] for n in range(4)], 1)
    shared["wb"], shared["wg"] = wb, wg
    shared["wo"] = np.stack([_chunks_fm(w_out[l], [d * 128 for d in range(8)]) for l in range(L)], 0)
    shared["wxq"] = np.stack([_chunks_fm(w_xq[l], [h * 128 for h in range(4)]) for l in range(L)], 0)
    shared["wxk"] = np.stack([_chunks_fm(w_xkv[l], [h * 128 for h in range(4)]) for l in range(L)], 0)
    shared["wxv"] = np.stack([_mov(w_xkv[l][:, 512:1024]) for l in range(L)], 0)
    shared["wxo"] = np.stack([_chunks_fm(w_xo[l], [d * 128 for d in range(8)]) for l in range(L)], 0)
    upcols = []
    for g in range(NG):
        upcols += [g * 128, DFF + g * 128]
    shared["wup"] = np.stack([_chunks_fm(w_up[l], upcols) for l in range(L)], 0)
    shared["wdn"] = np.stack([_chunks_fm(w_down[l], [d * 128 for d in range(8)]) for l in range(L)], 0)
    pp = np.zeros((128, NPP), np.float32)

    def put(l, name, arr):
        o, w = PP[name]
        assert arr.shape == (128, w), (name, arr.shape)
        pp[:, l * PPL + o:l * PPL + o + w] = arr
    for l in range(L):
        for nm, key in (("nmp", "norm_mix_pre"), ("nmo", "norm_mix_post"), ("nxp", "norm_x_pre"), ("nxo", "norm_x_post"),
                        ("nmem", "norm_mem"), ("nfp", "norm_ffn_pre"), ("nfo", "norm_ffn_post"), ("bglu", "b_glu"),
                        ("cdwb", "conv_dw_b"), ("clng", "conv_ln_g"), ("clnb", "conv_ln_b"), ("bgate", "b_gate"),
                        ("fdwb", "ffn_dw_b"), ("dnorm", "diff_norm")):
            put(l, nm, _cols(f(key)[l]))
        cd = f("conv_dw")[l]
        put(l, "cdw", np.concatenate([cd[:, c * 128:(c + 1) * 128].T for c in range(4)], 1))
        sc = f("sc_w")[l]
        put(l, "scw", np.concatenate([sc[:, c * 128:(c + 1) * 128].T for c in range(4)], 1))
        fd = f("ffn_dw")[l]
        put(l, "fdw", np.concatenate([_cols(fd[k]) for k in range(3)], 1))
        bf = np.zeros((128, 1), np.float32)
        bf[0:8, 0] = f("b_fgt")[l]
        put(l, "bfgt", bf)
        for nm, key in (("lq1", "lam_q1"), ("lk1", "lam_k1"), ("lq2", "lam_q2"), ("lk2", "lam_k2")):
            put(l, nm, np.broadcast_to(f(key)[l][None, :], (128, 64)).copy())
    shared["pp"] = pp
    kk = np.arange(128)[:, None]
    qq = np.arange(128)[None, :]
    if MASK_PE:
        shared["masks"] = np.concatenate([np.where(kk // 64 <= qq // 64, 0.0, NEG), np.where(kk <= qq, 0.0, NEG),
                                          np.eye(128)], 1).astype(np.float32)
    else:
        shared["masks"] = np.concatenate([(kk // 64 <= qq // 64), (kk <= qq), np.eye(128)], 1).astype(np.float32)
    x = f("x")
    mem = f("mem")
    maps = []
    for c in range(8):
        b, hf = c // 2, c % 2
        m = dict(shared)
        m["xT"] = np.ascontiguousarray(x[b, hf * T:(hf + 1) * T, :].T)
        m["memT"] = np.ascontiguousarray(mem[b].T)
        fl = np.zeros((128, 2), np.float32)
        fl[:, 0] = 0.0 if hf == 1 else NEG
        fl[:, 1] = 1.0 if hf == 1 else 0.0
        m["flags"] = fl
        maps.append(m)
    return maps


_NC = {}


def kernel(**inputs):
    if "nc" not in _NC:
        _NC["nc"] = build(L)[0]
    maps = prep_inputs(inputs)
    res = run_bass_kernel_spmd(_NC["nc"], maps, core_ids=list(range(8)))
    outp = np.zeros((4, 2 * T, D), np.float32)
    for c in range(8):
        b, hf = c // 2, c % 2
        outp[b, hf * T:(hf + 1) * T, :] = np.asarray(res.results[c]["out"]).T
    return outp
```

```python
import math
import numpy as np
import concourse.bass as bass
import concourse.mybir as mybir
from concourse.bass_utils import run_bass_kernel_spmd
from contextlib import ExitStack

F32 = mybir.dt.float32
BF16 = mybir.dt.bfloat16
ALU = mybir.AluOpType
AF = mybir.ActivationFunctionType
AX = mybir.AxisListType

L = 4
D = 1024
T = 2048
NT = 4
TW = 512
KC = 8
EPS = 1e-6
NEG = -30000.0
DFF = 2816
NG = 22
MASK_PE = True
F_PINGPONG = True

PP = {}
_o = 0
for _n, _w in [("nmp", 8), ("nmo", 8), ("nxp", 8), ("nxo", 8), ("nmem", 8), ("nfp", 8), ("nfo", 8),
               ("bglu", 8), ("cdw", 124), ("cdwb", 4), ("clng", 4), ("clnb", 4), ("scw", 12),
               ("bgate", 32), ("fdw", 132), ("fdwb", 44), ("dnorm", 1), ("bfgt", 1),
               ("lq1", 64), ("lk1", 64), ("lq2", 64), ("lk2", 64)]:
    PP[_n] = (_o, _w)
    _o += _w
PPL = _o
NPP = PPL * L


class _Op:
    __slots__ = ("eng", "fn", "deps", "signal", "sigidx", "dma", "sem", "semval", "prev", "idx", "cc")


class Prog:
    ENGS = ("pe", "act", "dve", "pool", "sp")
    KQ = 8

    def __init__(self):
        self.ops = []
        self.lastw = {}
        self.readers = {}
        self.gnames = set()
        self.groups = {}

    def _expand(self, reads, writes):
        r2, w2, extra = [], [], []
        for k in reads:
            if k in self.gnames:
                g = self.groups.setdefault(k, {"mem": [], "read": False, "n": 0})
                g["read"] = True
                r2.extend(g["mem"])
            else:
                r2.append(k)
        for k in writes:
            if k in self.gnames:
                g = self.groups.setdefault(k, {"mem": [], "read": False, "n": 0})
                if g["read"]:
                    extra.extend(g["mem"])
                    g["mem"] = []
                    g["read"] = False
                g["n"] += 1
                sk = ("#g", k, g["n"])
                g["mem"].append(sk)
                w2.append(sk)
            else:
                w2.append(k)
        return r2, w2, extra

    def add(self, eng, fn, reads=(), writes=(), dma=False, cc=False):
        op = _Op()
        op.eng, op.fn, op.dma, op.cc = eng, fn, dma or cc, cc
        op.signal = op.dma
        op.idx = len(self.ops)
        op.sem = op.semval = op.prev = op.sigidx = None
        deps = set()
        reads, writes, extra = self._expand(list(reads), list(writes))
        for k in extra:
            w = self.lastw.get(k)
            if w is not None:
                deps.add(w)
            for rd in self.readers.get(k, ()):
                deps.add(rd)
        for r in reads:
            w = self.lastw.get(r)
            if w is not None:
                deps.add(w)
        for k in writes:
            w = self.lastw.get(k)
            if w is not None:
                deps.add(w)
            for rd in self.readers.get(k, ()):
                deps.add(rd)
        op.deps = deps
        for d in deps:
            self.ops[d].signal = True
        for r in reads:
            self.readers.setdefault(r, []).append(op.idx)
        for k in writes:
            self.lastw[k] = op.idx
            self.readers[k] = []
        self.ops.append(op)
        return op.idx

    def finalize(self, nc, es):
        self.sems = {e: es.enter_context(nc.semaphore("pg_" + e)) for e in self.ENGS}
        self.dsems = {q: [es.enter_context(nc.semaphore("dq_%s%d" % (q, i))) for i in range(self.KQ)]
                      for q in ("sp", "pool", "act")}
        cnt = {e: 0 for e in self.ENGS}
        dq = {"sp": [], "pool": [], "act": []}
        for op in self.ops:
            if op.cc:
                op.sem = es.enter_context(nc.semaphore("cc%d" % op.idx))
                op.semval = 1
            elif op.dma:
                lst = dq[op.eng]
                n = len(lst)
                op.sem = self.dsems[op.eng][n % self.KQ]
                op.semval = 16 * (n // self.KQ + 1)
                op.prev = lst[n - self.KQ] if n >= self.KQ else None
                lst.append(op.idx)
            elif op.signal:
                cnt[op.eng] += 1
                op.sigidx = cnt[op.eng]

    def emit(self, ename, eng):
        seen = {}
        ops = self.ops

        def wait(sem, key, val):
            if seen.get(key, 0) < val:
                eng.wait_ge(sem, val)
                seen[key] = val

        for op in ops:
            if op.eng != ename:
                continue
            for d in sorted(op.deps):
                dop = ops[d]
                if dop.dma:
                    wait(dop.sem, ("d", id(dop.sem)), dop.semval)
                else:
                    if dop.eng == ename and ename == "pe" and not op.dma:
                        continue
                    wait(self.sems[dop.eng], dop.eng, dop.sigidx)
            if op.dma and not op.cc and op.prev is not None:
                p = ops[op.prev]
                wait(p.sem, ("d", id(p.sem)), p.semval)
            inst = op.fn(eng)
            if op.cc:
                inst.then_inc(op.sem)
            elif op.dma:
                inst.then_inc(op.sem, 16)
            elif op.signal:
                inst.then_inc(self.sems[ename], 1)


class _Stop(Exception):
    pass


def build(nlayers=L, stop=None):
    nc = bass.Bass("TRN2", target_bir_lowering=False)
    P = Prog()
    P.gnames = set(["qa", "cKA", "qf", "kfo", "cKF", "cKFa", "cV", "cH", "glu", "sxc", "sb"]
                   + [("mix", t) for t in range(NT)] + [("ysc", t) for t in range(NT)])

    def chk(name):
        if stop == name:
            raise _Stop()

    def din(name, shape):
        return nc.dram_tensor(name, list(shape), F32, kind="ExternalInput").ap()

    xT_in = din("xT", [D, T])
    memT_in = din("memT", [D, 256])
    pp_in = din("pp", [128, NPP])
    flags_in = din("flags", [128, 2])
    masks_in = din("masks", [128, 384])
    win = din("win", [L, 36, 128, 1024])
    wfg = din("wfg", [L, 128, 64])
    wv = din("wv", [L, 2, 128, 4096])
    wb = din("wb", [L, 8, 128, 2048])
    wg = din("wg", [L, 8, 128, 4096])
    wo = din("wo", [L, 8, 128, 1024])
    wxq = din("wxq", [L, 4, 128, 1024])
    wxk = din("wxk", [L, 4, 128, 1024])
    wxv = din("wxv", [L, 128, 4096])
    wxo = din("wxo", [L, 8, 128, 512])
    wup = din("wup", [L, 44, 128, 1024])
    wdn = din("wdn", [L, 8, 128, 2816])
    out = nc.dram_tensor("out", [D, T], F32, kind="ExternalOutput").ap()

    def dscr(name, shape, dt):
        return nc.dram_tensor(name, list(shape), dt).ap()

    xs = dscr("xs", [8, 128, T], F32)
    ysc = dscr("ysc", [8, 128, T], F32)
    qa_s = dscr("qa_s", [512, T], BF16)
    qf_s = dscr("qf_s", [8 * 68, T], BF16)
    kfo_s = dscr("kfo_s", [8 * 68, T], BF16)
    glu_s = dscr("glu_s", [512, T], BF16)
    sxc_s = dscr("sxc_s", [512, T], BF16)
    sb_s = dscr("sb_s", [512, T], BF16)
    zf_s = dscr("zf_s", [8, 128, T], F32)
    mix_s = dscr("mix_s", [8, 128, T], BF16)
    cKA = dscr("cKA", [512, T], BF16)
    oKA = dscr("oKA", [1024, T], BF16)
    cKFk = dscr("cKFk", [512, T], BF16)
    oKFk = dscr("oKFk", [1024, T], BF16)
    cKFa = dscr("cKFa", [32, T], BF16)
    oKFa = dscr("oKFa", [64, T], BF16)
    cVs_ = [dscr("cV%d" % i, [512, T], BF16) for i in range(3)]
    oVs_ = [dscr("oV%d" % i, [1024, T], BF16) for i in range(3)]
    cH_ = dscr("cH", [16, T], BF16)
    oH_ = dscr("oH", [32, T], BF16)
    cF_ = dscr("cF", [1, T], BF16)
    oF_ = dscr("oF", [2, T], BF16)
    cVg = [a.rearrange("r (a c) -> (r a) c", c=128).rearrange("(h t) c -> h t c", h=4) for a in cVs_]
    oVg = [a.rearrange("r (a c) -> (r a) c", c=128).rearrange("(r h t) c -> r h t c", r=2, h=4) for a in oVs_]
    cH = cH_.rearrange("r (a c) -> (r a) c", c=64)
    oH = oH_.rearrange("r (a c) -> (r a) c", c=64)
    cF = cF_.rearrange("r (a c) -> (r a) c", c=2)
    oF = oF_.rearrange("r (a c) -> (r a) c", c=2)
    RG = [[0, 1], [2, 3], [4, 5], [6, 7]]

    es = ExitStack()

    def sb(name, shape, dt):
        return es.enter_context(nc.sbuf_tensor("s_" + name, list(shape), dt))

    ARX = sb("ARX", [128, 32768], BF16)
    ARA = sb("ARA", [128, 16384], BF16)
    ARB = sb("ARB", [128, 14336], BF16)
    wsl = [sb("wsl%d" % i, [128, 4096], BF16) for i in range(3)]
    PTs = [sb("PT%d" % i, [128, 512], BF16) for i in range(4)]
    evs = [sb("ev%d" % i, [128, 512], F32) for i in range(4)]
    sq = sb("sq", [128, 8, 512], BF16)
    stg = [sb("stg%d" % i, [128, 512], BF16) for i in range(4)]
    pp = sb("pp", [128, NPP], F32)
    flags = sb("flags", [128, 2], F32)
    ones_bf = sb("ones_bf", [128, 128], BF16)
    meanD = sb("meanD", [128, 128], BF16)
    mean512 = sb("mean512", [128, 128], BF16)
    masks = sb("masks", [128, 384], BF16)
    small = sb("small", [128, 16], F32)
    uh = sb("uh", [128, 44, 2], F32)
    hfh = sb("hfh", [128, 8, 2], BF16)
    hfh2 = sb("hfh2", [128, 8, 2], BF16)
    vst = sb("vst", [128, 8, 128], BF16)
    ONE8t = sb("ONE8", [8, 2048], BF16)
    ONE8 = ONE8t[:, :]
    ARC = sb("ARC", [128, 8704], BF16)
    ps = [es.enter_context(nc.psum_tensor("ps%d" % i, [128, 512], F32)) for i in range(8)]

    mTb = ARB[:, 0:2048].rearrange("p (c n) -> p c n", c=8)
    xkT = ARB[:, 2048:3072].rearrange("p (c n) -> p c n", c=4)
    xvs = ARB[:, 3072:4096].rearrange("p (c n) -> p c n", c=2)
    A = ARA[:, :].rearrange("p (c n) -> p c n", c=8)
    BR = ARX[:, :].rearrange("p (c n) -> p c n", c=16)
    XF = ARX[:, :].bitcast(F32)
    pmask = flags[:, 0:1]
    hflag = flags[:, 1:2]

    ctr = {"ps": 0, "ev": 0, "stg": 0, "pt": 0, "w": 0}

    def rr(kind, n):
        i = ctr[kind] % n
        ctr[kind] += 1
        return i

    def ppc(l, name, j=0, n=1):
        o, w = PP[name]
        return pp[:, l * PPL + o + j: l * PPL + o + j + n]

    def pe(fn, reads, writes):
        return P.add("pe", fn, reads, writes)

    def act(fn, reads, writes):
        return P.add("act", fn, reads, writes)

    def dve(fn, reads, writes):
        return P.add("dve", fn, reads, writes)

    def dma(q, o, i, reads, writes):
        return P.add(q, lambda e, o=o, i=i: e.dma_start(out=o, in_=i), reads, writes, dma=True)

    def mm_group(bank, pairs, reads, n0=0, n1=512, rows=128):
        def fn(e, bank=bank, pairs=pairs):
            last = None
            for i, (lt, rh) in enumerate(pairs):
                last = e.matmul(ps[bank][0:rows, n0:n1], lt, rh, start=(i == 0), stop=(i == len(pairs) - 1))
            return last
        return pe(fn, reads, [("ps", bank)])

    dma("sp", pp[:, :], pp_in, [], ["pp"])
    dma("sp", flags[:, :], flags_in, [], ["flags"])
    dma("pool", masks[:, :], masks_in, [], ["masks"])
    dve(lambda e: e.memset(ones_bf[:, :], 1.0), [], ["consts"])
    dve(lambda e: e.memset(meanD[:, :], 1.0 / 1024), [], ["consts"])
    dve(lambda e: e.memset(mean512[:, :], 1.0 / 512), [], ["consts"])
    dve(lambda e: e.memset(vst[:, :, 64:128], 1.0), [], ["vst"])
    dve(lambda e: e.memset(ONE8, 1.0), [], ["ONE8"])
    dve(lambda e: e.memset(small[:, :], 0.0), [], ["small"])
    dve(lambda e: e.memset(small[:, 2:3], EPS), ["small"], ["small"])

    XKEYS = (["XT0", "YT0", ("XT0", 0), ("XT0", 1), ("YT0", 0), ("YT0", 1), "FZ", "FS", "FC", "FD", "FO", "HB", "MX", "ACT_T"]
             + [("XQ", h, t) for h in range(4) for t in range(NT)] + [("OX", h, t) for h in range(4) for t in range(NT)]
             + [("BR", c, t) for c in range(16) for t in range(NT)] + [("ACT_T", g, tt) for g in range(NG) for tt in range(2)])
    BKEYS = ([("KT", 0), ("KT", 1), ("KT1", 0), ("KT1", 1), ("VV", 0), ("VV", 1)] + [("QT", j) for j in range(NT)] + ["mTb", "xkT", "xvs"])

    def fenceX():
        P.add("dve", lambda e: e.memset(small[:, 8:9], 0.0), [], XKEYS)

    def fenceB():
        P.add("dve", lambda e: e.memset(small[:, 8:9], 0.0), [], BKEYS)

    def rstd_from_sq(nchunks, meanmat, sq_reads, n=512):
        b = 4 + rr("ps", 4)
        mm_group(b, [(meanmat[:, :], sq[:, c, 0:n]) for c in range(nchunks)], sq_reads + ["consts"], 0, n)
        i = rr("ev", 4)
        r = evs[i]
        act(lambda e, r=r, b=b: e.activation(out=r[:, 0:n], in_=ps[b][:, 0:n], func=AF.Ln, bias=small[:, 2:3], scale=1.0),
            [("ps", b), "small"], [("ev", i)])
        act(lambda e, r=r: e.activation(out=r[:, 0:n], in_=r[:, 0:n], func=AF.Exp, scale=-0.5),
            [("ev", i)], [("ev", i)])
        return r, ("ev", i)

    def prenorm_tile(xt, xkey, l, gname, t):
        for c in range(8):
            act(lambda e, c=c: e.activation(out=sq[:, c, :], in_=xt[:, c, :], func=AF.Square), [xkey], [("sq", c)])
        r, rk = rstd_from_sq(8, meanD, [("sq", c) for c in range(8)])
        for c in range(8):
            dve(lambda e, c=c, r=r: e.scalar_tensor_tensor(out=A[:, c, t * TW:(t + 1) * TW], in0=xt[:, c, :],
                                                        scalar=ppc(l, gname, c), in1=r[:, :],
                                                        op0=ALU.mult, op1=ALU.mult),
                [xkey, rk, "pp"], [("A", c, t)])

    XT0 = XF[:, 0:4096].rearrange("p (c n) -> p c n", c=8)
    YT0 = XF[:, 4096:8192].rearrange("p (c n) -> p c n", c=8)

    def post_pass(l, gpost, gnext, lnext, final=False):
        SW = 256
        NU = T // SW
        outv = out.rearrange("(c p) n -> p c n", p=128)

        def bufs(u):
            hh = u % 2
            return (YT0[:, :, hh * SW:(hh + 1) * SW], ("YT0", hh), XT0[:, :, hh * SW:(hh + 1) * SW], ("XT0", hh))

        def stage_a(u):
            YT, yk, XT, xk = bufs(u)
            c0 = u * SW
            dma("sp", YT, ysc[:, :, c0:c0 + SW].rearrange("c p n -> p c n"), [("ysc", u // 2)], [yk])
            dma("sp", XT, xs[:, :, c0:c0 + SW].rearrange("c p n -> p c n"), [("xs", u // 2)], [xk])
            for c in range(8):
                act(lambda e, c=c, YT=YT: e.activation(out=sq[:, c, 0:SW], in_=YT[:, c, :], func=AF.Square), [yk], [("sq", c)])
            return rstd_from_sq(8, meanD, [("sq", c) for c in range(8)], n=SW)

        def stage_b(u, r, rk):
            YT, yk, XT, xk = bufs(u)
            c0 = u * SW
            for c in range(8):
                dve(lambda e, c=c, r=r, YT=YT: e.scalar_tensor_tensor(out=YT[:, c, :], in0=YT[:, c, :], scalar=ppc(l, gpost, c),
                                                                   in1=r[:, 0:SW], op0=ALU.mult, op1=ALU.mult),
                    [yk, rk, "pp"], [yk])
                dve(lambda e, c=c, YT=YT, XT=XT: e.tensor_tensor(out=XT[:, c, :], in0=XT[:, c, :], in1=YT[:, c, :], op=ALU.add),
                    [yk, xk], [xk])
            if final:
                dma("sp", outv[:, :, c0:c0 + SW], XT, [xk], ["out"])
                return None
            dma("sp", xs[:, :, c0:c0 + SW].rearrange("c p n -> p c n"), XT, [xk], [("xs", u // 2)])
            for c in range(8):
                act(lambda e, c=c, XT=XT: e.activation(out=sq[:, c, 0:SW], in_=XT[:, c, :], func=AF.Square), [xk], [("sq", c)])
            return rstd_from_sq(8, meanD, [("sq", c) for c in range(8)], n=SW)

        def stage_c(u, r, rk):
            YT, yk, XT, xk = bufs(u)
            c0 = u * SW
            for c in range(8):
                dve(lambda e, c=c, r=r, XT=XT: e.scalar_tensor_tensor(out=A[:, c, c0:c0 + SW], in0=XT[:, c, :], scalar=ppc(lnext, gnext, c),
                                                                   in1=r[:, 0:SW], op0=ALU.mult, op1=ALU.mult),
                    [xk, rk, "pp"], [("A", c, u // 2)])

        ra = {0: stage_a(0)}
        for u in range(NU):
            if u + 1 < NU:
                ra[u + 1] = stage_a(u + 1)
            r2 = stage_b(u, *ra[u])
            if r2 is not None:
                stage_c(u, *r2)

    def load_w(src, ncol, key_extra=()):
        i = rr("w", 3)
        w = wsl[i]
        dma("pool", w[:, 0:ncol], src, [], [("w", i)])
        return w, ("w", i)

    def lin_fm(inp_keyfn, inp, kc, wsrcs, handler, tiles=range(NT)):
        for ci, src in enumerate(wsrcs):
            w, wk = load_w(src, kc * 128)
            for t in tiles:
                b = rr("ps", 8)
                mm_group(b, [(w[:, k * 128:(k + 1) * 128], inp[:, k, t * TW:(t + 1) * TW]) for k in range(kc)],
                         [wk] + [inp_keyfn(k, t) for k in range(kc)])
                handler(ci, t, b)

    def Akey(k, t):
        return ("A", k, t)

    for t in range(NT):
        dma("sp", XT0, xT_in.rearrange("(c p) n -> p c n", p=128)[:, :, t * TW:(t + 1) * TW], [], ["XT0"])
        dma("sp", xs[:, :, t * TW:(t + 1) * TW].rearrange("c p n -> p c n"), XT0, ["XT0"], [("xs", t)])
        prenorm_tile(XT0, "XT0", 0, "nmp", t)

    def do_layer(l):
      if True:
          lam_init = 0.8 - 0.6 * math.exp(-0.3 * l)
          tmp64 = evs[0]
          for j, (a_, b_) in enumerate((("lq1", "lk1"), ("lq2", "lk2"))):
              dve(lambda e, a_=a_, b_=b_: e.tensor_tensor(out=tmp64[:, 0:64], in0=ppc(l, a_, 0, 64), in1=ppc(l, b_, 0, 64), op=ALU.mult),
                  ["pp"], [("ev", 0)])
              dve(lambda e, j=j: e.reduce_sum(out=small[:, 4 + j:5 + j], in_=tmp64[:, 0:64], axis=AX.X), [("ev", 0)], ["small"])
          act(lambda e: e.activation(out=small[:, 4:6], in_=small[:, 4:6], func=AF.Exp), ["small"], ["small"])
          dve(lambda e: e.scalar_tensor_tensor(out=small[:, 0:1], in0=small[:, 5:6], scalar=-lam_init, in1=small[:, 4:5],
                                               op0=ALU.add, op1=ALU.subtract), ["small"], ["small"])
          dve(lambda e: e.tensor_scalar(out=small[:, 1:2], in0=ppc(l, "dnorm"), scalar1=1.0 - lam_init, scalar2=None, op0=ALU.mult),
              ["pp"], ["small"])
          dve(lambda e: e.tensor_scalar(out=small[:, 3:4], in0=ppc(l, "bfgt"), scalar1=-1.0, scalar2=None, op0=ALU.mult),
              ["pp"], ["small"])

          fenceX()
          fenceB()
          def evac_scaled_to(dst_rows_fn, scale):
              def h(ci, t, b):
                  i = rr("stg", 4)
                  s = stg[i]
                  act(lambda e, s=s, b=b: e.activation(out=s[:, :], in_=ps[b][:, :], func=AF.Copy, scale=scale),
                      [("ps", b)], [("stg", i)])
                  for (dst, key, r0, r1) in dst_rows_fn(ci):
                      dma("sp", dst[:, t * TW:(t + 1) * TW], s[r0:r1, :], [("stg", i)], [key])
              return h

          lin_fm(Akey, A, 8, [win[l, c] for c in range(4, 8)],
                 evac_scaled_to(lambda ci: [(cKA[ci * 128:(ci + 1) * 128, :], "cKA", 0, 128)], 1.0))
          lin_fm(Akey, A, 8, [win[l, c] for c in range(12, 16)],
                 evac_scaled_to(lambda ci: [(kfo_s[(2 * ci) * 68:(2 * ci) * 68 + 64, :], "kfo", 0, 64),
                                            (kfo_s[(2 * ci + 1) * 68:(2 * ci + 1) * 68 + 64, :], "kfo", 64, 128),
                                            (cKFk[ci * 128:(ci + 1) * 128, :], "cKF", 0, 128)], 1.0))

          FZ = XF[0:8, 0:2048]
          FS = XF[0:8, 2048:4096]
          FC = XF[0:8, 4096:6144]
          FD = XF[0:8, 6144:8192]
          FO = XF[0:8, 8192:10240]
          HB = ARX[0:8, 20480:32768].rearrange("p (a n) -> p a n", a=6)
          wfs, wfk = load_w(wfg[l], 64)
          for t in range(NT):
              b = rr("ps", 8)
              mm_group(b, [(wfs[:, k * 8:(k + 1) * 8], A[:, k, t * TW:(t + 1) * TW]) for k in range(8)],
                       [wfk] + [("A", k, t) for k in range(8)], rows=8)
              act(lambda e, b=b, t=t: e.activation(out=FZ[:, t * TW:(t + 1) * TW], in_=ps[b][0:8, :], func=AF.Exp,
                                                   bias=small[0:8, 3:4], scale=-1.0), [("ps", b), "small"], ["FZ"])
          act(lambda e: e.activation(out=FS, in_=FZ, func=AF.Ln, bias=1.0, scale=1.0), ["FZ"], ["FS"])
          dve(lambda e: e.memset(FO, 1.0), [], ["FO"])
          dve(lambda e: e.tensor_tensor_scan(out=FC, data0=FO, data1=FS, initial=0.0, op0=ALU.mult, op1=ALU.add),
              ["FO", "FS"], ["FC"])

          def hilo(src_fn, ihi, key):
              dve(lambda e: src_fn(e, FD), [key, "FC"], ["FD"])
              dve(lambda e: e.tensor_copy(out=HB[:, ihi, :], in_=FD), ["FD"], ["HB"])
              dve(lambda e: e.tensor_tensor(out=HB[:, ihi + 1, :], in0=FD, in1=HB[:, ihi, :], op=ALU.subtract), ["FD", "HB"], ["HB"])
          hilo(lambda e, o: e.tensor_scalar(out=o, in0=FC, scalar1=-1.0, scalar2=None, op0=ALU.mult), 0, "FC")
          hilo(lambda e, o: e.tensor_copy(out=o, in_=FC), 2, "FC")
          hilo(lambda e, o: e.tensor_scalar(out=o, in0=FC, scalar1=FC[:, 2047:2048], scalar2=None, op0=ALU.subtract), 4, "FC")
          qf3 = qf_s.rearrange("(h r) n -> h r n", r=68)
          kfo3 = kfo_s.rearrange("(h r) n -> h r n", r=68)
          cKFa3 = cKFa.rearrange("(h r) n -> h r n", r=4)
          dma("sp", qf3[:, 64, :], HB[:, 0, :], ["HB"], ["qf"])
          dma("sp", qf3[:, 65, :], HB[:, 1, :], ["HB"], ["qf"])
          dma("sp", qf3[:, 66, :], ONE8, ["ONE8"], ["qf"])
          dma("sp", qf3[:, 67, :], ONE8, ["ONE8"], ["qf"])
          for dst, key, ih, r0 in ((kfo3, "kfo", 2, 64), (cKFa3, "cKFa", 4, 0)):
              dma("sp", dst[:, r0, :], ONE8, ["ONE8"], [key])
              dma("sp", dst[:, r0 + 1, :], ONE8, ["ONE8"], [key])
              dma("sp", dst[:, r0 + 2, :], HB[:, ih, :], ["HB"], [key])
              dma("sp", dst[:, r0 + 3, :], HB[:, ih + 1, :], ["HB"], [key])

          dve(lambda e: e.memset(vst[:, :, 64:128], 1.0), [], ["vst"])
          for vi in range(2):
              w, wk = load_w(wv[l, vi], 4096)
              for blk in range(16):
                  b = rr("ps", 8)
                  t_, o_ = blk // 4, (blk % 4) * 128
                  mm_group(b, [(A[:, k, blk * 128:(blk + 1) * 128], w[:, k * 512:(k + 1) * 512]) for k in range(8)],
                           [wk] + [("A", k, t_) for k in range(8)])
                  if vi == 0:
                      si = rr("stg", 4)
                      s = stg[si]
                      act(lambda e, s=s, b=b: e.activation(out=s[:, :], in_=ps[b][:, :], func=AF.Copy), [("ps", b)], [("stg", si)])
                      dma("sp", cVg[0][:, blk * 128:(blk + 1) * 128, :].rearrange("h t c -> t h c"),
                          s[:, :].rearrange("p (h c) -> p h c", h=4), [("stg", si)], ["cV"])
                  else:
                      act(lambda e, b=b: e.activation(out=vst[:, :, 0:64], in_=ps[b][:, :].rearrange("p (h c) -> p h c", h=8), func=AF.Copy),
                          [("ps", b)], ["vst"])
                      dma("sp", cVg[1][:, blk * 128:(blk + 1) * 128, :].rearrange("h t c -> t h c"), vst[:, 0:4, :], ["vst"], ["cV"])
                      dma("sp", cVg[2][:, blk * 128:(blk + 1) * 128, :].rearrange("h t c -> t h c"), vst[:, 4:8, :], ["vst"], ["cV"])

          for j in range(4):
              wa, wak = load_w(win[l, 16 + j], 1024)
              wgl, wgk = load_w(win[l, 20 + j], 1024)
              for t in range(NT):
                  ba = rr("ps", 8)
                  mm_group(ba, [(wa[:, k * 128:(k + 1) * 128], A[:, k, t * TW:(t + 1) * TW]) for k in range(8)],
                           [wak] + [("A", k, t) for k in range(8)])
                  bg = rr("ps", 8)
                  mm_group(bg, [(wgl[:, k * 128:(k + 1) * 128], A[:, k, t * TW:(t + 1) * TW]) for k in range(8)],
                           [wgk] + [("A", k, t) for k in range(8)])
                  i = rr("ev", 4)
                  ev = evs[i]
                  act(lambda e, ev=ev, bg=bg, j=j: e.activation(out=ev[:, :], in_=ps[bg][:, :], func=AF.Sigmoid,
                                                               bias=ppc(l, "bglu", 4 + j), scale=1.0),
                      [("ps", bg), "pp"], [("ev", i)])
                  si = rr("stg", 4)
                  s = stg[si]
                  dve(lambda e, s=s, ev=ev, ba=ba, j=j: e.scalar_tensor_tensor(out=s[:, :], in0=ps[ba][:, :], scalar=ppc(l, "bglu", j),
                                                                           in1=ev[:, :], op0=ALU.add, op1=ALU.mult),
                      [("ps", ba), ("ev", i), "pp"], [("stg", si)])
                  dma("sp", glu_s[j * 128:(j + 1) * 128, t * TW:(t + 1) * TW], s[:, :], [("stg", si)], ["glu"])
                  if t == NT - 1:
                      dma("sp", cH[j * 128:(j + 1) * 128, 0:30], s[:, 482:512], [("stg", si)], ["cH"])
          for j in range(4):
              wa, wak = load_w(win[l, 24 + j], 1024)
              wgl, wgk = load_w(win[l, 32 + j], 1024)
              for t in range(NT):
                  ba = rr("ps", 8)
                  mm_group(ba, [(wa[:, k * 128:(k + 1) * 128], A[:, k, t * TW:(t + 1) * TW]) for k in range(8)],
                           [wak] + [("A", k, t) for k in range(8)])
                  bg = rr("ps", 8)
                  mm_group(bg, [(wgl[:, k * 128:(k + 1) * 128], A[:, k, t * TW:(t + 1) * TW]) for k in range(8)],
                           [wgk] + [("A", k, t) for k in range(8)])
                  i = rr("ev", 4)
                  ev = evs[i]
                  act(lambda e, ev=ev, bg=bg: e.activation(out=ev[:, :], in_=ps[bg][:, :], func=AF.Copy),
                      [("ps", bg)], [("ev", i)])
                  si = rr("stg", 4)
                  s = stg[si]
                  dve(lambda e, s=s, ev=ev, ba=ba: e.tensor_tensor(out=s[:, :], in0=ps[ba][:, :], in1=ev[:, :], op=ALU.mult),
                      [("ps", ba), ("ev", i)], [("stg", si)])
                  dma("sp", sxc_s[j * 128:(j + 1) * 128, t * TW:(t + 1) * TW], s[:, :], [("stg", si)], ["sxc"])
                  if t == NT - 1:
                      dma("sp", cH[j * 128:(j + 1) * 128, 32:34], s[:, 510:512], [("stg", si)], ["cH"])
          chk("S1")
          for (ci_, co_, ki, ko) in ((cKA, oKA, "cKA", "oKA"), (cKFk, oKFk, "cKF", "oKF"), (cKFa, oKFa, "cKFa", "oKFa"), (cVs_[0], oVs_[0], "cV", "oV0"),
                                       (cVs_[1], oVs_[1], "cV", "oV1"), (cVs_[2], oVs_[2], "cV", "oV2"), (cH_, oH_, "cH", "oH")):
              P.add("pool", lambda e, ci_=ci_, co_=co_: e.collective_compute("AllGather", ALU.bypass, replica_groups=RG,
                                                                            ins=[ci_], outs=[co_]),
                    [ki], [ko], cc=True)

          fenceX()
          fenceB()
          lin_fm(Akey, A, 8, [win[l, c] for c in range(0, 4)],
                 evac_scaled_to(lambda ci: [(qa_s[ci * 128:(ci + 1) * 128, :], "qa", 0, 128)], 0.125))
          lin_fm(Akey, A, 8, [win[l, c] for c in range(8, 12)],
                 evac_scaled_to(lambda ci: [(qf_s[(2 * ci) * 68:(2 * ci) * 68 + 64, :], "qf", 0, 64),
                                            (qf_s[(2 * ci + 1) * 68:(2 * ci + 1) * 68 + 64, :], "qf", 64, 128)], 0.125))
          lin_fm(Akey, A, 8, [win[l, c] for c in range(28, 32)],
                 evac_scaled_to(lambda ci: [(sb_s[ci * 128:(ci + 1) * 128, :], "sb", 0, 128)], 1.0))

          chk("CC")
          def conv_gen():
              GB = ARC[:, 0:2176].rearrange("p (c n) -> p c n", c=4)
              ACC = ARC[:, 2176:6272].bitcast(F32).rearrange("p (c n) -> p c n", c=4)
              HL = ARC[:, 6272:6528].rearrange("p (c n) -> p c n", c=4)
              SB2 = ARC[:, 6528:8576].rearrange("p (c n) -> p c n", c=4)
              dma("sp", HL, oH[0:512, :].rearrange("(c p) n -> p c n", p=128), ["oH"], ["HL"])
              dve(lambda e: e.tensor_scalar(out=HL, in0=HL, scalar1=hflag, scalar2=None, op0=ALU.mult), ["HL", "flags"], ["HL"])
              for t in range(NT):
                  if t == 0:
                      dve(lambda e: e.tensor_copy(out=GB[:, :, 0:30], in_=HL[:, :, 0:30]), ["HL"], ["GB"])
                      dma("sp", GB[:, :, 30:542], glu_s[:, 0:TW].rearrange("(c p) n -> p c n", p=128), ["glu"], ["GB"])
                  else:
                      dma("sp", GB[:, :, 0:542], glu_s[:, t * TW - 30:(t + 1) * TW].rearrange("(c p) n -> p c n", p=128), ["glu"], ["GB"])
                  for c in range(4):
                      dve(lambda e, c=c: e.tensor_scalar(out=ACC[:, c, :], in0=GB[:, c, 0:512], scalar1=ppc(l, "cdw", c * 31),
                                                         scalar2=ppc(l, "cdwb", c), op0=ALU.mult, op1=ALU.add),
                          ["GB", "pp"], [("ACC", c)])
                      for k in range(1, 31):
                          dve(lambda e, c=c, k=k: e.scalar_tensor_tensor(out=ACC[:, c, :], in0=GB[:, c, k:k + 512],
                                                                         scalar=ppc(l, "cdw", c * 31 + k), in1=ACC[:, c, :],
                                                                         op0=ALU.mult, op1=ALU.add),
                              ["GB", "pp", ("ACC", c)], [("ACC", c)])
                          if k % 3 == 0:
                              yield
                      yield ("ln" if c == 3 else None)
                  for c in range(4):
                      act(lambda e, c=c: e.activation(out=sq[:, c, :], in_=ACC[:, c, :], func=AF.Copy), [("ACC", c)], [("sq", c)])
                      act(lambda e, c=c: e.activation(out=sq[:, 4 + c, :], in_=ACC[:, c, :], func=AF.Square), [("ACC", c)], [("sq", 4 + c)])
                  bm = 4 + rr("ps", 4)
                  mm_group(bm, [(mean512[:, :], sq[:, c, :]) for c in range(4)], [("sq", c) for c in range(4)] + ["consts"])
                  bq = 4 + rr("ps", 4)
                  mm_group(bq, [(mean512[:, :], sq[:, 4 + c, :]) for c in range(4)], [("sq", 4 + c) for c in range(4)] + ["consts"])
                  i0 = rr("ev", 4)
                  mu = evs[i0]
                  act(lambda e, mu=mu, bm=bm: e.activation(out=mu[:, :], in_=ps[bm][:, :], func=AF.Copy), [("ps", bm)], [("ev", i0)])
                  i1 = rr("ev", 4)
                  rs = evs[i1]
                  dve(lambda e, rs=rs, mu=mu: e.tensor_tensor(out=rs[:, :], in0=mu[:, :], in1=mu[:, :], op=ALU.mult), [("ev", i0)], [("ev", i1)])
                  dve(lambda e, rs=rs, bq=bq: e.tensor_tensor(out=rs[:, :], in0=ps[bq][:, :], in1=rs[:, :], op=ALU.subtract),
                      [("ps", bq), ("ev", i1)], [("ev", i1)])
                  act(lambda e, rs=rs: e.activation(out=rs[:, :], in_=rs[:, :], func=AF.Ln, bias=small[:, 2:3], scale=1.0), [("ev", i1), "small"], [("ev", i1)])
                  act(lambda e, rs=rs: e.activation(out=rs[:, :], in_=rs[:, :], func=AF.Exp, scale=-0.5), [("ev", i1)], [("ev", i1)])
                  for c in range(4):
                      dve(lambda e, c=c, mu=mu: e.tensor_tensor(out=ACC[:, c, :], in0=ACC[:, c, :], in1=mu[:, :], op=ALU.subtract),
                          [("ACC", c), ("ev", i0)], [("ACC", c)])
                      dve(lambda e, c=c, rs=rs: e.tensor_tensor(out=ACC[:, c, :], in0=ACC[:, c, :], in1=rs[:, :], op=ALU.mult),
                          [("ACC", c), ("ev", i1)], [("ACC", c)])
                      act(lambda e, c=c, t=t: e.activation(out=BR[:, 8 + c, t * TW:(t + 1) * TW], in_=ACC[:, c, :], func=AF.Silu,
                                                           bias=ppc(l, "clnb", c), scale=ppc(l, "clng", c)),
                          [("ACC", c), "pp"], [("BR", 8 + c, t)])
                  yield
                  if t == 0:
                      dve(lambda e: e.tensor_copy(out=GB[:, :, 0:2], in_=HL[:, :, 32:34]), ["HL", ("BR", 11, t)], ["GB"])
                      dma("sp", GB[:, :, 2:514], sxc_s[:, 0:TW].rearrange("(c p) n -> p c n", p=128), ["sxc"], ["GB"])
                  else:
                      dma("sp", GB[:, :, 0:514], sxc_s[:, t * TW - 2:(t + 1) * TW].rearrange("(c p) n -> p c n", p=128),
                          ["sxc", ("BR", 11, t)], ["GB"])
                  dma("sp", SB2, sb_s[:, t * TW:(t + 1) * TW].rearrange("(c p) n -> p c n", p=128), ["sb"], ["SB2"])
                  for c in range(4):
                      dve(lambda e, c=c: e.tensor_scalar(out=ACC[:, c, :], in0=GB[:, c, 0:512], scalar1=ppc(l, "scw", c * 3),
                                                         scalar2=None, op0=ALU.mult), ["GB", "pp"], [("ACC", c)])
                      for k in range(1, 3):
                          dve(lambda e, c=c, k=k: e.scalar_tensor_tensor(out=ACC[:, c, :], in0=GB[:, c, k:k + 512],
                                                                         scalar=ppc(l, "scw", c * 3 + k), in1=ACC[:, c, :],
                                                                         op0=ALU.mult, op1=ALU.add),
                              ["GB", "pp", ("ACC", c)], [("ACC", c)])
                      dve(lambda e, c=c, t=t: e.tensor_tensor(out=BR[:, 12 + c, t * TW:(t + 1) * TW], in0=ACC[:, c, :], in1=SB2[:, c, :], op=ALU.mult),
                          [("ACC", c), "SB2"], [("BR", 12 + c, t)])
              return

          chk("S2a")
          KT = ARB[:, 0:4096]
          QT = ARB[:, 4096:6144]
          VV = ARB[:, 6144:10240].rearrange("p (b c) -> p b c", b=32)
          KT1 = ARB[:, 10240:14336]
          oKA2 = oKA
          cg = conv_gen()
          dve(lambda e: e.memset(KT[64:128, :], 0.0), [], [("KT", 0), ("KT", 1)])
          dve(lambda e: e.memset(KT1[0:64, :], 0.0), [], [("KT1", 0), ("KT1", 1)])

          fbank = [0]
          prefetch = [None]
          cg_state = [None]

          def pull(allow_psum):
              if cg_state[0] == "ln" and not allow_psum:
                  return
              cg_state[0] = next(cg, None)

          def attention(rows_list, kind, h, finalize):
              for j in range(NT):
                  steps = []
                  for kb in range(16):
                      steps.append((kb, 0, True))
                  for kb in range(4 * j + 4):
                      q0 = max(0, kb - 4 * j) * 128
                      steps.append((16 + kb, q0, False))
                  seq = [(s_, ci) for s_ in steps for ci in range(len(rows_list))]
                  sbank = {}

                  def qk(idx):
                      (kb, q0, prev), ci = seq[idx]
                      kbuf, kkey, r0, r1 = rows_list[ci]
                      b = 4 + rr("ps", 4)
                      sbank[idx] = b
                      if MASK_PE and (not prev) and kb - 16 >= 4 * j:
                          mo = 0 if kind == "A" else 128

                          def fn(e, b=b, kbuf=kbuf, r0=r0, r1=r1, kb=kb, q0=q0, mo=mo, j=j):
                              e.matmul(ps[b][:, q0:512], kbuf[r0:r1, kb * 128:(kb + 1) * 128], QT[r0:r1, j * TW + q0:(j + 1) * TW],
                                       start=True, stop=True)
                              return e.matmul(ps[b][:, q0:q0 + 128], masks[:, 256:384], masks[:, mo:mo + 128], start=False, stop=True)
                          pe(fn, [(kkey, kb // 16), ("QT", j), "masks"], [("ps", b)])
                      else:
                          mm_group(b, [(kbuf[r0:r1, kb * 128:(kb + 1) * 128], QT[r0:r1, j * TW + q0:(j + 1) * TW])],
                                   [(kkey, kb // 16), ("QT", j)], q0, 512)
                  for idx in range(min(3, len(seq))):
                      qk(idx)
                  for idx in range(len(seq)):
                      (kb, q0, prev), ci = seq[idx]
                      b = sbank[idx]
                      pi = rr("pt", 4)
                      pt = PTs[pi]
                      if prev:
                          act(lambda e, pt=pt, b=b: e.activation(out=pt[:, :], in_=ps[b][:, :], func=AF.Exp, bias=pmask, scale=1.0),
                              [("ps", b), "flags"], [("pt", pi)])
                      else:
                          act(lambda e, pt=pt, b=b, q0=q0: e.activation(out=pt[:, q0:512], in_=ps[b][:, q0:512], func=AF.Exp),
                              [("ps", b)], [("pt", pi)])
                          if (not MASK_PE) and kb - 16 >= 4 * j:
                              mo = 0 if kind == "A" else 128
                              dve(lambda e, pt=pt, q0=q0, mo=mo: e.tensor_tensor(out=pt[:, q0:q0 + 128], in0=pt[:, q0:q0 + 128],
                                                                                 in1=masks[:, mo:mo + 128], op=ALU.mult),
                                  [("pt", pi), "masks"], [("pt", pi)])
                      if idx + 3 < len(seq):
                          qk(idx + 3)
                      if j == NT - 1 and idx == 16 * len(rows_list) + 1 and prefetch[0] is not None:
                          prefetch[0]()
                          prefetch[0] = None
                      if idx == len(seq) // 2:
                          flush_pending()
                      if idx % 8 == 7:
                          pull(False)
                      first = (idx < len(rows_list))
                      last = (idx >= len(seq) - len(rows_list))
                      ob = ci if kind == "A" else (fbank[0] if F_PINGPONG else 0)
                      pe(lambda e, ob=ob, kb=kb, pt=pt, q0=q0, first=first, last=last:
                         e.matmul(ps[ob][:, q0:512], VV[:, kb, :], pt[:, q0:512], start=first, stop=last),
                         [("VV", kb // 16), ("pt", pi)], [("ps", ob)])
                      if kind == "A":
                          pe(lambda e, ob=ob, pt=pt, q0=q0, first=first, last=last:
                             e.matmul(ps[2 + ob][:, q0:512], ones_bf[:, :], pt[:, q0:512], start=first, stop=last),
                             ["consts", ("pt", pi)], [("ps", 2 + ob)])
                  finalize(h, j)
                  fbank[0] ^= 1
                  pull(True)

          SQF = sq[:, :, :].rearrange("p c n -> p (c n)").bitcast(F32).rearrange("p (c n) -> p c n", c=4)
          SQK = [("sq", c) for c in range(8)]

          YAP = vst[:, :, :].rearrange("p c n -> p (c n)").bitcast(F32)
          pending = []

          def flush_pending():
              while pending:
                  pending.pop(0)()

          def fin_A(h, j):
              for bi in (0, 2, 1, 3):
                  dve(lambda e, bi=bi: e.tensor_copy(out=SQF[:, bi, :], in_=ps[bi][:, :]), [("ps", bi)], SQK)
              dve(lambda e: e.reciprocal(out=SQF[:, 2, :], in_=SQF[:, 2, :]), SQK, SQK)
              dve(lambda e: e.tensor_tensor(out=SQF[:, 0, :], in0=SQF[:, 0, :], in1=SQF[:, 2, :], op=ALU.mult), SQK, SQK)
              dve(lambda e: e.reciprocal(out=SQF[:, 3, :], in_=SQF[:, 3, :]), SQK, SQK)
              dve(lambda e: e.tensor_tensor(out=SQF[:, 1, :], in0=SQF[:, 1, :], in1=SQF[:, 3, :], op=ALU.mult), SQK, SQK)
              dve(lambda e: e.scalar_tensor_tensor(out=YAP, in0=SQF[:, 1, :], scalar=small[:, 0:1], in1=SQF[:, 0, :],
                                                   op0=ALU.mult, op1=ALU.add), SQK + ["small"], ["vst"])

              def part2():
                  si = rr("stg", 4)
                  sg = stg[si]
                  act(lambda e: e.activation(out=sg[:, :], in_=YAP, func=AF.Square), ["vst"], [("stg", si)])
                  b = 4 + rr("ps", 4)
                  mm_group(b, [(ones_bf[:, :], sg[:, :])], [("stg", si), "consts"])
                  i = rr("ev", 4)
                  r = evs[i]
                  act(lambda e: e.activation(out=r[:, :], in_=ps[b][:, :], func=AF.Ln, bias=small[:, 2:3], scale=1.0 / 128),
                      [("ps", b), "small"], [("ev", i)])
                  act(lambda e: e.activation(out=r[:, :], in_=r[:, :], func=AF.Exp, scale=-0.5), [("ev", i)], [("ev", i)])
                  dve(lambda e: e.scalar_tensor_tensor(out=BR[:, h, j * TW:(j + 1) * TW], in0=YAP, scalar=small[:, 1:2], in1=r[:, :],
                                                       op0=ALU.mult, op1=ALU.mult), ["vst", ("ev", i), "small"], [("BR", h, j)])
              pending.append(part2)

          def fin_F(h, j):
              i = rr("ev", 4)
              ev = evs[i]
              i2 = rr("ev", 4)
              dn = evs[i2]
              p0 = (h % 2) * 64
              zb = fbank[0] if F_PINGPONG else 0
              dve(lambda e: e.tensor_copy(out=ev[:, :], in_=ps[zb][:, :]), [("ps", zb)], [("ev", i)])
              dma("sp", dn[0:64, :], ev[64:128, :], [("ev", i)], [("ev", i2)])
              dve(lambda e: e.reciprocal(out=dn[0:64, :], in_=dn[0:64, :]), [("ev", i2)], [("ev", i2)])
              dve(lambda e: e.tensor_tensor(out=BR[p0:p0 + 64, 4 + h // 2, j * TW:(j + 1) * TW], in0=ev[0:64, :], in1=dn[0:64, :], op=ALU.mult),
                  [("ev", i), ("ev", i2)], [("BR", 4 + h // 2, j)])

          def load_q(src, rows, tiles):
              for j in tiles:
                  dma("sp", QT[0:rows, j * TW:(j + 1) * TW], src[:, j * TW:(j + 1) * TW], ["qa", "qf"], [("QT", j)])

          def a_early(h):
              dma("sp", KT[0:64, 0:2048], oKA2[h * 128:h * 128 + 64, :], ["oKA"], [("KT", 0)])
              dma("sp", KT1[64:128, 0:2048], oKA2[h * 128 + 64:(h + 1) * 128, :], ["oKA"], [("KT1", 0)])
              load_q(qa_s[h * 128:(h + 1) * 128, :], 128, range(0, 3))
              dma("sp", VV[:, 0:16, :], oVg[0][0, h].rearrange("(b p) c -> p b c", p=128), ["oV0"], [("VV", 0)])

          def a_late(h):
              load_q(qa_s[h * 128:(h + 1) * 128, :], 128, range(3, 4))
              dma("sp", KT[0:64, 2048:4096], cKA[h * 128:h * 128 + 64, :], ["cKA"], [("KT", 1)])
              dma("sp", KT1[64:128, 2048:4096], cKA[h * 128 + 64:(h + 1) * 128, :], ["cKA"], [("KT1", 1)])
              dma("sp", VV[:, 16:32, :], cVg[0][h].rearrange("(b p) c -> p b c", p=128), ["cV"], [("VV", 1)])

          def f_early(h):
              dma("sp", KT[0:64, 0:2048], oKFk[h * 64:(h + 1) * 64, :], ["oKF"], [("KT", 0)])
              dma("sp", KT[64:68, 0:2048], oKFa[h * 4:(h + 1) * 4, :], ["oKFa"], [("KT", 0)])
              load_q(qf3[h], 68, range(0, 3))
              dma("sp", VV[:, 0:16, :], oVg[1 + h // 4][0, h % 4].rearrange("(b p) c -> p b c", p=128), ["oV%d" % (1 + h // 4)], [("VV", 0)])

          def f_late(h):
              load_q(qf3[h], 68, range(3, 4))
              dma("sp", KT[0:68, 2048:4096], kfo3[h], ["kfo"], [("KT", 1)])
              dma("sp", VV[:, 16:32, :], cVg[1 + h // 4][h % 4].rearrange("(b p) c -> p b c", p=128), ["cV"], [("VV", 1)])

          heads = ([(a_early, a_late, [(KT, "KT", 0, 128), (KT1, "KT1", 0, 128)], "A", h, fin_A) for h in range(4)]
                   + [(f_early, f_late, [(KT, "KT", 0, 68)], "F", h, fin_F) for h in range(8)])
          for hi, (early, late, rl, kind, h, fin) in enumerate(heads):
              if hi == 0:
                  early(h)
              late(h)
              nxt = heads[hi + 1] if hi + 1 < len(heads) else None
              prefetch[0] = (lambda nxt=nxt: nxt[0](nxt[4])) if nxt is not None else None
              attention(rl, kind, h, fin)

          flush_pending()
          for _ in cg:
              pass
          chk("S2b")
          for d in range(8):
              wbs, wbk = load_w(wb[l, d], 2048)
              wgs, wgk = load_w(wg[l, d], 4096)
              for t in range(NT):
                  pb = []
                  for n in range(4):
                      b = n
                      mm_group(b, [(wbs[:, (n * 4 + k) * 128:(n * 4 + k + 1) * 128], BR[:, n * 4 + k, t * TW:(t + 1) * TW]) for k in range(4)],
                               [wbk] + [("BR", n * 4 + k, t) for k in range(4)])
                      pb.append(b)
                  i = rr("ev", 4)
                  acc = evs[i]
                  for n in range(4):
                      b = 4 + n
                      mm_group(b, [(wgs[:, (n * 8 + k) * 128:(n * 8 + k + 1) * 128], A[:, k, t * TW:(t + 1) * TW]) for k in range(8)],
                               [wgk] + [("A", k, t) for k in range(8)])
                      gi = rr("pt", 4)
                      gt = PTs[gi]
                      act(lambda e, gt=gt, b=b, n=n, d=d: e.activation(out=gt[:, :], in_=ps[b][:, :], func=AF.Sigmoid,
                                                                      bias=ppc(l, "bgate", n * 8 + d), scale=1.0),
                          [("ps", b), "pp"], [("pt", gi)])
                      if n == 0:
                          dve(lambda e, gt=gt, acc=acc: e.tensor_tensor(out=acc[:, :], in0=ps[0][:, :], in1=gt[:, :], op=ALU.mult),
                              [("ps", 0), ("pt", gi)], [("ev", i)])
                      else:
                          ti = rr("ev", 4)
                          if ti == i:
                              ti = rr("ev", 4)
                          tm = evs[ti]
                          dve(lambda e, gt=gt, tm=tm, n=n: e.tensor_tensor(out=tm[:, :], in0=ps[n][:, :], in1=gt[:, :], op=ALU.mult),
                              [("ps", n), ("pt", gi)], [("ev", ti)])
                          dve(lambda e, tm=tm, acc=acc: e.tensor_tensor(out=acc[:, :], in0=acc[:, :], in1=tm[:, :], op=ALU.add),
                              [("ev", ti), ("ev", i)], [("ev", i)])
                  si = rr("stg", 4)
                  s = stg[si]
                  act(lambda e, s=s, acc=acc: e.activation(out=s[:, :], in_=acc[:, :], func=AF.Copy), [("ev", i)], [("stg", si)])
                  dma("sp", mix_s[d, :, t * TW:(t + 1) * TW], s[:, :], [("stg", si)], [("mix", t)])

          chk("S3")
          def y_to_ysc(ci, t, b):
              i = rr("ev", 4)
              ev = evs[i]
              act(lambda e, ev=ev, b=b: e.activation(out=ev[:, :], in_=ps[b][:, :], func=AF.Copy), [("ps", b)], [("ev", i)])
              dma("sp", ysc[ci, :, t * TW:(t + 1) * TW], ev[:, :], [("ev", i)], [("ysc", t)])

          for t in range(NT):
              for c in range(8):
                  dma("sp", A[:, c, t * TW:(t + 1) * TW], mix_s[c, :, t * TW:(t + 1) * TW], [("mix", t)], [("A", c, t)])
          lin_fm(Akey, A, 8, [wo[l, d] for d in range(8)], y_to_ysc)
          fenceX()
          post_pass(l, "nmo", "nxp", l)
          fenceX()

          chk("S4")
          fenceB()
          MX = XF[:, 8192:10240].rearrange("p (c n) -> p c n", c=8)
          XQ = ARX[:, 20480:28672].rearrange("p (c n) -> p c n", c=4)
          dma("sp", MX, memT_in.rearrange("(c p) n -> p c n", p=128), [], ["MX"])
          for c in range(8):
              act(lambda e, c=c: e.activation(out=sq[:, c, 0:256], in_=MX[:, c, :], func=AF.Square), ["MX"], [("sq", c)])
          r, rk = rstd_from_sq(8, meanD, [("sq", c) for c in range(8)], n=256)
          for c in range(8):
              dve(lambda e, c=c, r=r: e.scalar_tensor_tensor(out=mTb[:, c, :], in0=MX[:, c, :], scalar=ppc(l, "nmem", c), in1=r[:, 0:256],
                                                          op0=ALU.mult, op1=ALU.mult), ["MX", rk, "pp"], ["mTb"])
          for h in range(4):
              w, wk = load_w(wxk[l, h], 1024)
              b = rr("ps", 8)
              mm_group(b, [(w[:, k * 128:(k + 1) * 128], mTb[:, k, :]) for k in range(8)], [wk, "mTb"], 0, 256)
              act(lambda e, b=b, h=h: e.activation(out=xkT[:, h, :], in_=ps[b][:, 0:256], func=AF.Copy), [("ps", b)], ["xkT"])
          w, wk = load_w(wxv[l], 4096)
          for kc2 in range(2):
              b = rr("ps", 8)
              mm_group(b, [(mTb[:, k, kc2 * 128:(kc2 + 1) * 128], w[:, k * 512:(k + 1) * 512]) for k in range(8)], [wk, "mTb"])
              act(lambda e, b=b, kc2=kc2: e.activation(out=xvs[:, kc2, :], in_=ps[b][:, :], func=AF.Copy), [("ps", b)], ["xvs"])

          def xq_h(ci, t, b):
              act(lambda e, b=b, ci=ci, t=t: e.activation(out=XQ[:, ci, t * TW:(t + 1) * TW], in_=ps[b][:, :], func=AF.Copy),
                  [("ps", b)], [("XQ", ci, t)])
          lin_fm(Akey, A, 8, [wxq[l, h] for h in range(4)], xq_h)
          xscale = 128 ** -0.5
          OX = ARX[:, 0:8192].rearrange("p (c n) -> p c n", c=4)
          for h in range(4):
              for t in range(NT):
                  pts = []
                  bo = 2 * ((h * NT + t) % 2)
                  for kc2 in range(2):
                      b = 4 + rr("ps", 4)
                      mm_group(b, [(xkT[:, h, kc2 * 128:(kc2 + 1) * 128], XQ[:, h, t * TW:(t + 1) * TW])], ["xkT", ("XQ", h, t)])
                      pi = rr("pt", 4)
                      pt = PTs[pi]
                      act(lambda e, pt=pt, b=b: e.activation(out=pt[:, :], in_=ps[b][:, :], func=AF.Exp, scale=xscale), [("ps", b)], [("pt", pi)])
                      pts.append((pt, pi))
                  mm_group(bo, [(xvs[:, kc2, h * 128:(h + 1) * 128], pts[kc2][0][:, :]) for kc2 in range(2)],
                           ["xvs"] + [("pt", p_[1]) for p_ in pts])
                  mm_group(bo + 1, [(ones_bf[:, :], pts[kc2][0][:, :]) for kc2 in range(2)], ["consts"] + [("pt", p_[1]) for p_ in pts])
                  i = rr("ev", 4)
                  ev = evs[i]
                  act(lambda e, ev=ev, bo=bo: e.activation(out=ev[:, :], in_=ps[bo + 1][:, :], func=AF.Ln), [("ps", bo + 1)], [("ev", i)])
                  act(lambda e, ev=ev: e.activation(out=ev[:, :], in_=ev[:, :], func=AF.Exp, scale=-1.0), [("ev", i)], [("ev", i)])
                  dve(lambda e, ev=ev, h=h, t=t, bo=bo: e.tensor_tensor(out=OX[:, h, t * TW:(t + 1) * TW], in0=ps[bo][:, :], in1=ev[:, :], op=ALU.mult),
                      [("ps", bo), ("ev", i)], [("OX", h, t)])
          lin_fm(lambda k, t: ("OX", k, t), OX, 4, [wxo[l, d] for d in range(8)], y_to_ysc)
          fenceX()
          post_pass(l, "nxo", "nfp", l)

          chk("S5")
          dma("sp", cF.rearrange("(c p) n -> p c n", p=128), A[:, :, 2046:2048], [("A", c, 3) for c in range(8)], ["cF"])
          P.add("pool", lambda e: e.collective_compute("AllGather", ALU.bypass, replica_groups=RG, ins=[cF_], outs=[oF_]),
                ["cF"], ["oF", "ccorder"], cc=True)
          dma("sp", hfh2[:, :, :], oF[0:1024, :].rearrange("(c p) n -> p c n", p=128), ["oF"], ["hfh2"])
          dve(lambda e: e.tensor_scalar(out=hfh[:, :, :], in0=hfh2[:, :, :], scalar1=hflag, scalar2=None, op0=ALU.mult), ["hfh2", "flags"], ["hfh"])
          ACT_T = ARX[:, 0:22528].rearrange("p (g n) -> p g n", g=NG)
          fenceX()
          for half in range(2):
              for g in range(NG):
                  i = rr("w", 3)
                  w = wsl[i]
                  wk = ("w", i)
                  dma("pool", w[:, 0:1024], wup[l, 2 * g], [], [wk])
                  dma("pool", w[:, 1024:2048], wup[l, 2 * g + 1], [], [wk])
                  if half == 0:
                      for gv in range(2):
                          b = rr("ps", 8)
                          mm_group(b, [(w[:, gv * 1024 + k * 128: gv * 1024 + (k + 1) * 128], hfh[:, k, :]) for k in range(8)],
                                   [wk, "hfh"], 0, 2)
                          dve(lambda e, b=b, g=g, gv=gv: e.tensor_copy(out=uh[:, 2 * g + gv, :], in_=ps[b][:, 0:2]), [("ps", b)], [("uh", g)])
                  for tt in range(2):
                      t = half * 2 + tt
                      res = []
                      for gv in range(2):
                          b = rr("ps", 8)
                          mm_group(b, [(w[:, gv * 1024 + k * 128: gv * 1024 + (k + 1) * 128], A[:, k, t * TW:(t + 1) * TW]) for k in range(8)],
                                   [wk] + [("A", k, t) for k in range(8)])
                          ch = 2 * g + gv
                          col = (gv * NG + g)
                          i2 = rr("ev", 4)
                          ev = evs[i2]
                          act(lambda e, ev=ev, b=b, col=col: e.activation(out=ev[:, :], in_=ps[b][:, :], func=AF.Identity,
                                                                          scale=ppc(l, "fdw", 2 * 44 + col), bias=ppc(l, "fdwb", col)),
                              [("ps", b), "pp"], [("ev", i2)])
                          dve(lambda e, ev=ev, b=b, col=col: e.scalar_tensor_tensor(out=ev[:, 1:512], in0=ps[b][:, 0:511], scalar=ppc(l, "fdw", 44 + col),
                                                                                    in1=ev[:, 1:512], op0=ALU.mult, op1=ALU.add),
                              [("ps", b), "pp", ("ev", i2)], [("ev", i2)])
                          dve(lambda e, ev=ev, b=b, col=col: e.scalar_tensor_tensor(out=ev[:, 2:512], in0=ps[b][:, 0:510], scalar=ppc(l, "fdw", col),
                                                                                    in1=ev[:, 2:512], op0=ALU.mult, op1=ALU.add),
                              [("ps", b), "pp", ("ev", i2)], [("ev", i2)])
                          dve(lambda e, ev=ev, ch=ch, col=col: e.scalar_tensor_tensor(out=ev[:, 0:1], in0=uh[:, ch, 1:2], scalar=ppc(l, "fdw", 44 + col),
                                                                                      in1=ev[:, 0:1], op0=ALU.mult, op1=ALU.add),
                              [("uh", g), "pp", ("ev", i2)], [("ev", i2)])
                          dve(lambda e, ev=ev, ch=ch, col=col: e.scalar_tensor_tensor(out=ev[:, 0:2], in0=uh[:, ch, 0:2], scalar=ppc(l, "fdw", col),
                                                                                      in1=ev[:, 0:2], op0=ALU.mult, op1=ALU.add),
                              [("uh", g), "pp", ("ev", i2)], [("ev", i2)])
                          dve(lambda e, b=b, ch=ch: e.tensor_copy(out=uh[:, ch, :], in_=ps[b][:, 510:512]), [("ps", b), ("ev", i2)], [("uh", g)])
                          res.append((ev, i2))
                      (eg, ig), (evv, iv) = res
                      si = rr("stg", 4)
                      s = stg[si]
                      act(lambda e, s=s, eg=eg: e.activation(out=s[:, :], in_=eg[:, :], func=AF.Silu), [("ev", ig)], [("stg", si)])
                      dve(lambda e, s=s, evv=evv, g=g, tt=tt: e.tensor_tensor(out=ACT_T[:, g, tt * TW:(tt + 1) * TW], in0=s[:, :], in1=evv[:, :], op=ALU.mult),
                          [("stg", si), ("ev", iv)], [("ACT_T", g, tt)])
              for d in range(8):
                  w, wk = load_w(wdn[l, d], 2816)
                  for tt in range(2):
                      t = half * 2 + tt
                      b = rr("ps", 8)
                      mm_group(b, [(w[:, g * 128:(g + 1) * 128], ACT_T[:, g, tt * TW:(tt + 1) * TW]) for g in range(NG)],
                               [wk] + [("ACT_T", g, tt) for g in range(NG)])
                      y_to_ysc(d, t, b)
          fenceX()
          last = (l == nlayers - 1)
          post_pass(l, "nfo", "nmp", min(l + 1, L - 1), final=last)

    for l_ in range(nlayers if stop != "S0" else 0):
        try:
            do_layer(l_)
        except _Stop:
            break

    P.add("sp", lambda e: e.nop(), ["out"], [])

    P.finalize(nc, es)
    with es:
        with nc.Block() as block:
            @block.tensor
            def _(e):
                P.emit("pe", e)

            @block.scalar
            def _(e):
                P.emit("act", e)

            @block.vector
            def _(e):
                P.emit("dve", e)

            @block.gpsimd
            def _(e):
                P.emit("pool", e)

            @block.sync
            def _(e):
                P.emit("sp", e)
    return nc, P


def _chunks_fm(W, cols):
    K = W.shape[0]
    kc = K // 128
    outl = []
    for c0 in cols:
        blk = W[:, c0:c0 + 128].reshape(kc, 128, 128).transpose(1, 0, 2)
        outl.append(blk.reshape(128, kc * 128))
    return np.ascontiguousarray(np.stack(outl, 0))


def _mov(W):
    K, n = W.shape
    return np.ascontiguousarray(W.reshape(K // 128, 128, n).transpose(1, 0, 2).reshape(128, (K // 128) * n))


def _cols(v):
    return np.ascontiguousarray(np.asarray(v, np.float32).reshape(-1, 128).T)


def prep_inputs(inp):
    f = lambda k: np.asarray(inp[k], np.float32)
    w_in, w_branch, w_gate, w_out = f("w_in"), f("w_branch"), f("w_gate"), f("w_out")
    w_xq, w_xkv, w_xo, w_up, w_down = f("w_xq"), f("w_xkv"), f("w_xo"), f("w_up"), f("w_down")
    seg = [0, 512, 1536, 2048, 3080, 3592, 4104, 4616, 5128]
    cols36 = [s + j * 128 for s in seg for j in range(4)]
    shared = {}
    shared["win"] = np.stack([_chunks_fm(w_in[l], cols36) for l in range(L)], 0)
    shared["wfg"] = np.stack([_mov(w_in[l][:, 3072:3080]) for l in range(L)], 0)
    shared["wv"] = np.stack([np.stack([_mov(w_in[l][:, 1024:1536]), _mov(w_in[l][:, 2560:3072])], 0) for l in range(L)], 0)
    wb = np.zeros((L, 8, 128, 2048), np.float32)
    wg = np.zeros((L, 8, 128, 4096), np.float32)
    for l in range(L):
        for d in range(8):
            wb[l, d] = np.concatenate([_chunks_fm(w_branch[l, n], [d * 128])[0] for n in range(4)], 1)
            wg[l, d] = np.concatenate([_chunks_fm(w_gate[l], [n * 1024 + d * 128])[0] for n in range(4)], 1)
    shared["wb"], shared["wg"] = wb, wg
    shared["wo"] = np.stack([_chunks_fm(w_out[l], [d * 128 for d in range(8)]) for l in range(L)], 0)
    shared["wxq"] = np.stack([_chunks_fm(w_xq[l], [h * 128 for h in range(4)]) for l in range(L)], 0)
    shared["wxk"] = np.stack([_chunks_fm(w_xkv[l], [h * 128 for h in range(4)]) for l in range(L)], 0)
    shared["wxv"] = np.stack([_mov(w_xkv[l][:, 512:1024]) for l in range(L)], 0)
    shared["wxo"] = np.stack([_chunks_fm(w_xo[l], [d * 128 for d in range(8)]) for l in range(L)], 0)
    upcols = []
    for g in range(NG):
        upcols += [g * 128, DFF + g * 128]
    shared["wup"] = np.stack([_chunks_fm(w_up[l], upcols) for l in range(L)], 0)
    shared["wdn"] = np.stack([_chunks_fm(w_down[l], [d * 128 for d in range(8)]) for l in range(L)], 0)
    pp = np.zeros((128, NPP), np.float32)

    def put(l, name, arr):
        o, w = PP[name]
        assert arr.shape == (128, w), (name, arr.shape)
        pp[:, l * PPL + o:l * PPL + o + w] = arr
    for l in range(L):
        for nm, key in (("nmp", "norm_mix_pre"), ("nmo", "norm_mix_post"), ("nxp", "norm_x_pre"), ("nxo", "norm_x_post"),
                        ("nmem", "norm_mem"), ("nfp", "norm_ffn_pre"), ("nfo", "norm_ffn_post"), ("bglu", "b_glu"),
                        ("cdwb", "conv_dw_b"), ("clng", "conv_ln_g"), ("clnb", "conv_ln_b"), ("bgate", "b_gate"),
                        ("fdwb", "ffn_dw_b"), ("dnorm", "diff_norm")):
            put(l, nm, _cols(f(key)[l]))
        cd = f("conv_dw")[l]
        put(l, "cdw", np.concatenate([cd[:, c * 128:(c + 1) * 128].T for c in range(4)], 1))
        sc = f("sc_w")[l]
        put(l, "scw", np.concatenate([sc[:, c * 128:(c + 1) * 128].T for c in range(4)], 1))
        fd = f("ffn_dw")[l]
        put(l, "fdw", np.concatenate([_cols(fd[k]) for k in range(3)], 1))
        bf = np.zeros((128, 1), np.float32)
        bf[0:8, 0] = f("b_fgt")[l]
        put(l, "bfgt", bf)
        for nm, key in (("lq1", "lam_q1"), ("lk1", "lam_k1"), ("lq2", "lam_q2"), ("lk2", "lam_k2")):
            put(l, nm, np.broadcast_to(f(key)[l][None, :], (128, 64)).copy())
    shared["pp"] = pp
    kk = np.arange(128)[:, None]
    qq = np.arange(128)[None, :]
    if MASK_PE:
        shared["masks"] = np.concatenate([np.where(kk // 64 <= qq // 64, 0.0, NEG), np.where(kk <= qq, 0.0, NEG),
                                          np.eye(128)], 1).astype(np.float32)
    else:
        shared["masks"] = np.concatenate([(kk // 64 <= qq // 64), (kk <= qq), np.eye(128)], 1).astype(np.float32)
    x = f("x")
    mem = f("mem")
    maps = []
    for c in range(8):
        b, hf = c // 2, c % 2
        m = dict(shared)
        m["xT"] = np.ascontiguousarray(x[b, hf * T:(hf + 1) * T, :].T)
        m["memT"] = np.ascontiguousarray(mem[b].T)
        fl = np.zeros((128, 2), np.float32)
        fl[:, 0] = 0.0 if hf == 1 else NEG
        fl[:, 1] = 1.0 if hf == 1 else 0.0
        m["flags"] = fl
        maps.append(m)
    return maps


_NC = {}


def kernel(**inputs):
    if "nc" not in _NC:
        _NC["nc"] = build(L)[0]
    maps = prep_inputs(inputs)
    res = run_bass_kernel_spmd(_NC["nc"], maps, core_ids=list(range(8)))
    outp = np.zeros((4, 2 * T, D), np.float32)
    for c in range(8):
        b, hf = c // 2, c % 2
        outp[b, hf * T:(hf + 1) * T, :] = np.asarray(res.results[c]["out"]).T
    return outp
```

```python
import math
import numpy as np
import concourse.bass as bass
import concourse.mybir as mybir
from concourse.bass_utils import run_bass_kernel_spmd
from contextlib import ExitStack

F32 = mybir.dt.float32
BF16 = mybir.dt.bfloat16
ALU = mybir.AluOpType
AF = mybir.ActivationFunctionType
AX = mybir.AxisListType

L = 4
D = 1024
T = 2048
NT = 4
TW = 512
KC = 8
EPS = 1e-6
NEG = -30000.0
DFF = 2816
NG = 22
MASK_PE = True
F_PINGPONG = True

PP = {}
_o = 0
for _n, _w in [("nmp", 8), ("nmo", 8), ("nxp", 8), ("nxo", 8), ("nmem", 8), ("nfp", 8), ("nfo", 8),
               ("bglu", 8), ("cdw", 124), ("cdwb", 4), ("clng", 4), ("clnb", 4), ("scw", 12),
               ("bgate", 32), ("fdw", 132), ("fdwb", 44), ("dnorm", 1), ("bfgt", 1),
               ("lq1", 64), ("lk1", 64), ("lq2", 64), ("lk2", 64)]:
    PP[_n] = (_o, _w)
    _o += _w
PPL = _o
NPP = PPL * L


class _Op:
    __slots__ = ("eng", "fn", "deps", "signal", "sigidx", "dma", "sem", "semval", "prev", "idx", "cc")


class Prog:
    ENGS = ("pe", "act", "dve", "pool", "sp")
    KQ = 8

    def __init__(self):
        self.ops = []
        self.lastw = {}
        self.readers = {}
        self.gnames = set()
        self.groups = {}

    def _expand(self, reads, writes):
        r2, w2, extra = [], [], []
        for k in reads:
            if k in self.gnames:
                g = self.groups.setdefault(k, {"mem": [], "read": False, "n": 0})
                g["read"] = True
                r2.extend(g["mem"])
            else:
                r2.append(k)
        for k in writes:
            if k in self.gnames:
                g = self.groups.setdefault(k, {"mem": [], "read": False, "n": 0})
                if g["read"]:
                    extra.extend(g["mem"])
                    g["mem"] = []
                    g["read"] = False
                g["n"] += 1
                sk = ("#g", k, g["n"])
                g["mem"].append(sk)
                w2.append(sk)
            else:
                w2.append(k)
        return r2, w2, extra

    def add(self, eng, fn, reads=(), writes=(), dma=False, cc=False):
        op = _Op()
        op.eng, op.fn, op.dma, op.cc = eng, fn, dma or cc, cc
        op.signal = op.dma
        op.idx = len(self.ops)
        op.sem = op.semval = op.prev = op.sigidx = None
        deps = set()
        reads, writes, extra = self._expand(list(reads), list(writes))
        for k in extra:
            w = self.lastw.get(k)
            if w is not None:
                deps.add(w)
            for rd in self.readers.get(k, ()):
                deps.add(rd)
        for r in reads:
            w = self.lastw.get(r)
            if w is not None:
                deps.add(w)
        for k in writes:
            w = self.lastw.get(k)
            if w is not None:
                deps.add(w)
            for rd in self.readers.get(k, ()):
                deps.add(rd)
        op.deps = deps
        for d in deps:
            self.ops[d].signal = True
        for r in reads:
            self.readers.setdefault(r, []).append(op.idx)
        for k in writes:
            self.lastw[k] = op.idx
            self.readers[k] = []
        self.ops.append(op)
        return op.idx

    def finalize(self, nc, es):
        self.sems = {e: es.enter_context(nc.semaphore("pg_" + e)) for e in self.ENGS}
        self.dsems = {q: [es.enter_context(nc.semaphore("dq_%s%d" % (q, i))) for i in range(self.KQ)]
                      for q in ("sp", "pool", "act")}
        cnt = {e: 0 for e in self.ENGS}
        dq = {"sp": [], "pool": [], "act": []}
        for op in self.ops:
            if op.cc:
                op.sem = es.enter_context(nc.semaphore("cc%d" % op.idx))
                op.semval = 1
            elif op.dma:
                lst = dq[op.eng]
                n = len(lst)
                op.sem = self.dsems[op.eng][n % self.KQ]
                op.semval = 16 * (n // self.KQ + 1)
                op.prev = lst[n - self.KQ] if n >= self.KQ else None
                lst.append(op.idx)
            elif op.signal:
                cnt[op.eng] += 1
                op.sigidx = cnt[op.eng]

    def emit(self, ename, eng):
        seen = {}
        ops = self.ops

        def wait(sem, key, val):
            if seen.get(key, 0) < val:
                eng.wait_ge(sem, val)
                seen[key] = val

        for op in ops:
            if op.eng != ename:
                continue
            for d in sorted(op.deps):
                dop = ops[d]
                if dop.dma:
                    wait(dop.sem, ("d", id(dop.sem)), dop.semval)
                else:
                    if dop.eng == ename and ename == "pe" and not op.dma:
                        continue
                    wait(self.sems[dop.eng], dop.eng, dop.sigidx)
            if op.dma and not op.cc and op.prev is not None:
                p = ops[op.prev]
                wait(p.sem, ("d", id(p.sem)), p.semval)
            inst = op.fn(eng)
            if op.cc:
                inst.then_inc(op.sem)
            elif op.dma:
                inst.then_inc(op.sem, 16)
            elif op.signal:
                inst.then_inc(self.sems[ename], 1)


class _Stop(Exception):
    pass


def build(nlayers=L, stop=None):
    nc = bass.Bass("TRN2", target_bir_lowering=False)
    P = Prog()
    P.gnames = set(["qa", "cKA", "qf", "kfo", "cKF", "cKFa", "cV", "cH", "glu", "sxc", "sb"]
                   + [("mix", t) for t in range(NT)] + [("ysc", t) for t in range(NT)])

    def chk(name):
        if stop == name:
            raise _Stop()

    def din(name, shape):
        return nc.dram_tensor(name, list(shape), F32, kind="ExternalInput").ap()

    xT_in = din("xT", [D, T])
    memT_in = din("memT", [D, 256])
    pp_in = din("pp", [128, NPP])
    flags_in = din("flags", [128, 2])
    masks_in = din("masks", [128, 384])
    win = din("win", [L, 36, 128, 1024])
    wfg = din("wfg", [L, 128, 64])
    wv = din("wv", [L, 2, 128, 4096])
    wb = din("wb", [L, 8, 128, 2048])
    wg = din("wg", [L, 8, 128, 4096])
    wo = din("wo", [L, 8, 128, 1024])
    wxq = din("wxq", [L, 4, 128, 1024])
    wxk = din("wxk", [L, 4, 128, 1024])
    wxv = din("wxv", [L, 128, 4096])
    wxo = din("wxo", [L, 8, 128, 512])
    wup = din("wup", [L, 44, 128, 1024])
    wdn = din("wdn", [L, 8, 128, 2816])
    out = nc.dram_tensor("out", [D, T], F32, kind="ExternalOutput").ap()

    def dscr(name, shape, dt):
        return nc.dram_tensor(name, list(shape), dt).ap()

    xs = dscr("xs", [8, 128, T], F32)
    ysc = dscr("ysc", [8, 128, T], F32)
    qa_s = dscr("qa_s", [512, T], BF16)
    qf_s = dscr("qf_s", [8 * 68, T], BF16)
    kfo_s = dscr("kfo_s", [8 * 68, T], BF16)
    glu_s = dscr("glu_s", [512, T], BF16)
    sxc_s = dscr("sxc_s", [512, T], BF16)
    sb_s = dscr("sb_s", [512, T], BF16)
    zf_s = dscr("zf_s", [8, 128, T], F32)
    mix_s = dscr("mix_s", [8, 128, T], BF16)
    cKA = dscr("cKA", [512, T], BF16)
    oKA = dscr("oKA", [1024, T], BF16)
    cKFk = dscr("cKFk", [512, T], BF16)
    oKFk = dscr("oKFk", [1024, T], BF16)
    cKFa = dscr("cKFa", [32, T], BF16)
    oKFa = dscr("oKFa", [64, T], BF16)
    cVs_ = [dscr("cV%d" % i, [512, T], BF16) for i in range(3)]
    oVs_ = [dscr("oV%d" % i, [1024, T], BF16) for i in range(3)]
    cH_ = dscr("cH", [16, T], BF16)
    oH_ = dscr("oH", [32, T], BF16)
    cF_ = dscr("cF", [1, T], BF16)
    oF_ = dscr("oF", [2, T], BF16)
    cVg = [a.rearrange("r (a c) -> (r a) c", c=128).rearrange("(h t) c -> h t c", h=4) for a in cVs_]
    oVg = [a.rearrange("r (a c) -> (r a) c", c=128).rearrange("(r h t) c -> r h t c", r=2, h=4) for a in oVs_]
    cH = cH_.rearrange("r (a c) -> (r a) c", c=64)
    oH = oH_.rearrange("r (a c) -> (r a) c", c=64)
    cF = cF_.rearrange("r (a c) -> (r a) c", c=2)
    oF = oF_.rearrange("r (a c) -> (r a) c", c=2)
    RG = [[0, 1], [2, 3], [4, 5], [6, 7]]

    es = ExitStack()

    def sb(name, shape, dt):
        return es.enter_context(nc.sbuf_tensor("s_" + name, list(shape), dt))

    ARX = sb("ARX", [128, 32768], BF16)
    ARA = sb("ARA", [128, 16384], BF16)
    ARB = sb("ARB", [128, 14336], BF16)
    wsl = [sb("wsl%d" % i, [128, 4096], BF16) for i in range(3)]
    PTs = [sb("PT%d" % i, [128, 512], BF16) for i in range(4)]
    evs = [sb("ev%d" % i, [128, 512], F32) for i in range(4)]
    sq = sb("sq", [128, 8, 512], BF16)
    stg = [sb("stg%d" % i, [128, 512], BF16) for i in range(4)]
    pp = sb("pp", [128, NPP], F32)
    flags = sb("flags", [128, 2], F32)
    ones_bf = sb("ones_bf", [128, 128], BF16)
    meanD = sb("meanD", [128, 128], BF16)
    mean512 = sb("mean512", [128, 128], BF16)
    masks = sb("masks", [128, 384], BF16)
    small = sb("small", [128, 16], F32)
    uh = sb("uh", [128, 44, 2], F32)
    hfh = sb("hfh", [128, 8, 2], BF16)
    hfh2 = sb("hfh2", [128, 8, 2], BF16)
    vst = sb("vst", [128, 8, 128], BF16)
    ONE8t = sb("ONE8", [8, 2048], BF16)
    ONE8 = ONE8t[:, :]
    ARC = sb("ARC", [128, 8704], BF16)
    ps = [es.enter_context(nc.psum_tensor("ps%d" % i, [128, 512], F32)) for i in range(8)]

    mTb = ARB[:, 0:2048].rearrange("p (c n) -> p c n", c=8)
    xkT = ARB[:, 2048:3072].rearrange("p (c n) -> p c n", c=4)
    xvs = ARB[:, 3072:4096].rearrange("p (c n) -> p c n", c=2)
    A = ARA[:, :].rearrange("p (c n) -> p c n", c=8)
    BR = ARX[:, :].rearrange("p (c n) -> p c n", c=16)
    XF = ARX[:, :].bitcast(F32)
    pmask = flags[:, 0:1]
    hflag = flags[:, 1:2]

    ctr = {"ps": 0, "ev": 0, "stg": 0, "pt": 0, "w": 0}

    def rr(kind, n):
        i = ctr[kind] % n
        ctr[kind] += 1
        return i

    def ppc(l, name, j=0, n=1):
        o, w = PP[name]
        return pp[:, l * PPL + o + j: l * PPL + o + j + n]

    def pe(fn, reads, writes):
        return P.add("pe", fn, reads, writes)

    def act(fn, reads, writes):
        return P.add("act", fn, reads, writes)

    def dve(fn, reads, writes):
        return P.add("dve", fn, reads, writes)

    def dma(q, o, i, reads, writes):
        return P.add(q, lambda e, o=o, i=i: e.dma_start(out=o, in_=i), reads, writes, dma=True)

    def mm_group(bank, pairs, reads, n0=0, n1=512, rows=128):
        def fn(e, bank=bank, pairs=pairs):
            last = None
            for i, (lt, rh) in enumerate(pairs):
                last = e.matmul(ps[bank][0:rows, n0:n1], lt, rh, start=(i == 0), stop=(i == len(pairs) - 1))
            return last
        return pe(fn, reads, [("ps", bank)])

    dma("sp", pp[:, :], pp_in, [], ["pp"])
    dma("sp", flags[:, :], flags_in, [], ["flags"])
    dma("pool", masks[:, :], masks_in, [], ["masks"])
    dve(lambda e: e.memset(ones_bf[:, :], 1.0), [], ["consts"])
    dve(lambda e: e.memset(meanD[:, :], 1.0 / 1024), [], ["consts"])
    dve(lambda e: e.memset(mean512[:, :], 1.0 / 512), [], ["consts"])
    dve(lambda e: e.memset(vst[:, :, 64:128], 1.0), [], ["vst"])
    dve(lambda e: e.memset(ONE8, 1.0), [], ["ONE8"])
    dve(lambda e: e.memset(small[:, :], 0.0), [], ["small"])
    dve(lambda e: e.memset(small[:, 2:3], EPS), ["small"], ["small"])

    XKEYS = (["XT0", "YT0", ("XT0", 0), ("XT0", 1), ("YT0", 0), ("YT0", 1), "FZ", "FS", "FC", "FD", "FO", "HB", "MX", "ACT_T"]
             + [("XQ", h, t) for h in range(4) for t in range(NT)] + [("OX", h, t) for h in range(4) for t in range(NT)]
             + [("BR", c, t) for c in range(16) for t in range(NT)] + [("ACT_T", g, tt) for g in range(NG) for tt in range(2)])
    BKEYS = ([("KT", 0), ("KT", 1), ("KT1", 0), ("KT1", 1), ("VV", 0), ("VV", 1)] + [("QT", j) for j in range(NT)] + ["mTb", "xkT", "xvs"])

    def fenceX():
        P.add("dve", lambda e: e.memset(small[:, 8:9], 0.0), [], XKEYS)

    def fenceB():
        P.add("dve", lambda e: e.memset(small[:, 8:9], 0.0), [], BKEYS)

    def rstd_from_sq(nchunks, meanmat, sq_reads, n=512):
        b = 4 + rr("ps", 4)
        mm_group(b, [(meanmat[:, :], sq[:, c, 0:n]) for c in range(nchunks)], sq_reads + ["consts"], 0, n)
        i = rr("ev", 4)
        r = evs[i]
        act(lambda e, r=r, b=b: e.activation(out=r[:, 0:n], in_=ps[b][:, 0:n], func=AF.Ln, bias=small[:, 2:3], scale=1.0),
            [("ps", b), "small"], [("ev", i)])
        act(lambda e, r=r: e.activation(out=r[:, 0:n], in_=r[:, 0:n], func=AF.Exp, scale=-0.5),
            [("ev", i)], [("ev", i)])
        return r, ("ev", i)

    def prenorm_tile(xt, xkey, l, gname, t):
        for c in range(8):
            act(lambda e, c=c: e.activation(out=sq[:, c, :], in_=xt[:, c, :], func=AF.Square), [xkey], [("sq", c)])
        r, rk = rstd_from_sq(8, meanD, [("sq", c) for c in range(8)])
        for c in range(8):
            dve(lambda e, c=c, r=r: e.scalar_tensor_tensor(out=A[:, c, t * TW:(t + 1) * TW], in0=xt[:, c, :],
                                                        scalar=ppc(l, gname, c), in1=r[:, :],
                                                        op0=ALU.mult, op1=ALU.mult),
                [xkey, rk, "pp"], [("A", c, t)])

    XT0 = XF[:, 0:4096].rearrange("p (c n) -> p c n", c=8)
    YT0 = XF[:, 4096:8192].rearrange("p (c n) -> p c n", c=8)

    def post_pass(l, gpost, gnext, lnext, final=False, tiled=False):
        SW = 256
        NU = T // SW
        outv = out.rearrange("(c p) n -> p c n", p=128)

        def bufs(u):
            hh = u % 2
            return (YT0[:, :, hh * SW:(hh + 1) * SW], ("YT0", hh), XT0[:, :, hh * SW:(hh + 1) * SW], ("XT0", hh))

        def stage_a(u):
            YT, yk, XT, xk = bufs(u)
            c0 = u * SW
            dma("sp", YT, ysc[:, :, c0:c0 + SW].rearrange("c p n -> p c n"), [("ysc", u // 2)], [yk])
            dma("sp", XT, xs[:, :, c0:c0 + SW].rearrange("c p n -> p c n"), [("xs", u // 2)], [xk])
            for c in range(8):
                act(lambda e, c=c, YT=YT: e.activation(out=sq[:, c, 0:SW], in_=YT[:, c, :], func=AF.Square), [yk], [("sq", c)])
            return rstd_from_sq(8, meanD, [("sq", c) for c in range(8)], n=SW)

        def stage_b(u, r, rk):
            YT, yk, XT, xk = bufs(u)
            c0 = u * SW
            for c in range(8):
                dve(lambda e, c=c, r=r, YT=YT: e.scalar_tensor_tensor(out=YT[:, c, :], in0=YT[:, c, :], scalar=ppc(l, gpost, c),
                                                                   in1=r[:, 0:SW], op0=ALU.mult, op1=ALU.mult),
                    [yk, rk, "pp"], [yk])
                dve(lambda e, c=c, YT=YT, XT=XT: e.tensor_tensor(out=XT[:, c, :], in0=XT[:, c, :], in1=YT[:, c, :], op=ALU.add),
                    [yk, xk], [xk])
            if final:
                dma("sp", outv[:, :, c0:c0 + SW], XT, [xk], ["out"])
                return None
            dma("sp", xs[:, :, c0:c0 + SW].rearrange("c p n -> p c n"), XT, [xk], [("xs", u // 2)])
            for c in range(8):
                act(lambda e, c=c, XT=XT: e.activation(out=sq[:, c, 0:SW], in_=XT[:, c, :], func=AF.Square), [xk], [("sq", c)])
            return rstd_from_sq(8, meanD, [("sq", c) for c in range(8)], n=SW)

        def stage_c(u, r, rk):
            YT, yk, XT, xk = bufs(u)
            c0 = u * SW
            for c in range(8):
                dve(lambda e, c=c, r=r, XT=XT: e.scalar_tensor_tensor(out=A[:, c, c0:c0 + SW], in0=XT[:, c, :], scalar=ppc(lnext, gnext, c),
                                                                   in1=r[:, 0:SW], op0=ALU.mult, op1=ALU.mult),
                    [xk, rk, "pp"], [("A", c, u // 2)])

        def do_tile(t):
            ra0 = stage_a(2 * t)
            ra1 = stage_a(2 * t + 1)
            for u, ra_ in ((2 * t, ra0), (2 * t + 1, ra1)):
                r2 = stage_b(u, *ra_)
                if r2 is not None:
                    stage_c(u, *r2)

        if tiled:
            return do_tile
        ra = {0: stage_a(0)}
        for u in range(NU):
            if u + 1 < NU:
                ra[u + 1] = stage_a(u + 1)
            r2 = stage_b(u, *ra[u])
            if r2 is not None:
                stage_c(u, *r2)

    def load_w(src, ncol, key_extra=()):
        i = rr("w", 3)
        w = wsl[i]
        dma("pool", w[:, 0:ncol], src, [], [("w", i)])
        return w, ("w", i)

    def lin_fm(inp_keyfn, inp, kc, wsrcs, handler, tiles=range(NT)):
        for ci, src in enumerate(wsrcs):
            w, wk = load_w(src, kc * 128)
            for t in tiles:
                b = rr("ps", 8)
                mm_group(b, [(w[:, k * 128:(k + 1) * 128], inp[:, k, t * TW:(t + 1) * TW]) for k in range(kc)],
                         [wk] + [inp_keyfn(k, t) for k in range(kc)])
                handler(ci, t, b)

    def lin_tile_outer(inp_keyfn, inp, kc, srcs, nchunk_per_src, handler, after_tile):
        ws = []
        for src in srcs:
            i = rr("w", 3)
            w = wsl[i]
            dma("pool", w[:, 0:nchunk_per_src * kc * 128].rearrange("p (c n) -> p c n", c=nchunk_per_src),
                src.rearrange("c p n -> p c n"), [], [("w", i)])
            ws.append((w, ("w", i)))
        for t in range(NT):
            for d in range(len(srcs) * nchunk_per_src):
                w, wk = ws[d // nchunk_per_src]
                o = (d % nchunk_per_src) * kc * 128
                b = rr("ps", 8)
                mm_group(b, [(w[:, o + k * 128:o + (k + 1) * 128], inp[:, k, t * TW:(t + 1) * TW]) for k in range(kc)],
                         [wk] + [inp_keyfn(k, t) for k in range(kc)])
                handler(d, t, b)
            if t >= 1:
                after_tile(t - 1)
        after_tile(NT - 1)

    def Akey(k, t):
        return ("A", k, t)

    for t in range(NT):
        dma("sp", XT0, xT_in.rearrange("(c p) n -> p c n", p=128)[:, :, t * TW:(t + 1) * TW], [], ["XT0"])
        dma("sp", xs[:, :, t * TW:(t + 1) * TW].rearrange("c p n -> p c n"), XT0, ["XT0"], [("xs", t)])
        prenorm_tile(XT0, "XT0", 0, "nmp", t)

    def do_layer(l):
      if True:
          lam_init = 0.8 - 0.6 * math.exp(-0.3 * l)
          tmp64 = evs[0]
          for j, (a_, b_) in enumerate((("lq1", "lk1"), ("lq2", "lk2"))):
              dve(lambda e, a_=a_, b_=b_: e.tensor_tensor(out=tmp64[:, 0:64], in0=ppc(l, a_, 0, 64), in1=ppc(l, b_, 0, 64), op=ALU.mult),
                  ["pp"], [("ev", 0)])
              dve(lambda e, j=j: e.reduce_sum(out=small[:, 4 + j:5 + j], in_=tmp64[:, 0:64], axis=AX.X), [("ev", 0)], ["small"])
          act(lambda e: e.activation(out=small[:, 4:6], in_=small[:, 4:6], func=AF.Exp), ["small"], ["small"])
          dve(lambda e: e.scalar_tensor_tensor(out=small[:, 0:1], in0=small[:, 5:6], scalar=-lam_init, in1=small[:, 4:5],
                                               op0=ALU.add, op1=ALU.subtract), ["small"], ["small"])
          dve(lambda e: e.tensor_scalar(out=small[:, 1:2], in0=ppc(l, "dnorm"), scalar1=1.0 - lam_init, scalar2=None, op0=ALU.mult),
              ["pp"], ["small"])
          dve(lambda e: e.tensor_scalar(out=small[:, 3:4], in0=ppc(l, "bfgt"), scalar1=-1.0, scalar2=None, op0=ALU.mult),
              ["pp"], ["small"])

          fenceX()
          fenceB()
          def evac_scaled_to(dst_rows_fn, scale):
              def h(ci, t, b):
                  i = rr("stg", 4)
                  s = stg[i]
                  act(lambda e, s=s, b=b: e.activation(out=s[:, :], in_=ps[b][:, :], func=AF.Copy, scale=scale),
                      [("ps", b)], [("stg", i)])
                  for (dst, key, r0, r1) in dst_rows_fn(ci):
                      dma("sp", dst[:, t * TW:(t + 1) * TW], s[r0:r1, :], [("stg", i)], [key])
              return h

          lin_fm(Akey, A, 8, [win[l, c] for c in range(4, 8)],
                 evac_scaled_to(lambda ci: [(cKA[ci * 128:(ci + 1) * 128, :], "cKA", 0, 128)], 1.0))
          lin_fm(Akey, A, 8, [win[l, c] for c in range(12, 16)],
                 evac_scaled_to(lambda ci: [(kfo_s[(2 * ci) * 68:(2 * ci) * 68 + 64, :], "kfo", 0, 64),
                                            (kfo_s[(2 * ci + 1) * 68:(2 * ci + 1) * 68 + 64, :], "kfo", 64, 128),
                                            (cKFk[ci * 128:(ci + 1) * 128, :], "cKF", 0, 128)], 1.0))

          FZ = XF[0:8, 0:2048]
          FS = XF[0:8, 2048:4096]
          FC = XF[0:8, 4096:6144]
          FD = XF[0:8, 6144:8192]
          FO = XF[0:8, 8192:10240]
          HB = ARX[0:8, 20480:32768].rearrange("p (a n) -> p a n", a=6)
          wfs, wfk = load_w(wfg[l], 64)
          for t in range(NT):
              b = rr("ps", 8)
              mm_group(b, [(wfs[:, k * 8:(k + 1) * 8], A[:, k, t * TW:(t + 1) * TW]) for k in range(8)],
                       [wfk] + [("A", k, t) for k in range(8)], rows=8)
              act(lambda e, b=b, t=t: e.activation(out=FZ[:, t * TW:(t + 1) * TW], in_=ps[b][0:8, :], func=AF.Exp,
                                                   bias=small[0:8, 3:4], scale=-1.0), [("ps", b), "small"], ["FZ"])
          act(lambda e: e.activation(out=FS, in_=FZ, func=AF.Ln, bias=1.0, scale=1.0), ["FZ"], ["FS"])
          dve(lambda e: e.memset(FO, 1.0), [], ["FO"])
          dve(lambda e: e.tensor_tensor_scan(out=FC, data0=FO, data1=FS, initial=0.0, op0=ALU.mult, op1=ALU.add),
              ["FO", "FS"], ["FC"])

          def hilo(src_fn, ihi, key):
              dve(lambda e: src_fn(e, FD), [key, "FC"], ["FD"])
              dve(lambda e: e.tensor_copy(out=HB[:, ihi, :], in_=FD), ["FD"], ["HB"])
              dve(lambda e: e.tensor_tensor(out=HB[:, ihi + 1, :], in0=FD, in1=HB[:, ihi, :], op=ALU.subtract), ["FD", "HB"], ["HB"])
          hilo(lambda e, o: e.tensor_scalar(out=o, in0=FC, scalar1=-1.0, scalar2=None, op0=ALU.mult), 0, "FC")
          hilo(lambda e, o: e.tensor_copy(out=o, in_=FC), 2, "FC")
          hilo(lambda e, o: e.tensor_scalar(out=o, in0=FC, scalar1=FC[:, 2047:2048], scalar2=None, op0=ALU.subtract), 4, "FC")
          qf3 = qf_s.rearrange("(h r) n -> h r n", r=68)
          kfo3 = kfo_s.rearrange("(h r) n -> h r n", r=68)
          cKFa3 = cKFa.rearrange("(h r) n -> h r n", r=4)
          dma("sp", qf3[:, 64, :], HB[:, 0, :], ["HB"], ["qf"])
          dma("sp", qf3[:, 65, :], HB[:, 1, :], ["HB"], ["qf"])
          dma("sp", qf3[:, 66, :], ONE8, ["ONE8"], ["qf"])
          dma("sp", qf3[:, 67, :], ONE8, ["ONE8"], ["qf"])
          for dst, key, ih, r0 in ((kfo3, "kfo", 2, 64), (cKFa3, "cKFa", 4, 0)):
              dma("sp", dst[:, r0, :], ONE8, ["ONE8"], [key])
              dma("sp", dst[:, r0 + 1, :], ONE8, ["ONE8"], [key])
              dma("sp", dst[:, r0 + 2, :], HB[:, ih, :], ["HB"], [key])
              dma("sp", dst[:, r0 + 3, :], HB[:, ih + 1, :], ["HB"], [key])

          dve(lambda e: e.memset(vst[:, :, 64:128], 1.0), [], ["vst"])
          for vi in range(2):
              w, wk = load_w(wv[l, vi], 4096)
              for blk in range(16):
                  b = rr("ps", 8)
                  t_, o_ = blk // 4, (blk % 4) * 128
                  mm_group(b, [(A[:, k, blk * 128:(blk + 1) * 128], w[:, k * 512:(k + 1) * 512]) for k in range(8)],
                           [wk] + [("A", k, t_) for k in range(8)])
                  if vi == 0:
                      si = rr("stg", 4)
                      s = stg[si]
                      act(lambda e, s=s, b=b: e.activation(out=s[:, :], in_=ps[b][:, :], func=AF.Copy), [("ps", b)], [("stg", si)])
                      dma("sp", cVg[0][:, blk * 128:(blk + 1) * 128, :].rearrange("h t c -> t h c"),
                          s[:, :].rearrange("p (h c) -> p h c", h=4), [("stg", si)], ["cV"])
                  else:
                      act(lambda e, b=b: e.activation(out=vst[:, :, 0:64], in_=ps[b][:, :].rearrange("p (h c) -> p h c", h=8), func=AF.Copy),
                          [("ps", b)], ["vst"])
                      dma("sp", cVg[1][:, blk * 128:(blk + 1) * 128, :].rearrange("h t c -> t h c"), vst[:, 0:4, :], ["vst"], ["cV"])
                      dma("sp", cVg[2][:, blk * 128:(blk + 1) * 128, :].rearrange("h t c -> t h c"), vst[:, 4:8, :], ["vst"], ["cV"])

          for j in range(4):
              wa, wak = load_w(win[l, 16 + j], 1024)
              wgl, wgk = load_w(win[l, 20 + j], 1024)
              for t in range(NT):
                  ba = rr("ps", 8)
                  mm_group(ba, [(wa[:, k * 128:(k + 1) * 128], A[:, k, t * TW:(t + 1) * TW]) for k in range(8)],
                           [wak] + [("A", k, t) for k in range(8)])
                  bg = rr("ps", 8)
                  mm_group(bg, [(wgl[:, k * 128:(k + 1) * 128], A[:, k, t * TW:(t + 1) * TW]) for k in range(8)],
                           [wgk] + [("A", k, t) for k in range(8)])
                  i = rr("ev", 4)
                  ev = evs[i]
                  act(lambda e, ev=ev, bg=bg, j=j: e.activation(out=ev[:, :], in_=ps[bg][:, :], func=AF.Sigmoid,
                                                               bias=ppc(l, "bglu", 4 + j), scale=1.0),
                      [("ps", bg), "pp"], [("ev", i)])
                  si = rr("stg", 4)
                  s = stg[si]
                  dve(lambda e, s=s, ev=ev, ba=ba, j=j: e.scalar_tensor_tensor(out=s[:, :], in0=ps[ba][:, :], scalar=ppc(l, "bglu", j),
                                                                           in1=ev[:, :], op0=ALU.add, op1=ALU.mult),
                      [("ps", ba), ("ev", i), "pp"], [("stg", si)])
                  dma("sp", glu_s[j * 128:(j + 1) * 128, t * TW:(t + 1) * TW], s[:, :], [("stg", si)], ["glu"])
                  if t == NT - 1:
                      dma("sp", cH[j * 128:(j + 1) * 128, 0:30], s[:, 482:512], [("stg", si)], ["cH"])
          for j in range(4):
              wa, wak = load_w(win[l, 24 + j], 1024)
              wgl, wgk = load_w(win[l, 32 + j], 1024)
              for t in range(NT):
                  ba = rr("ps", 8)
                  mm_group(ba, [(wa[:, k * 128:(k + 1) * 128], A[:, k, t * TW:(t + 1) * TW]) for k in range(8)],
                           [wak] + [("A", k, t) for k in range(8)])
                  bg = rr("ps", 8)
                  mm_group(bg, [(wgl[:, k * 128:(k + 1) * 128], A[:, k, t * TW:(t + 1) * TW]) for k in range(8)],
                           [wgk] + [("A", k, t) for k in range(8)])
                  i = rr("ev", 4)
                  ev = evs[i]
                  act(lambda e, ev=ev, bg=bg: e.activation(out=ev[:, :], in_=ps[bg][:, :], func=AF.Copy),
                      [("ps", bg)], [("ev", i)])
                  si = rr("stg", 4)
                  s = stg[si]
                  dve(lambda e, s=s, ev=ev, ba=ba: e.tensor_tensor(out=s[:, :], in0=ps[ba][:, :], in1=ev[:, :], op=ALU.mult),
                      [("ps", ba), ("ev", i)], [("stg", si)])
                  dma("sp", sxc_s[j * 128:(j + 1) * 128, t * TW:(t + 1) * TW], s[:, :], [("stg", si)], ["sxc"])
                  if t == NT - 1:
                      dma("sp", cH[j * 128:(j + 1) * 128, 32:34], s[:, 510:512], [("stg", si)], ["cH"])
          chk("S1")
          for (ci_, co_, ki, ko) in ((cKA, oKA, "cKA", "oKA"), (cKFk, oKFk, "cKF", "oKF"), (cKFa, oKFa, "cKFa", "oKFa"), (cVs_[0], oVs_[0], "cV", "oV0"),
                                       (cVs_[1], oVs_[1], "cV", "oV1"), (cVs_[2], oVs_[2], "cV", "oV2"), (cH_, oH_, "cH", "oH")):
              P.add("pool", lambda e, ci_=ci_, co_=co_: e.collective_compute("AllGather", ALU.bypass, replica_groups=RG,
                                                                            ins=[ci_], outs=[co_]),
                    [ki], [ko], cc=True)

          fenceX()
          fenceB()
          lin_fm(Akey, A, 8, [win[l, c] for c in range(0, 4)],
                 evac_scaled_to(lambda ci: [(qa_s[ci * 128:(ci + 1) * 128, :], "qa", 0, 128)], 0.125))
          lin_fm(Akey, A, 8, [win[l, c] for c in range(8, 12)],
                 evac_scaled_to(lambda ci: [(qf_s[(2 * ci) * 68:(2 * ci) * 68 + 64, :], "qf", 0, 64),
                                            (qf_s[(2 * ci + 1) * 68:(2 * ci + 1) * 68 + 64, :], "qf", 64, 128)], 0.125))
          lin_fm(Akey, A, 8, [win[l, c] for c in range(28, 32)],
                 evac_scaled_to(lambda ci: [(sb_s[ci * 128:(ci + 1) * 128, :], "sb", 0, 128)], 1.0))

          chk("CC")
          def conv_gen():
              GB = ARC[:, 0:2176].rearrange("p (c n) -> p c n", c=4)
              ACC = ARC[:, 2176:6272].bitcast(F32).rearrange("p (c n) -> p c n", c=4)
              HL = ARC[:, 6272:6528].rearrange("p (c n) -> p c n", c=4)
              SB2 = ARC[:, 6528:8576].rearrange("p (c n) -> p c n", c=4)
              dma("sp", HL, oH[0:512, :].rearrange("(c p) n -> p c n", p=128), ["oH"], ["HL"])
              dve(lambda e: e.tensor_scalar(out=HL, in0=HL, scalar1=hflag, scalar2=None, op0=ALU.mult), ["HL", "flags"], ["HL"])
              for t in range(NT):
                  if t == 0:
                      dve(lambda e: e.tensor_copy(out=GB[:, :, 0:30], in_=HL[:, :, 0:30]), ["HL"], ["GB"])
                      dma("sp", GB[:, :, 30:542], glu_s[:, 0:TW].rearrange("(c p) n -> p c n", p=128), ["glu"], ["GB"])
                  else:
                      dma("sp", GB[:, :, 0:542], glu_s[:, t * TW - 30:(t + 1) * TW].rearrange("(c p) n -> p c n", p=128), ["glu"], ["GB"])
                  for c in range(4):
                      dve(lambda e, c=c: e.tensor_scalar(out=ACC[:, c, :], in0=GB[:, c, 0:512], scalar1=ppc(l, "cdw", c * 31),
                                                         scalar2=ppc(l, "cdwb", c), op0=ALU.mult, op1=ALU.add),
                          ["GB", "pp"], [("ACC", c)])
                      for k in range(1, 31):
                          dve(lambda e, c=c, k=k: e.scalar_tensor_tensor(out=ACC[:, c, :], in0=GB[:, c, k:k + 512],
                                                                         scalar=ppc(l, "cdw", c * 31 + k), in1=ACC[:, c, :],
                                                                         op0=ALU.mult, op1=ALU.add),
                              ["GB", "pp", ("ACC", c)], [("ACC", c)])
                          if k % 3 == 0:
                              yield
                      yield ("ln" if c == 3 else None)
                  for c in range(4):
                      act(lambda e, c=c: e.activation(out=sq[:, c, :], in_=ACC[:, c, :], func=AF.Copy), [("ACC", c)], [("sq", c)])
                      act(lambda e, c=c: e.activation(out=sq[:, 4 + c, :], in_=ACC[:, c, :], func=AF.Square), [("ACC", c)], [("sq", 4 + c)])
                  bm = 4 + rr("ps", 4)
                  mm_group(bm, [(mean512[:, :], sq[:, c, :]) for c in range(4)], [("sq", c) for c in range(4)] + ["consts"])
                  bq = 4 + rr("ps", 4)
                  mm_group(bq, [(mean512[:, :], sq[:, 4 + c, :]) for c in range(4)], [("sq", 4 + c) for c in range(4)] + ["consts"])
                  i0 = rr("ev", 4)
                  mu = evs[i0]
                  act(lambda e, mu=mu, bm=bm: e.activation(out=mu[:, :], in_=ps[bm][:, :], func=AF.Copy), [("ps", bm)], [("ev", i0)])
                  i1 = rr("ev", 4)
                  rs = evs[i1]
                  dve(lambda e, rs=rs, mu=mu: e.tensor_tensor(out=rs[:, :], in0=mu[:, :], in1=mu[:, :], op=ALU.mult), [("ev", i0)], [("ev", i1)])
                  dve(lambda e, rs=rs, bq=bq: e.tensor_tensor(out=rs[:, :], in0=ps[bq][:, :], in1=rs[:, :], op=ALU.subtract),
                      [("ps", bq), ("ev", i1)], [("ev", i1)])
                  act(lambda e, rs=rs: e.activation(out=rs[:, :], in_=rs[:, :], func=AF.Ln, bias=small[:, 2:3], scale=1.0), [("ev", i1), "small"], [("ev", i1)])
                  act(lambda e, rs=rs: e.activation(out=rs[:, :], in_=rs[:, :], func=AF.Exp, scale=-0.5), [("ev", i1)], [("ev", i1)])
                  for c in range(4):
                      dve(lambda e, c=c, mu=mu: e.tensor_tensor(out=ACC[:, c, :], in0=ACC[:, c, :], in1=mu[:, :], op=ALU.subtract),
                          [("ACC", c), ("ev", i0)], [("ACC", c)])
                      dve(lambda e, c=c, rs=rs: e.tensor_tensor(out=ACC[:, c, :], in0=ACC[:, c, :], in1=rs[:, :], op=ALU.mult),
                          [("ACC", c), ("ev", i1)], [("ACC", c)])
                      act(lambda e, c=c, t=t: e.activation(out=BR[:, 8 + c, t * TW:(t + 1) * TW], in_=ACC[:, c, :], func=AF.Silu,
                                                           bias=ppc(l, "clnb", c), scale=ppc(l, "clng", c)),
                          [("ACC", c), "pp"], [("BR", 8 + c, t)])
                  yield
                  if t == 0:
                      dve(lambda e: e.tensor_copy(out=GB[:, :, 0:2], in_=HL[:, :, 32:34]), ["HL", ("BR", 11, t)], ["GB"])
                      dma("sp", GB[:, :, 2:514], sxc_s[:, 0:TW].rearrange("(c p) n -> p c n", p=128), ["sxc"], ["GB"])
                  else:
                      dma("sp", GB[:, :, 0:514], sxc_s[:, t * TW - 2:(t + 1) * TW].rearrange("(c p) n -> p c n", p=128),
                          ["sxc", ("BR", 11, t)], ["GB"])
                  dma("sp", SB2, sb_s[:, t * TW:(t + 1) * TW].rearrange("(c p) n -> p c n", p=128), ["sb"], ["SB2"])
                  for c in range(4):
                      dve(lambda e, c=c: e.tensor_scalar(out=ACC[:, c, :], in0=GB[:, c, 0:512], scalar1=ppc(l, "scw", c * 3),
                                                         scalar2=None, op0=ALU.mult), ["GB", "pp"], [("ACC", c)])
                      for k in range(1, 3):
                          dve(lambda e, c=c, k=k: e.scalar_tensor_tensor(out=ACC[:, c, :], in0=GB[:, c, k:k + 512],
                                                                         scalar=ppc(l, "scw", c * 3 + k), in1=ACC[:, c, :],
                                                                         op0=ALU.mult, op1=ALU.add),
                              ["GB", "pp", ("ACC", c)], [("ACC", c)])
                      dve(lambda e, c=c, t=t: e.tensor_tensor(out=BR[:, 12 + c, t * TW:(t + 1) * TW], in0=ACC[:, c, :], in1=SB2[:, c, :], op=ALU.mult),
                          [("ACC", c), "SB2"], [("BR", 12 + c, t)])
              return

          chk("S2a")
          KT = ARB[:, 0:4096]
          QT = ARB[:, 4096:6144]
          VV = ARB[:, 6144:10240].rearrange("p (b c) -> p b c", b=32)
          KT1 = ARB[:, 10240:14336]
          oKA2 = oKA
          cg = conv_gen()
          dve(lambda e: e.memset(KT[64:128, :], 0.0), [], [("KT", 0), ("KT", 1)])
          dve(lambda e: e.memset(KT1[0:64, :], 0.0), [], [("KT1", 0), ("KT1", 1)])

          fbank = [0]
          prefetch = [None]
          cg_state = [None]

          def pull(allow_psum):
              if cg_state[0] == "ln" and not allow_psum:
                  return
              cg_state[0] = next(cg, None)

          def attention(rows_list, kind, h, finalize):
              for j in range(NT):
                  steps = []
                  for kb in range(16):
                      steps.append((kb, 0, True))
                  for kb in range(4 * j + 4):
                      q0 = max(0, kb - 4 * j) * 128
                      steps.append((16 + kb, q0, False))
                  seq = [(s_, ci) for s_ in steps for ci in range(len(rows_list))]
                  sbank = {}

                  def qk(idx):
                      (kb, q0, prev), ci = seq[idx]
                      kbuf, kkey, r0, r1 = rows_list[ci]
                      b = 4 + rr("ps", 4)
                      sbank[idx] = b
                      if MASK_PE and (not prev) and kb - 16 >= 4 * j:
                          mo = 0 if kind == "A" else 128

                          def fn(e, b=b, kbuf=kbuf, r0=r0, r1=r1, kb=kb, q0=q0, mo=mo, j=j):
                              e.matmul(ps[b][:, q0:512], kbuf[r0:r1, kb * 128:(kb + 1) * 128], QT[r0:r1, j * TW + q0:(j + 1) * TW],
                                       start=True, stop=True)
                              return e.matmul(ps[b][:, q0:q0 + 128], masks[:, 256:384], masks[:, mo:mo + 128], start=False, stop=True)
                          pe(fn, [(kkey, kb // 16), ("QT", j), "masks"], [("ps", b)])
                      else:
                          mm_group(b, [(kbuf[r0:r1, kb * 128:(kb + 1) * 128], QT[r0:r1, j * TW + q0:(j + 1) * TW])],
                                   [(kkey, kb // 16), ("QT", j)], q0, 512)
                  for idx in range(min(3, len(seq))):
                      qk(idx)
                  for idx in range(len(seq)):
                      (kb, q0, prev), ci = seq[idx]
                      b = sbank[idx]
                      pi = rr("pt", 4)
                      pt = PTs[pi]
                      if prev:
                          act(lambda e, pt=pt, b=b: e.activation(out=pt[:, :], in_=ps[b][:, :], func=AF.Exp, bias=pmask, scale=1.0),
                              [("ps", b), "flags"], [("pt", pi)])
                      else:
                          act(lambda e, pt=pt, b=b, q0=q0: e.activation(out=pt[:, q0:512], in_=ps[b][:, q0:512], func=AF.Exp),
                              [("ps", b)], [("pt", pi)])
                          if (not MASK_PE) and kb - 16 >= 4 * j:
                              mo = 0 if kind == "A" else 128
                              dve(lambda e, pt=pt, q0=q0, mo=mo: e.tensor_tensor(out=pt[:, q0:q0 + 128], in0=pt[:, q0:q0 + 128],
                                                                                 in1=masks[:, mo:mo + 128], op=ALU.mult),
                                  [("pt", pi), "masks"], [("pt", pi)])
                      if idx + 3 < len(seq):
                          qk(idx + 3)
                      if j == NT - 1 and idx == 16 * len(rows_list) + 1 and prefetch[0] is not None:
                          prefetch[0]()
                          prefetch[0] = None
                      if idx == len(seq) // 2:
                          flush_pending()
                      if idx % 8 == 7:
                          pull(False)
                      first = (idx < len(rows_list))
                      last = (idx >= len(seq) - len(rows_list))
                      ob = ci if kind == "A" else (fbank[0] if F_PINGPONG else 0)
                      pe(lambda e, ob=ob, kb=kb, pt=pt, q0=q0, first=first, last=last:
                         e.matmul(ps[ob][:, q0:512], VV[:, kb, :], pt[:, q0:512], start=first, stop=last),
                         [("VV", kb // 16), ("pt", pi)], [("ps", ob)])
                      if kind == "A":
                          pe(lambda e, ob=ob, pt=pt, q0=q0, first=first, last=last:
                             e.matmul(ps[2 + ob][:, q0:512], ones_bf[:, :], pt[:, q0:512], start=first, stop=last),
                             ["consts", ("pt", pi)], [("ps", 2 + ob)])
                  finalize(h, j)
                  fbank[0] ^= 1
                  pull(True)

          SQF = sq[:, :, :].rearrange("p c n -> p (c n)").bitcast(F32).rearrange("p (c n) -> p c n", c=4)
          SQK = [("sq", c) for c in range(8)]

          YAP = vst[:, :, :].rearrange("p c n -> p (c n)").bitcast(F32)
          pending = []

          def flush_pending():
              while pending:
                  pending.pop(0)()

          def fin_A(h, j):
              for bi in (0, 2, 1, 3):
                  dve(lambda e, bi=bi: e.tensor_copy(out=SQF[:, bi, :], in_=ps[bi][:, :]), [("ps", bi)], SQK)
              dve(lambda e: e.reciprocal(out=SQF[:, 2, :], in_=SQF[:, 2, :]), SQK, SQK)
              dve(lambda e: e.tensor_tensor(out=SQF[:, 0, :], in0=SQF[:, 0, :], in1=SQF[:, 2, :], op=ALU.mult), SQK, SQK)
              dve(lambda e: e.reciprocal(out=SQF[:, 3, :], in_=SQF[:, 3, :]), SQK, SQK)
              dve(lambda e: e.tensor_tensor(out=SQF[:, 1, :], in0=SQF[:, 1, :], in1=SQF[:, 3, :], op=ALU.mult), SQK, SQK)
              dve(lambda e: e.scalar_tensor_tensor(out=YAP, in0=SQF[:, 1, :], scalar=small[:, 0:1], in1=SQF[:, 0, :],
                                                   op0=ALU.mult, op1=ALU.add), SQK + ["small"], ["vst"])

              def part2():
                  si = rr("stg", 4)
                  sg = stg[si]
                  act(lambda e: e.activation(out=sg[:, :], in_=YAP, func=AF.Square), ["vst"], [("stg", si)])
                  b = 4 + rr("ps", 4)
                  mm_group(b, [(ones_bf[:, :], sg[:, :])], [("stg", si), "consts"])
                  i = rr("ev", 4)
                  r = evs[i]
                  act(lambda e: e.activation(out=r[:, :], in_=ps[b][:, :], func=AF.Ln, bias=small[:, 2:3], scale=1.0 / 128),
                      [("ps", b), "small"], [("ev", i)])
                  act(lambda e: e.activation(out=r[:, :], in_=r[:, :], func=AF.Exp, scale=-0.5), [("ev", i)], [("ev", i)])
                  dve(lambda e: e.scalar_tensor_tensor(out=BR[:, h, j * TW:(j + 1) * TW], in0=YAP, scalar=small[:, 1:2], in1=r[:, :],
                                                       op0=ALU.mult, op1=ALU.mult), ["vst", ("ev", i), "small"], [("BR", h, j)])
              pending.append(part2)

          def fin_F(h, j):
              i = rr("ev", 4)
              ev = evs[i]
              i2 = rr("ev", 4)
              dn = evs[i2]
              p0 = (h % 2) * 64
              zb = fbank[0] if F_PINGPONG else 0
              dve(lambda e: e.tensor_copy(out=ev[:, :], in_=ps[zb][:, :]), [("ps", zb)], [("ev", i)])
              dma("sp", dn[0:64, :], ev[64:128, :], [("ev", i)], [("ev", i2)])
              dve(lambda e: e.reciprocal(out=dn[0:64, :], in_=dn[0:64, :]), [("ev", i2)], [("ev", i2)])
              dve(lambda e: e.tensor_tensor(out=BR[p0:p0 + 64, 4 + h // 2, j * TW:(j + 1) * TW], in0=ev[0:64, :], in1=dn[0:64, :], op=ALU.mult),
                  [("ev", i), ("ev", i2)], [("BR", 4 + h // 2, j)])

          def load_q(src, rows, tiles):
              for j in tiles:
                  dma("sp", QT[0:rows, j * TW:(j + 1) * TW], src[:, j * TW:(j + 1) * TW], ["qa", "qf"], [("QT", j)])

          def a_early(h):
              dma("sp", KT[0:64, 0:2048], oKA2[h * 128:h * 128 + 64, :], ["oKA"], [("KT", 0)])
              dma("sp", KT1[64:128, 0:2048], oKA2[h * 128 + 64:(h + 1) * 128, :], ["oKA"], [("KT1", 0)])
              load_q(qa_s[h * 128:(h + 1) * 128, :], 128, range(0, 3))
              dma("sp", VV[:, 0:16, :], oVg[0][0, h].rearrange("(b p) c -> p b c", p=128), ["oV0"], [("VV", 0)])

          def a_late(h):
              load_q(qa_s[h * 128:(h + 1) * 128, :], 128, range(3, 4))
              dma("sp", KT[0:64, 2048:4096], cKA[h * 128:h * 128 + 64, :], ["cKA"], [("KT", 1)])
              dma("sp", KT1[64:128, 2048:4096], cKA[h * 128 + 64:(h + 1) * 128, :], ["cKA"], [("KT1", 1)])
              dma("sp", VV[:, 16:32, :], cVg[0][h].rearrange("(b p) c -> p b c", p=128), ["cV"], [("VV", 1)])

          def f_early(h):
              dma("sp", KT[0:64, 0:2048], oKFk[h * 64:(h + 1) * 64, :], ["oKF"], [("KT", 0)])
              dma("sp", KT[64:68, 0:2048], oKFa[h * 4:(h + 1) * 4, :], ["oKFa"], [("KT", 0)])
              load_q(qf3[h], 68, range(0, 3))
              dma("sp", VV[:, 0:16, :], oVg[1 + h // 4][0, h % 4].rearrange("(b p) c -> p b c", p=128), ["oV%d" % (1 + h // 4)], [("VV", 0)])

          def f_late(h):
              load_q(qf3[h], 68, range(3, 4))
              dma("sp", KT[0:68, 2048:4096], kfo3[h], ["kfo"], [("KT", 1)])
              dma("sp", VV[:, 16:32, :], cVg[1 + h // 4][h % 4].rearrange("(b p) c -> p b c", p=128), ["cV"], [("VV", 1)])

          heads = ([(a_early, a_late, [(KT, "KT", 0, 128), (KT1, "KT1", 0, 128)], "A", h, fin_A) for h in range(4)]
                   + [(f_early, f_late, [(KT, "KT", 0, 68)], "F", h, fin_F) for h in range(8)])
          for hi, (early, late, rl, kind, h, fin) in enumerate(heads):
              if hi == 0:
                  early(h)
              late(h)
              nxt = heads[hi + 1] if hi + 1 < len(heads) else None
              prefetch[0] = (lambda nxt=nxt: nxt[0](nxt[4])) if nxt is not None else None
              attention(rl, kind, h, fin)

          flush_pending()
          for _ in cg:
              pass
          chk("S2b")
          for d in range(8):
              wbs, wbk = load_w(wb[l, d], 2048)
              wgs, wgk = load_w(wg[l, d], 4096)
              for t in range(NT):
                  pb = []
                  for n in range(4):
                      b = n
                      mm_group(b, [(wbs[:, (n * 4 + k) * 128:(n * 4 + k + 1) * 128], BR[:, n * 4 + k, t * TW:(t + 1) * TW]) for k in range(4)],
                               [wbk] + [("BR", n * 4 + k, t) for k in range(4)])
                      pb.append(b)
                  i = rr("ev", 4)
                  acc = evs[i]
                  for n in range(4):
                      b = 4 + n
                      mm_group(b, [(wgs[:, (n * 8 + k) * 128:(n * 8 + k + 1) * 128], A[:, k, t * TW:(t + 1) * TW]) for k in range(8)],
                               [wgk] + [("A", k, t) for k in range(8)])
                      gi = rr("pt", 4)
                      gt = PTs[gi]
                      act(lambda e, gt=gt, b=b, n=n, d=d: e.activation(out=gt[:, :], in_=ps[b][:, :], func=AF.Sigmoid,
                                                                      bias=ppc(l, "bgate", n * 8 + d), scale=1.0),
                          [("ps", b), "pp"], [("pt", gi)])
                      if n == 0:
                          dve(lambda e, gt=gt, acc=acc: e.tensor_tensor(out=acc[:, :], in0=ps[0][:, :], in1=gt[:, :], op=ALU.mult),
                              [("ps", 0), ("pt", gi)], [("ev", i)])
                      else:
                          ti = rr("ev", 4)
                          if ti == i:
                              ti = rr("ev", 4)
                          tm = evs[ti]
                          dve(lambda e, gt=gt, tm=tm, n=n: e.tensor_tensor(out=tm[:, :], in0=ps[n][:, :], in1=gt[:, :], op=ALU.mult),
                              [("ps", n), ("pt", gi)], [("ev", ti)])
                          dve(lambda e, tm=tm, acc=acc: e.tensor_tensor(out=acc[:, :], in0=acc[:, :], in1=tm[:, :], op=ALU.add),
                              [("ev", ti), ("ev", i)], [("ev", i)])
                  si = rr("stg", 4)
                  s = stg[si]
                  act(lambda e, s=s, acc=acc: e.activation(out=s[:, :], in_=acc[:, :], func=AF.Copy), [("ev", i)], [("stg", si)])
                  dma("sp", mix_s[d, :, t * TW:(t + 1) * TW], s[:, :], [("stg", si)], [("mix", t)])

          chk("S3")
          def y_to_ysc(ci, t, b):
              i = rr("ev", 4)
              ev = evs[i]
              act(lambda e, ev=ev, b=b: e.activation(out=ev[:, :], in_=ps[b][:, :], func=AF.Copy), [("ps", b)], [("ev", i)])
              dma("sp", ysc[ci, :, t * TW:(t + 1) * TW], ev[:, :], [("ev", i)], [("ysc", t)])

          for t in range(NT):
              for c in range(8):
                  dma("sp", A[:, c, t * TW:(t + 1) * TW], mix_s[c, :, t * TW:(t + 1) * TW], [("mix", t)], [("A", c, t)])
          fenceX()
          lin_tile_outer(Akey, A, 8, [wo[l, 0:4], wo[l, 4:8]], 4, y_to_ysc, post_pass(l, "nmo", "nxp", l, tiled=True))
          fenceX()

          chk("S4")
          fenceB()
          MX = XF[:, 8192:10240].rearrange("p (c n) -> p c n", c=8)
          XQ = ARX[:, 20480:28672].rearrange("p (c n) -> p c n", c=4)
          dma("sp", MX, memT_in.rearrange("(c p) n -> p c n", p=128), [], ["MX"])
          for c in range(8):
              act(lambda e, c=c: e.activation(out=sq[:, c, 0:256], in_=MX[:, c, :], func=AF.Square), ["MX"], [("sq", c)])
          r, rk = rstd_from_sq(8, meanD, [("sq", c) for c in range(8)], n=256)
          for c in range(8):
              dve(lambda e, c=c, r=r: e.scalar_tensor_tensor(out=mTb[:, c, :], in0=MX[:, c, :], scalar=ppc(l, "nmem", c), in1=r[:, 0:256],
                                                          op0=ALU.mult, op1=ALU.mult), ["MX", rk, "pp"], ["mTb"])
          for h in range(4):
              w, wk = load_w(wxk[l, h], 1024)
              b = rr("ps", 8)
              mm_group(b, [(w[:, k * 128:(k + 1) * 128], mTb[:, k, :]) for k in range(8)], [wk, "mTb"], 0, 256)
              act(lambda e, b=b, h=h: e.activation(out=xkT[:, h, :], in_=ps[b][:, 0:256], func=AF.Copy), [("ps", b)], ["xkT"])
          w, wk = load_w(wxv[l], 4096)
          for kc2 in range(2):
              b = rr("ps", 8)
              mm_group(b, [(mTb[:, k, kc2 * 128:(kc2 + 1) * 128], w[:, k * 512:(k + 1) * 512]) for k in range(8)], [wk, "mTb"])
              act(lambda e, b=b, kc2=kc2: e.activation(out=xvs[:, kc2, :], in_=ps[b][:, :], func=AF.Copy), [("ps", b)], ["xvs"])

          def xq_h(ci, t, b):
              act(lambda e, b=b, ci=ci, t=t: e.activation(out=XQ[:, ci, t * TW:(t + 1) * TW], in_=ps[b][:, :], func=AF.Copy),
                  [("ps", b)], [("XQ", ci, t)])
          lin_fm(Akey, A, 8, [wxq[l, h] for h in range(4)], xq_h)
          xscale = 128 ** -0.5
          OX = XQ
          for h in range(4):
              for t in range(NT):
                  pts = []
                  bo = 2 * ((h * NT + t) % 2)
                  for kc2 in range(2):
                      b = 4 + rr("ps", 4)
                      mm_group(b, [(xkT[:, h, kc2 * 128:(kc2 + 1) * 128], XQ[:, h, t * TW:(t + 1) * TW])], ["xkT", ("XQ", h, t)])
                      pi = rr("pt", 4)
                      pt = PTs[pi]
                      act(lambda e, pt=pt, b=b: e.activation(out=pt[:, :], in_=ps[b][:, :], func=AF.Exp, scale=xscale), [("ps", b)], [("pt", pi)])
                      pts.append((pt, pi))
                  mm_group(bo, [(xvs[:, kc2, h * 128:(h + 1) * 128], pts[kc2][0][:, :]) for kc2 in range(2)],
                           ["xvs"] + [("pt", p_[1]) for p_ in pts])
                  mm_group(bo + 1, [(ones_bf[:, :], pts[kc2][0][:, :]) for kc2 in range(2)], ["consts"] + [("pt", p_[1]) for p_ in pts])
                  i = rr("ev", 4)
                  ev = evs[i]
                  act(lambda e, ev=ev, bo=bo: e.activation(out=ev[:, :], in_=ps[bo + 1][:, :], func=AF.Ln), [("ps", bo + 1)], [("ev", i)])
                  act(lambda e, ev=ev: e.activation(out=ev[:, :], in_=ev[:, :], func=AF.Exp, scale=-1.0), [("ev", i)], [("ev", i)])
                  dve(lambda e, ev=ev, h=h, t=t, bo=bo: e.tensor_tensor(out=OX[:, h, t * TW:(t + 1) * TW], in0=ps[bo][:, :], in1=ev[:, :], op=ALU.mult),
                      [("ps", bo), ("ev", i)], [("XQ", h, t)])
          lin_tile_outer(lambda k, t: ("XQ", k, t), OX, 4, [wxo[l, 0:8]], 8, y_to_ysc, post_pass(l, "nxo", "nfp", l, tiled=True))

          chk("S5")
          dma("sp", cF.rearrange("(c p) n -> p c n", p=128), A[:, :, 2046:2048], [("A", c, 3) for c in range(8)], ["cF"])
          P.add("pool", lambda e: e.collective_compute("AllGather", ALU.bypass, replica_groups=RG, ins=[cF_], outs=[oF_]),
                ["cF"], ["oF", "ccorder"], cc=True)
          dma("sp", hfh2[:, :, :], oF[0:1024, :].rearrange("(c p) n -> p c n", p=128), ["oF"], ["hfh2"])
          dve(lambda e: e.tensor_scalar(out=hfh[:, :, :], in0=hfh2[:, :, :], scalar1=hflag, scalar2=None, op0=ALU.mult), ["hfh2", "flags"], ["hfh"])
          ACT_T = ARX[:, 0:22528].rearrange("p (g n) -> p g n", g=NG)
          fenceX()
          for half in range(2):
              for g in range(NG):
                  i = rr("w", 3)
                  w = wsl[i]
                  wk = ("w", i)
                  dma("pool", w[:, 0:1024], wup[l, 2 * g], [], [wk])
                  dma("pool", w[:, 1024:2048], wup[l, 2 * g + 1], [], [wk])
                  if half == 0:
                      for gv in range(2):
                          b = rr("ps", 8)
                          mm_group(b, [(w[:, gv * 1024 + k * 128: gv * 1024 + (k + 1) * 128], hfh[:, k, :]) for k in range(8)],
                                   [wk, "hfh"], 0, 2)
                          dve(lambda e, b=b, g=g, gv=gv: e.tensor_copy(out=uh[:, 2 * g + gv, :], in_=ps[b][:, 0:2]), [("ps", b)], [("uh", g)])
                  for tt in range(2):
                      t = half * 2 + tt
                      res = []
                      for gv in range(2):
                          b = rr("ps", 8)
                          mm_group(b, [(w[:, gv * 1024 + k * 128: gv * 1024 + (k + 1) * 128], A[:, k, t * TW:(t + 1) * TW]) for k in range(8)],
                                   [wk] + [("A", k, t) for k in range(8)])
                          ch = 2 * g + gv
                          col = (gv * NG + g)
                          i2 = rr("ev", 4)
                          ev = evs[i2]
                          act(lambda e, ev=ev, b=b, col=col: e.activation(out=ev[:, :], in_=ps[b][:, :], func=AF.Identity,
                                                                          scale=ppc(l, "fdw", 2 * 44 + col), bias=ppc(l, "fdwb", col)),
                              [("ps", b), "pp"], [("ev", i2)])
                          dve(lambda e, ev=ev, b=b, col=col: e.scalar_tensor_tensor(out=ev[:, 1:512], in0=ps[b][:, 0:511], scalar=ppc(l, "fdw", 44 + col),
                                                                                    in1=ev[:, 1:512], op0=ALU.mult, op1=ALU.add),
                              [("ps", b), "pp", ("ev", i2)], [("ev", i2)])
                          dve(lambda e, ev=ev, b=b, col=col: e.scalar_tensor_tensor(out=ev[:, 2:512], in0=ps[b][:, 0:510], scalar=ppc(l, "fdw", col),
                                                                                    in1=ev[:, 2:512], op0=ALU.mult, op1=ALU.add),
                              [("ps", b), "pp", ("ev", i2)], [("ev", i2)])
                          dve(lambda e, ev=ev, ch=ch, col=col: e.scalar_tensor_tensor(out=ev[:, 0:1], in0=uh[:, ch, 1:2], scalar=ppc(l, "fdw", 44 + col),
                                                                                      in1=ev[:, 0:1], op0=ALU.mult, op1=ALU.add),
                              [("uh", g), "pp", ("ev", i2)], [("ev", i2)])
                          dve(lambda e, ev=ev, ch=ch, col=col: e.scalar_tensor_tensor(out=ev[:, 0:2], in0=uh[:, ch, 0:2], scalar=ppc(l, "fdw", col),
                                                                                      in1=ev[:, 0:2], op0=ALU.mult, op1=ALU.add),
                              [("uh", g), "pp", ("ev", i2)], [("ev", i2)])
                          dve(lambda e, b=b, ch=ch: e.tensor_copy(out=uh[:, ch, :], in_=ps[b][:, 510:512]), [("ps", b), ("ev", i2)], [("uh", g)])
                          res.append((ev, i2))
                      (eg, ig), (evv, iv) = res
                      si = rr("stg", 4)
                      s = stg[si]
                      act(lambda e, s=s, eg=eg: e.activation(out=s[:, :], in_=eg[:, :], func=AF.Silu), [("ev", ig)], [("stg", si)])
                      dve(lambda e, s=s, evv=evv, g=g, tt=tt: e.tensor_tensor(out=ACT_T[:, g, tt * TW:(tt + 1) * TW], in0=s[:, :], in1=evv[:, :], op=ALU.mult),
                          [("stg", si), ("ev", iv)], [("ACT_T", g, tt)])
              for d in range(8):
                  w, wk = load_w(wdn[l, d], 2816)
                  for tt in range(2):
                      t = half * 2 + tt
                      b = rr("ps", 8)
                      mm_group(b, [(w[:, g * 128:(g + 1) * 128], ACT_T[:, g, tt * TW:(tt + 1) * TW]) for g in range(NG)],
                               [wk] + [("ACT_T", g, tt) for g in range(NG)])
                      y_to_ysc(d, t, b)
          fenceX()
          last = (l == nlayers - 1)
          post_pass(l, "nfo", "nmp", min(l + 1, L - 1), final=last)

    for l_ in range(nlayers if stop != "S0" else 0):
        try:
            do_layer(l_)
        except _Stop:
            break

    P.add("sp", lambda e: e.nop(), ["out"], [])

    P.finalize(nc, es)
    with es:
        with nc.Block() as block:
            @block.tensor
            def _(e):
                P.emit("pe", e)

            @block.scalar
            def _(e):
                P.emit("act", e)

            @block.vector
            def _(e):
                P.emit("dve", e)

            @block.gpsimd
            def _(e):
                P.emit("pool", e)

            @block.sync
            def _(e):
                P.emit("sp", e)
    return nc, P


def _chunks_fm(W, cols):
    K = W.shape[0]
    kc = K // 128
    outl = []
    for c0 in cols:
        blk = W[:, c0:c0 + 128].reshape(kc, 128, 128).transpose(1, 0, 2)
        outl.append(blk.reshape(128, kc * 128))
    return np.ascontiguousarray(np.stack(outl, 0))


def _mov(W):
    K, n = W.shape
    return np.ascontiguousarray(W.reshape(K // 128, 128, n).transpose(1, 0, 2).reshape(128, (K // 128) * n))


def _cols(v):
    return np.ascontiguousarray(np.asarray(v, np.float32).reshape(-1, 128).T)


def prep_inputs(inp):
    f = lambda k: np.asarray(inp[k], np.float32)
    w_in, w_branch, w_gate, w_out = f("w_in"), f("w_branch"), f("w_gate"), f("w_out")
    w_xq, w_xkv, w_xo, w_up, w_down = f("w_xq"), f("w_xkv"), f("w_xo"), f("w_up"), f("w_down")
    seg = [0, 512, 1536, 2048, 3080, 3592, 4104, 4616, 5128]
    cols36 = [s + j * 128 for s in seg for j in range(4)]
    shared = {}
    shared["win"] = np.stack([_chunks_fm(w_in[l], cols36) for l in range(L)], 0)
    shared["wfg"] = np.stack([_mov(w_in[l][:, 3072:3080]) for l in range(L)], 0)
    shared["wv"] = np.stack([np.stack([_mov(w_in[l][:, 1024:1536]), _mov(w_in[l][:, 2560:3072])], 0) for l in range(L)], 0)
    wb = np.zeros((L, 8, 128, 2048), np.float32)
    wg = np.zeros((L, 8, 128, 4096), np.float32)
    for l in range(L):
        for d in range(8):
            wb[l, d] = np.concatenate([_chunks_fm(w_branch[l, n], [d * 128])[0] for n in range(4)], 1)
            wg[l, d] = np.concatenate([_chunks_fm(w_gate[l], [n * 1024 + d * 128])[0] for n in range(4)], 1)
    shared["wb"], shared["wg"] = wb, wg
    shared["wo"] = np.stack([_chunks_fm(w_out[l], [d * 128 for d in range(8)]) for l in range(L)], 0)
    shared["wxq"] = np.stack([_chunks_fm(w_xq[l], [h * 128 for h in range(4)]) for l in range(L)], 0)
    shared["wxk"] = np.stack([_chunks_fm(w_xkv[l], [h * 128 for h in range(4)]) for l in range(L)], 0)
    shared["wxv"] = np.stack([_mov(w_xkv[l][:, 512:1024]) for l in range(L)], 0)
    shared["wxo"] = np.stack([_chunks_fm(w_xo[l], [d * 128 for d in range(8)]) for l in range(L)], 0)
    upcols = []
    for g in range(NG):
        upcols += [g * 128, DFF + g * 128]
    shared["wup"] = np.stack([_chunks_fm(w_up[l], upcols) for l in range(L)], 0)
    shared["wdn"] = np.stack([_chunks_fm(w_down[l], [d * 128 for d in range(8)]) for l in range(L)], 0)
    pp = np.zeros((128, NPP), np.float32)

    def put(l, name, arr):
        o, w = PP[name]
        assert arr.shape == (128, w), (name, arr.shape)
        pp[:, l * PPL + o:l * PPL + o + w] = arr
    for l in range(L):
        for nm, key in (("nmp", "norm_mix_pre"), ("nmo", "norm_mix_post"), ("nxp", "norm_x_pre"), ("nxo", "norm_x_post"),
                        ("nmem", "norm_mem"), ("nfp", "norm_ffn_pre"), ("nfo", "norm_ffn_post"), ("bglu", "b_glu"),
                        ("cdwb", "conv_dw_b"), ("clng", "conv_ln_g"), ("clnb", "conv_ln_b"), ("bgate", "b_gate"),
                        ("fdwb", "ffn_dw_b"), ("dnorm", "diff_norm")):
            put(l, nm, _cols(f(key)[l]))
        cd = f("conv_dw")[l]
        put(l, "cdw", np.concatenate([cd[:, c * 128:(c + 1) * 128].T for c in range(4)], 1))
        sc = f("sc_w")[l]
        put(l, "scw", np.concatenate([sc[:, c * 128:(c + 1) * 128].T for c in range(4)], 1))
        fd = f("ffn_dw")[l]
        put(l, "fdw", np.concatenate([_cols(fd[k]) for k in range(3)], 1))
        bf = np.zeros((128, 1), np.float32)
        bf[0:8, 0] = f("b_fgt")[l]
        put(l, "bfgt", bf)
        for nm, key in (("lq1", "lam_q1"), ("lk1", "lam_k1"), ("lq2", "lam_q2"), ("lk2", "lam_k2")):
            put(l, nm, np.broadcast_to(f(key)[l][None, :], (128, 64)).copy())
    shared["pp"] = pp
    kk = np.arange(128)[:, None]
    qq = np.arange(128)[None, :]
    if MASK_PE:
        shared["masks"] = np.concatenate([np.where(kk // 64 <= qq // 64, 0.0, NEG), np.where(kk <= qq, 0.0, NEG),
                                          np.eye(128)], 1).astype(np.float32)
    else:
        shared["masks"] = np.concatenate([(kk // 64 <= qq // 64), (kk <= qq), np.eye(128)], 1).astype(np.float32)
    x = f("x")
    mem = f("mem")
    maps = []
    for c in range(8):
        b, hf = c // 2, c % 2
        m = dict(shared)
        m["xT"] = np.ascontiguousarray(x[b, hf * T:(hf + 1) * T, :].T)
        m["memT"] = np.ascontiguousarray(mem[b].T)
        fl = np.zeros((128, 2), np.float32)
        fl[:, 0] = 0.0 if hf == 1 else NEG
        fl[:, 1] = 1.0 if hf == 1 else 0.0
        m["flags"] = fl
        maps.append(m)
    return maps


_NC = {}


def kernel(**inputs):
    if "nc" not in _NC:
        _NC["nc"] = build(L)[0]
    maps = prep_inputs(inputs)
    res = run_bass_kernel_spmd(_NC["nc"], maps, core_ids=list(range(8)))
    outp = np.zeros((4, 2 * T, D), np.float32)
    for c in range(8):
        b, hf = c // 2, c % 2
        outp[b, hf * T:(hf + 1) * T, :] = np.asarray(res.results[c]["out"]).T
    return outp
```
